# Optimizing a Trainium2 kernel written in Bass

```python
import math
import jax
import jax.numpy as jnp
from jax import lax
import numpy as np

D_MODEL = 4096
BATCH = 2
SEQ = 4096
DEPTH = 2
DEC_BATCH = 8
DEC_SEQ = 2048
PAST_LEN = 128

GRID_W = 64
N_EVEN = (DEPTH + 1) // 2
N_ODD = DEPTH // 2
HEAD_DIM = 128
NA_HEADS = 16
NA_WIDTH = NA_HEADS * HEAD_DIM
NA_MAX_ROWS = 8
NA_COLS = 16
NA_QBLOCK = 16
NA_KBLOCK = 32
HY_WIDTH = D_MODEL - NA_WIDTH
HY_SHORT = 3
HY_BANDS = 16
HY_EMB = 2 * HY_BANDS + 1
HY_HIDDEN = 64
HY_DECAY_MIN = -math.log(1e-2) / 1.5
HY_DECAY_MAX = -math.log(1e-2) / 0.3
S5_GROUP = 16
S5_GROUPS = D_MODEL // S5_GROUP
S5_STATE = 64
X_HEADS = 4
X_WIDTH = X_HEADS * HEAD_DIM
N_MEM = 256
FFN_HIDDEN = ((8 * D_MODEL + 3 * 256 - 1) // (3 * 256)) * 256
EPS = 1e-6

kernel_name = "hybrid_na_hyena_s5_encoder"


def rms_norm(x, g):
    xf = x.astype(jnp.float32)
    y = xf * lax.rsqrt(jnp.mean(xf * xf, axis=-1, keepdims=True) + EPS)
    return (y * g.astype(jnp.float32)).astype(x.dtype)


def neighbourhood_attention(q, k, v, rpb):
    b, l, h, dh = q.shape
    rows = l // GRID_W
    kr = min(NA_MAX_ROWS, rows)
    n_cb = GRID_W // NA_QBLOCK
    qcol = np.arange(GRID_W).reshape(n_cb, NA_QBLOCK)
    qstart = np.clip(qcol - NA_COLS // 2, 0, GRID_W - NA_COLS)
    kstart = np.clip(np.arange(n_cb) * NA_QBLOCK - NA_COLS // 2, 0, GRID_W - NA_KBLOCK)
    key_cols = kstart[:, None] + np.arange(NA_KBLOCK)
    kc = key_cols[:, None, :]
    col_valid = (kc >= qstart[..., None]) & (kc < qstart[..., None] + NA_COLS)
    col_idx = np.clip(kc - qcol[..., None] + NA_COLS - 1, 0, 2 * NA_COLS - 2)
    scale = dh ** -0.5
    q_rows = q.reshape(b, rows, n_cb, NA_QBLOCK, h, dh).transpose(1, 0, 2, 3, 4, 5)
    k_grid = k.reshape(b, rows, GRID_W, h, dh)
    v_grid = v.reshape(b, rows, GRID_W, h, dh)

    def one_row(args):
        r, q_r = args
        r0 = jnp.clip(r - kr // 2, 0, rows - kr)
        k_r = lax.dynamic_slice_in_dim(k_grid, r0, kr, axis=1)[:, :, key_cols]
        v_r = lax.dynamic_slice_in_dim(v_grid, r0, kr, axis=1)[:, :, key_cols]
        s = jnp.einsum("bjqhd,bkjchd->bhjqkc", q_r, k_r, preferred_element_type=jnp.float32) * scale
        row_idx = r0 + jnp.arange(kr) - r + NA_MAX_ROWS - 1
        bias = rpb.astype(jnp.float32)[:, row_idx][:, :, col_idx].transpose(0, 2, 3, 1, 4)
        s = jnp.where(col_valid[:, :, None, :], s + bias, -jnp.inf)
        p = jax.nn.softmax(s, axis=(-2, -1)).astype(v.dtype)
        o = jnp.einsum("bhjqkc,bkjchd->bjqhd", p, v_r)
        return o.reshape(b, GRID_W, h * dh)

    out = lax.map(one_row, (jnp.arange(rows), q_rows))
    return out.transpose(1, 0, 2, 3).reshape(b, l, h * dh)


def centred_depthwise_conv(u, w, bias):
    y = lax.conv_general_dilated(u, w[:, None, :].astype(u.dtype), window_strides=(1,),
                                 padding=[(HY_SHORT // 2, HY_SHORT // 2)],
                                 dimension_numbers=("NWC", "WIO", "NWC"),
                                 feature_group_count=u.shape[-1])
    return y + bias.astype(u.dtype)


def implicit_filters(l, w1, b1, w2, b2, w3, b3, freq, log_decay):
    f32 = jnp.float32
    t = jnp.arange(l, dtype=f32) / l
    ang = 2.0 * math.pi * t[:, None] * jnp.arange(1, HY_BANDS + 1, dtype=f32)
    z = jnp.concatenate([t[:, None], jnp.cos(ang), jnp.sin(ang)], axis=-1)
    freq = freq.astype(f32)
    hid = jnp.sin(freq * (z @ w1.astype(f32) + b1.astype(f32)))
    hid = jnp.sin(freq * (hid @ w2.astype(f32) + b2.astype(f32)))
    filt = (hid @ w3.astype(f32) + b3.astype(f32)).reshape(l, 2, HY_WIDTH)
    filt = filt * jnp.exp(-jnp.exp(log_decay.astype(f32)) * t[:, None, None])
    return filt / jnp.sum(jnp.abs(filt), axis=(0, 1), keepdims=True)


def bidirectional_fft_conv(u, filt, bias):
    l = u.shape[1]
    h_fwd, h_bwd = filt[:, 0], filt[:, 1]
    taps = jnp.concatenate([h_fwd, jnp.zeros_like(h_fwd[:1]), h_bwd[:0:-1]], axis=0)
    u_f = jnp.fft.rfft(u.astype(jnp.float32), n=2 * l, axis=1)
    t_f = jnp.fft.rfft(taps, n=2 * l, axis=0)
    y = jnp.fft.irfft(u_f * t_f[None], n=2 * l, axis=1)[:, :l]
    return (y + u.astype(jnp.float32) * bias.astype(jnp.float32)).astype(u.dtype)


def hyena_mixer(z, conv_w, conv_b, f_w1, f_b1, f_w2, f_b2, f_w3, f_b3, f_freq, log_decay, hy_bias):
    l = z.shape[1]
    z = centred_depthwise_conv(z, conv_w, conv_b)
    x0, x1, v = jnp.split(z, 3, axis=-1)
    filt = implicit_filters(l, f_w1, f_b1, f_w2, f_b2, f_w3, f_b3, f_freq, log_decay)
    y = bidirectional_fft_conv(v * x1, filt, hy_bias)
    return y * x0


def even_mixer(xn, w_in, q_gain, k_gain, rpb, conv_w, conv_b, f_w1, f_b1, f_w2, f_b2, f_w3, f_b3,
               f_freq, log_decay, hy_bias, w_out):
    b, l, _ = xn.shape
    proj = xn @ w_in
    q, k, v, z = jnp.split(proj, [NA_WIDTH, 2 * NA_WIDTH, 3 * NA_WIDTH], axis=-1)
    q = rms_norm(q.reshape(b, l, NA_HEADS, HEAD_DIM), q_gain)
    k = rms_norm(k.reshape(b, l, NA_HEADS, HEAD_DIM), k_gain)
    v = v.reshape(b, l, NA_HEADS, HEAD_DIM)
    y_a = neighbourhood_attention(q, k, v, rpb)
    y_b = hyena_mixer(z, conv_w, conv_b, f_w1, f_b1, f_w2, f_b2, f_w3, f_b3, f_freq, log_decay, hy_bias)
    return jnp.concatenate([y_a, y_b], axis=-1) @ w_out


def _ssm_combine(left, right):
    a_l, b_l = left
    a_r, b_r = right
    return a_r * a_l, a_r * b_l + b_r


def s5_mixer(xn, lam_re, lam_im, log_step, b_re, b_im, c_re, c_im, d_skip, w_glu):
    f32 = jnp.float32
    bsz, l, d = xn.shape
    u = xn.astype(f32).reshape(bsz, l, S5_GROUPS, S5_GROUP)
    lam = lax.complex(jnp.minimum(lam_re.astype(f32), -1e-4), lam_im.astype(f32))
    step = jnp.exp(log_step.astype(f32))[..., None]
    lam_bar = jnp.exp(lam * step)
    b_bar = ((lam_bar - 1.0) / lam)[..., None] * lax.complex(b_re.astype(f32), b_im.astype(f32))
    c_mat = lax.complex(c_re.astype(f32), c_im.astype(f32))
    skip = d_skip.astype(f32)

    def one_sequence(u_s):
        y = skip * u_s
        for direction in range(2):
            bu = jnp.einsum("gpc,lgc->lgp", b_bar[direction], u_s)
            if direction == 1:
                bu = jnp.flip(bu, axis=0)
            a = jnp.broadcast_to(lam_bar[direction], bu.shape)
            _, states = lax.associative_scan(_ssm_combine, (a, bu), axis=0)
            if direction == 1:
                states = jnp.flip(states, axis=0)
            y = y + jnp.einsum("gcp,lgp->lgc", c_mat[direction], states).real
        return y

    y = lax.map(one_sequence, u).reshape(bsz, l, d)
    y = jax.nn.gelu(y).astype(xn.dtype)
    val, gate = jnp.split(y @ w_glu, 2, axis=-1)
    return val * jax.nn.sigmoid(gate)


def memory_cross_attention(xn, mem, g_mem, wq, wk, wv, wo, q_gain, k_gain):
    b, l, _ = xn.shape
    m = mem.shape[1]
    mn = rms_norm(mem, g_mem)
    q = rms_norm((xn @ wq).reshape(b, l, X_HEADS, HEAD_DIM), q_gain)
    k = rms_norm((mn @ wk).reshape(b, m, X_HEADS, HEAD_DIM), k_gain)
    v = (mn @ wv).reshape(b, m, X_HEADS, HEAD_DIM)
    s = jnp.einsum("blhd,bmhd->bhlm", q, k, preferred_element_type=jnp.float32) * HEAD_DIM ** -0.5
    p = jax.nn.softmax(s, axis=-1).astype(v.dtype)
    o = jnp.einsum("bhlm,bmhd->blhd", p, v).reshape(b, l, X_WIDTH)
    return o @ wo


def swiglu(xn, w_gate, w_up, w_down):
    return (jax.nn.silu(xn @ w_gate) * (xn @ w_up)) @ w_down


def setup_inputs(seed: int = 0) -> dict:
    key = jax.random.key(seed)
    keys = iter(jax.random.split(key, 64))
    f32 = jnp.float32

    def nrm(shape, scale=1.0):
        return jax.random.normal(next(keys), shape, f32) * scale

    def gain(shape):
        return 1.0 + nrm(shape, 0.05)

    d = D_MODEL
    decay0 = jnp.log(jnp.linspace(HY_DECAY_MIN, HY_DECAY_MAX, HY_WIDTH, dtype=f32))
    state_n = jnp.arange(S5_STATE, dtype=f32)
    mix_w = NA_WIDTH + HY_WIDTH
    return {
        "x_prompt": nrm((BATCH, SEQ, d)),
        "x_sample": nrm((DEC_BATCH, DEC_SEQ, d)),
        "mem_prompt": nrm((BATCH, N_MEM, d)),
        "mem_sample": nrm((DEC_BATCH, N_MEM, d)),
        "g_mix": gain((DEPTH, d)),
        "g_cross": gain((DEPTH, d)),
        "g_mem": gain((DEPTH, d)),
        "g_ffn": gain((DEPTH, d)),
        "w_in": nrm((N_EVEN, d, 3 * NA_WIDTH + 3 * HY_WIDTH), d ** -0.5),
        "na_q_gain": gain((N_EVEN, HEAD_DIM)),
        "na_k_gain": gain((N_EVEN, HEAD_DIM)),
        "na_rpb": nrm((N_EVEN, NA_HEADS, 2 * NA_MAX_ROWS - 1, 2 * NA_COLS - 1), 0.1),
        "hy_conv_w": nrm((N_EVEN, HY_SHORT, 3 * HY_WIDTH), HY_SHORT ** -0.5),
        "hy_conv_b": nrm((N_EVEN, 3 * HY_WIDTH), 0.01),
        "hy_f_w1": nrm((N_EVEN, HY_EMB, HY_HIDDEN), HY_EMB ** -0.5),
        "hy_f_b1": nrm((N_EVEN, HY_HIDDEN), 0.1),
        "hy_f_w2": nrm((N_EVEN, HY_HIDDEN, HY_HIDDEN), HY_HIDDEN ** -0.5),
        "hy_f_b2": nrm((N_EVEN, HY_HIDDEN), 0.1),
        "hy_f_w3": nrm((N_EVEN, HY_HIDDEN, 2 * HY_WIDTH), HY_HIDDEN ** -0.5),
        "hy_f_b3": nrm((N_EVEN, 2 * HY_WIDTH), 0.1),
        "hy_f_freq": gain((N_EVEN, HY_HIDDEN)),
        "hy_log_decay": decay0 + nrm((N_EVEN, 2, HY_WIDTH), 0.05),
        "hy_bias": nrm((N_EVEN, HY_WIDTH)),
        "w_out": nrm((N_EVEN, mix_w, d), mix_w ** -0.5),
        "s5_lam_re": -0.5 + nrm((N_ODD, 2, S5_GROUPS, S5_STATE), 0.01),
        "s5_lam_im": math.pi * state_n + nrm((N_ODD, 2, S5_GROUPS, S5_STATE), 0.01),
        "s5_log_step": jax.random.uniform(next(keys), (N_ODD, 2, S5_GROUPS), f32, math.log(1e-3), math.log(1e-1)),
        "s5_b_re": nrm((N_ODD, 2, S5_GROUPS, S5_STATE, S5_GROUP), (2 * S5_GROUP) ** -0.5),
        "s5_b_im": nrm((N_ODD, 2, S5_GROUPS, S5_STATE, S5_GROUP), (2 * S5_GROUP) ** -0.5),
        "s5_c_re": nrm((N_ODD, 2, S5_GROUPS, S5_GROUP, S5_STATE), (2 * S5_STATE) ** -0.5),
        "s5_c_im": nrm((N_ODD, 2, S5_GROUPS, S5_GROUP, S5_STATE), (2 * S5_STATE) ** -0.5),
        "s5_d": nrm((N_ODD, S5_GROUPS, S5_GROUP)),
        "w_glu": nrm((N_ODD, d, 2 * d), d ** -0.5),
        "x_wq": nrm((DEPTH, d, X_WIDTH), d ** -0.5),
        "x_wk": nrm((DEPTH, d, X_WIDTH), d ** -0.5),
        "x_wv": nrm((DEPTH, d, X_WIDTH), d ** -0.5),
        "x_wo": nrm((DEPTH, X_WIDTH, d), X_WIDTH ** -0.5),
        "x_q_gain": gain((DEPTH, HEAD_DIM)),
        "x_k_gain": gain((DEPTH, HEAD_DIM)),
        "w_ffn_gate": nrm((DEPTH, d, FFN_HIDDEN), d ** -0.5),
        "w_ffn_up": nrm((DEPTH, d, FFN_HIDDEN), d ** -0.5),
        "w_ffn_down": nrm((DEPTH, FFN_HIDDEN, d), FFN_HIDDEN ** -0.5),
    }


def reference(x_prompt, x_sample, mem_prompt, mem_sample, g_mix, g_cross, g_mem, g_ffn,
              w_in, na_q_gain, na_k_gain, na_rpb, hy_conv_w, hy_conv_b, hy_f_w1, hy_f_b1,
              hy_f_w2, hy_f_b2, hy_f_w3, hy_f_b3, hy_f_freq, hy_log_decay, hy_bias, w_out,
              s5_lam_re, s5_lam_im, s5_log_step, s5_b_re, s5_b_im, s5_c_re, s5_c_im, s5_d, w_glu,
              x_wq, x_wk, x_wv, x_wo, x_q_gain, x_k_gain, w_ffn_gate, w_ffn_up, w_ffn_down):

    def run(x, mem):
        for layer in range(DEPTH):
            i = layer // 2
            xn = rms_norm(x, g_mix[layer])
            if layer % 2 == 0:
                mix = even_mixer(xn, w_in[i], na_q_gain[i], na_k_gain[i], na_rpb[i], hy_conv_w[i],
                                 hy_conv_b[i], hy_f_w1[i], hy_f_b1[i], hy_f_w2[i], hy_f_b2[i],
                                 hy_f_w3[i], hy_f_b3[i], hy_f_freq[i], hy_log_decay[i], hy_bias[i],
                                 w_out[i])
            else:
                mix = s5_mixer(xn, s5_lam_re[i], s5_lam_im[i], s5_log_step[i], s5_b_re[i], s5_b_im[i],
                               s5_c_re[i], s5_c_im[i], s5_d[i], w_glu[i])
            x = x + mix
            x = x + memory_cross_attention(rms_norm(x, g_cross[layer]), mem, g_mem[layer], x_wq[layer],
                                           x_wk[layer], x_wv[layer], x_wo[layer], x_q_gain[layer],
                                           x_k_gain[layer])
            x = x + swiglu(rms_norm(x, g_ffn[layer]), w_ffn_gate[layer], w_ffn_up[layer], w_ffn_down[layer])
        return x

    y_prompt = run(x_prompt, mem_prompt)
    y_sample = run(x_sample, mem_sample)
    return (y_prompt, y_sample)
```

```python
import math
from contextlib import ExitStack

import numpy as np
import ml_dtypes
import concourse.bass as bass
import concourse.mybir as mybir
from concourse.bass_utils import run_bass_kernel_spmd

F32 = mybir.dt.float32
BF16 = mybir.dt.bfloat16
AF = mybir.ActivationFunctionType
ALU = mybir.AluOpType
AX = mybir.AxisListType
EPS = 1e-6


class Tok:
    __slots__ = ("sem", "val")

    def __init__(self, sem, val):
        self.sem = sem
        self.val = val


class Buf:
    def __init__(self, name=""):
        self.name = name
        self.lw = None
        self.rd = {}
        self.ds = None
        self.ds2 = None


class Eng:
    def __init__(self, nc, name, e):
        self.name = name
        self.e = e
        self.sem = nc.alloc_semaphore("sem_" + name)
        self.cnt = 0
        self.seen = {}


class DSem:
    def __init__(self, nc, i):
        self.sem = nc.alloc_semaphore("dsem%d" % i)
        self.cnt = 0


class KB:
    def __init__(self, nc):
        self.nc = nc
        self.pe = Eng(nc, "pe", nc.tensor)
        self.act = Eng(nc, "act", nc.scalar)
        self.dve = Eng(nc, "dve", nc.vector)
        self.pool = Eng(nc, "pool", nc.gpsimd)
        self.sync = Eng(nc, "sync", nc.sync)
        self.engs = [self.pe, self.act, self.dve, self.pool, self.sync]
        self.dfree = []
        self.dfree2 = []
        self.dall = []
        self.pending = {}
        self.nds = 0

    def get_ds(self, store=False):
        fl = self.dfree2 if store else self.dfree
        if fl:
            return fl.pop()
        d = DSem(self.nc, self.nds)
        self.nds += 1
        self.dall.append(d)
        return d

    def _wait(self, E, tok):
        if tok is None:
            return
        if E is self.pe and tok.sem is E.sem:
            return
        key = id(tok.sem)
        if E.seen.get(key, 0) >= tok.val:
            return
        E.e.wait_ge(tok.sem, tok.val)
        E.seen[key] = tok.val

    def _deps(self, E, reads, writes):
        for b in reads:
            self._wait(E, b.lw)
        for b in writes:
            self._wait(E, b.lw)
            for t in b.rd.values():
                self._wait(E, t)

    def _commit(self, tok, reads, writes):
        for b in reads:
            key = id(tok.sem)
            o = b.rd.get(key)
            if o is None or o.val < tok.val:
                b.rd[key] = tok
        for b in writes:
            b.lw = tok
            b.rd = {}

    def op(self, E, fn, reads=(), writes=()):
        self._deps(E, reads, writes)
        ins = fn(E.e)
        E.cnt += 1
        ins.then_inc(E.sem, 1)
        tok = Tok(E.sem, E.cnt)
        self._commit(tok, reads, writes)
        return tok

    def op_w(self, E, fn, waits):
        for t in waits:
            self._wait(E, t)
        ins = fn(E.e)
        E.cnt += 1
        ins.then_inc(E.sem, 1)
        return Tok(E.sem, E.cnt)

    def dma(self, Q, out, in_, sb, load, extra_reads=(), extra_writes=(), **kw):
        reads = list(extra_reads) + ([] if load else [sb])
        writes = list(extra_writes) + ([sb] if load else [])
        self._deps(Q, reads, writes)
        if load:
            if sb.ds is None:
                sb.ds = self.get_ds()
            ds = sb.ds
        else:
            if sb.ds2 is None:
                sb.ds2 = self.get_ds(True)
            ds = sb.ds2
        ins = Q.e.dma_start(out=out, in_=in_, **kw)
        ds.cnt += 16
        ins.then_inc(ds.sem, 16)
        tok = Tok(ds.sem, ds.cnt)
        self.pending[id(ds.sem)] = tok
        self._commit(tok, reads, writes)
        return tok

    def release(self, bufs):
        for b in bufs:
            if b.ds is not None:
                self.dfree.append(b.ds)
                b.ds = None
            if b.ds2 is not None:
                self.dfree2.append(b.ds2)
                b.ds2 = None

    def barrier(self):
        toks = [Tok(E.sem, E.cnt) for E in self.engs if E.cnt > 0]
        toks += list(self.pending.values())
        for E in self.engs:
            for t in toks:
                if t.sem is not E.sem:
                    self._wait(E, t)
        self.pending = {}


class Phase:
    uid = 0

    def __init__(self, k):
        self.k = k
        self.es = ExitStack()
        self.bufs = []
        self.n = 0

    def sb(self, shape, dt, name=None):
        Phase.uid += 1
        t = self.es.enter_context(self.k.nc.sbuf_tensor("%s_%d" % (name or "sb", Phase.uid), list(shape), dt))
        return t

    def ps(self, shape, dt, name=None):
        Phase.uid += 1
        t = self.es.enter_context(self.k.nc.psum_tensor("%s_%d" % (name or "ps", Phase.uid), list(shape), dt))
        return t

    def buf(self, name=""):
        b = Buf(name)
        self.bufs.append(b)
        return b

    def close(self):
        self.k.barrier()
        self.k.release(self.bufs)
        self.es.close()


class Ring:
    def __init__(self, ph, n, shape, dt, name, psum=False):
        self.t = [(ph.ps if psum else ph.sb)(shape, dt, name) for _ in range(n)]
        self.b = [ph.buf(name + str(i)) for i in range(n)]
        self.i = 0
        self.n = n

    def next(self):
        j = self.i % self.n
        self.i += 1
        return self.t[j], self.b[j]


def phase_normT(k, C, src, g_ap, dstT, Tn):
    D = C["D"]
    DC = D // 128
    ph = Phase(k)
    gbc = ph.sb([128, D], F32, "gbc")
    gb = ph.buf("gbc")
    k.dma(k.sync, gbc[:], g_ap.partition_broadcast(128), gb, True)
    xr = Ring(ph, 2, [128, D], F32, "xt")
    xn = Ring(ph, 2, [128, D], BF16, "xn")
    junk = ph.sb([128, D], BF16, "junk")
    jb = ph.buf("junk")
    ssr = Ring(ph, 4, [128, 2], F32, "ss")
    st = Ring(ph, 2, [128, DC, 512], BF16, "stage")
    pt = Ring(ph, 4, [128, 8, 128], BF16, "pst", psum=True)
    TQ = min(512, Tn)
    nj = TQ // 128
    for tq in range(Tn // TQ):
        stg, stb = st.next()
        for j in range(nj):
            t0 = tq * TQ + j * 128
            xt, xb = xr.next()
            k.dma(k.sync, xt[:], src[t0:t0 + 128, :], xb, True)
            ss, sb_ = ssr.next()
            k.op(k.act, lambda e: e.activation(out=junk[:], in_=xt[:], func=AF.Square, accum_out=ss[:, 0:1]),
                 reads=[xb], writes=[jb, sb_])
            k.op(k.act, lambda e: e.activation(out=ss[:, 1:2], in_=ss[:, 0:1], func=AF.Sqrt, scale=1.0 / D, bias=C["eps"][:, 0:1]),
                 reads=[sb_, C["epsb"]], writes=[sb_])
            k.op(k.dve, lambda e: e.reciprocal(out=ss[:, 1:2], in_=ss[:, 1:2]), reads=[sb_], writes=[sb_])
            xnt, xnb = xn.next()
            k.op(k.dve, lambda e: e.scalar_tensor_tensor(out=xnt[:], in0=xt[:], scalar=ss[:, 1:2], in1=gbc[:],
                                                         op0=ALU.mult, op1=ALU.mult),
                 reads=[xb, sb_, gb], writes=[xnb])
            G8 = min(8, DC)
            for c8 in range(DC // G8):
                p, pb = pt.next()
                for c in range(G8):
                    cc = c8 * G8 + c
                    k.op(k.pe, lambda e: e.transpose(out=p[:, c, :], in_=xnt[:, cc * 128:(cc + 1) * 128],
                                                     identity=C["ident"][:]),
                         reads=[xnb, C["identb"]], writes=[pb])
                E = k.act if (c8 % 2 == 0) else k.dve
                if E is k.act:
                    k.op(E, lambda e: e.activation(out=stg[:, c8 * G8:(c8 + 1) * G8, j * 128:(j + 1) * 128],
                                                   in_=p[:, 0:G8, :], func=AF.Copy), reads=[pb], writes=[stb])
                else:
                    k.op(E, lambda e: e.tensor_copy(out=stg[:, c8 * G8:(c8 + 1) * G8, j * 128:(j + 1) * 128],
                                                    in_=p[:, 0:G8, :]), reads=[pb], writes=[stb])
        k.dma(k.pool, dstT.rearrange("(c p) t -> p c t", p=128)[:, :, tq * TQ:(tq + 1) * TQ], stg[:, :, 0:TQ],
              stb, False)
    ph.close()


def pick_nb(N, dual=False):
    nb = 256 if dual else 512
    nb = min(nb, N)
    while N % nb != 0:
        nb -= 128
    return nb


def blk_host(W, NB):
    Kd, N = W.shape
    return np.ascontiguousarray(W.reshape(Kd // 128, 128, N // NB, NB).transpose(2, 1, 0, 3))


def phase_cast_weights(k, items):
    ph = Phase(k)
    sr = Ring(ph, 3, [128, 2048], F32, "cws")
    br = Ring(ph, 3, [128, 2048], BF16, "cwb")
    ci = 0
    for (src, dst) in items:
        NBLK, _, KC, NB = dst.shape
        N = NBLK * NB
        CW = (2048 // NB) * NB
        for kc in range(KC):
            for c0 in range(0, N, CW):
                w = min(CW, N - c0)
                st_, sb_ = sr.next()
                k.dma(k.sync, st_[:, 0:w], src[kc * 128:(kc + 1) * 128, c0:c0 + w], sb_, True)
                bt, bb = br.next()
                if ci % 3 == 2:
                    k.op(k.act, lambda e: e.activation(out=bt[:, 0:w], in_=st_[:, 0:w], func=AF.Copy), reads=[sb_], writes=[bb])
                else:
                    k.op(k.dve, lambda e: e.tensor_copy(out=bt[:, 0:w], in_=st_[:, 0:w]), reads=[sb_], writes=[bb])
                ci += 1
                k.dma(k.pool, dst[c0 // NB:(c0 + w) // NB, :, kc, :].rearrange("b p n -> p b n"),
                      bt[:, 0:w].rearrange("p (b n) -> p b n", n=NB), bb, False)
    ph.close()


def gemm(k, C, actT, ws, Tn, mode, epi, TG=1024, KBmax=32, epi_setup=None):
    Kd = actT.shape[0]
    NBLK, _, KC, NB = ws[0].shape
    assert Kd == KC * 128
    nw = len(ws)
    TG = min(TG, Tn)
    nkb = (KC + KBmax - 1) // KBmax
    KBs = [KC // nkb + (1 if i < KC % nkb else 0) for i in range(nkb)]
    kb0 = [sum(KBs[:i]) for i in range(nkb)]
    KBm = max(KBs)
    ph = Phase(k)
    act = Ring(ph, 1 if KC * TG * 2 > 40000 else 2, [128, KC, TG], BF16, "act")
    wb = Ring(ph, 2 if nw * KBm * NB * 2 > 40000 else 3, [128, nw, KBm, NB], BF16, "wb")
    psr = [Ring(ph, 4 if nw == 2 else 6, [128, 512], F32, "ps%d" % i, psum=True) for i in range(nw)]
    est = epi_setup(ph) if epi_setup else None
    for tg in range(Tn // TG):
        at, ab = act.next()
        for kc in range(KC):
            k.dma(k.sync, at[:, kc, :], actT[kc * 128:(kc + 1) * 128, tg * TG:(tg + 1) * TG], ab, True)
        for nb in range(NBLK):
            if mode == "FM":
                tiles = [(m, tb) for tb in range(TG // 512 if TG >= 512 else 1) for m in range(NB // 128)]
            else:
                tiles = [(0, tt) for tt in range(TG // 128)]
            maxt = psr[0].n
            if nkb > 1:
                assert len(tiles) <= maxt
            for s0 in range(0, len(tiles), maxt):
                sub = tiles[s0:s0 + maxt]
                pts = [[psr[i].next() for i in range(nw)] for _ in sub]
                for kbi in range(nkb):
                    if s0 == 0 or nkb > 1:
                        wt, wbuf = wb.next()
                        for wi in range(nw):
                            for q0 in range(0, KBs[kbi], 8):
                                q1 = min(q0 + 8, KBs[kbi])
                                k.dma(k.sync, wt[:, wi, q0:q1, :], ws[wi][nb, :, kb0[kbi] + q0:kb0[kbi] + q1, :], wbuf, True)
                    for ti, (m, tb) in enumerate(sub):
                        for wi in range(nw):
                            p, pb = pts[ti][wi]
                            for kk in range(KBs[kbi]):
                                kc = kb0[kbi] + kk
                                first = (kbi == 0 and kk == 0)
                                last = (kbi == nkb - 1 and kk == KBs[kbi] - 1)
                                if mode == "FM":
                                    tsz = min(512, TG)
                                    k.op(k.pe, lambda e: e.matmul(p[:, 0:tsz], lhsT=wt[:, wi, kk, m * 128:(m + 1) * 128],
                                                                  rhs=at[:, kc, tb * 512:tb * 512 + tsz],
                                                                  start=first, stop=last),
                                         reads=[wbuf, ab], writes=[pb])
                                else:
                                    k.op(k.pe, lambda e: e.matmul(p[:, 0:NB], lhsT=at[:, kc, tb * 128:(tb + 1) * 128],
                                                                  rhs=wt[:, wi, kk, :], start=first, stop=last),
                                         reads=[wbuf, ab], writes=[pb])
                for ti, (m, tb) in enumerate(sub):
                    if mode == "FM":
                        epi(ph, est, pts[ti], nb * NB + m * 128, 128, tg * TG + tb * 512, min(512, TG))
                    else:
                        epi(ph, est, pts[ti], nb * NB, NB, tg * TG + tb * 128, 128)
    ph.close()


def epi_store_fm(k, dstT, row0=0):
    def setup(ph):
        return Ring(ph, 3, [128, 512], BF16, "eo")

    def epi(ph, st, pts, n0, nsz, t0, tsz):
        p, pb = pts[0]
        o, ob = st.next()
        k.op(k.act, lambda e: e.activation(out=o[:, 0:tsz], in_=p[:, 0:tsz], func=AF.Copy), reads=[pb], writes=[ob])
        k.dma(k.pool, dstT[row0 + n0:row0 + n0 + 128, t0:t0 + tsz], o[:, 0:tsz], ob, False)
    return setup, epi


def epi_store_tm(k, dst, col0=0):
    def setup(ph):
        return Ring(ph, 3, [128, 512], BF16, "eo")

    def epi(ph, st, pts, n0, nsz, t0, tsz):
        p, pb = pts[0]
        o, ob = st.next()
        k.op(k.act, lambda e: e.activation(out=o[:, 0:nsz], in_=p[:, 0:nsz], func=AF.Copy), reads=[pb], writes=[ob])
        k.dma(k.pool, dst[t0:t0 + 128, col0 + n0:col0 + n0 + nsz], o[:, 0:nsz], ob, False)
    return setup, epi


def epi_headnorm_fm(k, C, dstT, gain_ap, row0=0):
    def setup(ph):
        g = ph.sb([128, 2], F32, "hg")
        gb = ph.buf("hg")
        k.dma(k.sync, g[:, 0:1], gain_ap.rearrange("(p o) -> p o", o=1), gb, True)
        k.op(k.dve, lambda e: e.tensor_copy(out=g[:, 1:2], in_=g[:, 0:1]), reads=[gb], writes=[gb])
        return dict(g=g, gb=gb, sq=Ring(ph, 2, [128, 512], BF16, "sq"), ps=Ring(ph, 2, [128, 512], F32, "pss", psum=True),
                    rs=Ring(ph, 2, [128, 512], F32, "rs"), o=Ring(ph, 3, [128, 512], BF16, "eo"))

    def epi(ph, st, pts, n0, nsz, t0, tsz):
        p, pb = pts[0]
        sq, sqb = st["sq"].next()
        k.op(k.act, lambda e: e.activation(out=sq[:, 0:tsz], in_=p[:, 0:tsz], func=AF.Square), reads=[pb], writes=[sqb])
        ps2, ps2b = st["ps"].next()
        k.op(k.pe, lambda e: e.matmul(ps2[:, 0:tsz], lhsT=C["ones"][:], rhs=sq[:, 0:tsz], start=True, stop=True),
             reads=[sqb, C["onesb"]], writes=[ps2b])
        rs, rsb = st["rs"].next()
        k.op(k.act, lambda e: e.activation(out=rs[:, 0:tsz], in_=ps2[:, 0:tsz], func=AF.Sqrt, scale=1.0 / 128.0,
                                           bias=C["eps"][:, 0:1]), reads=[ps2b, C["epsb"]], writes=[rsb])
        k.op(k.dve, lambda e: e.reciprocal(out=rs[:, 0:tsz], in_=rs[:, 0:tsz]), reads=[rsb], writes=[rsb])
        o, ob = st["o"].next()
        k.op(k.dve, lambda e: e.scalar_tensor_tensor(out=o[:, 0:tsz], in0=p[:, 0:tsz], scalar=st["g"][:, 1:2],
                                                     in1=rs[:, 0:tsz], op0=ALU.mult, op1=ALU.mult),
             reads=[pb, rsb, st["gb"]], writes=[ob])
        k.dma(k.pool, dstT[row0 + n0:row0 + n0 + 128, t0:t0 + tsz], o[:, 0:tsz], ob, False)
    return setup, epi


def epi_resid_tm(k, src, dst, glu=False):
    def setup(ph):
        return dict(r=Ring(ph, 3, [128, 512], F32, "res"), sg=Ring(ph, 2, [128, 512], F32, "sg"))

    def epi(ph, st, pts, n0, nsz, t0, tsz):
        r, rb = st["r"].next()
        k.dma(k.pool, r[:, 0:nsz], src[t0:t0 + 128, n0:n0 + nsz], rb, True)
        p, pb = pts[0]
        if glu:
            p2, pb2 = pts[1]
            sg, sgb = st["sg"].next()
            k.op(k.act, lambda e: e.activation(out=sg[:, 0:nsz], in_=p2[:, 0:nsz], func=AF.Sigmoid),
                 reads=[pb2], writes=[sgb])
            k.op(k.dve, lambda e: e.tensor_tensor(out=sg[:, 0:nsz], in0=sg[:, 0:nsz], in1=p[:, 0:nsz], op=ALU.mult),
                 reads=[pb, sgb], writes=[sgb])
            k.op(k.dve, lambda e: e.tensor_tensor(out=r[:, 0:nsz], in0=r[:, 0:nsz], in1=sg[:, 0:nsz], op=ALU.add),
                 reads=[rb, sgb], writes=[rb])
        else:
            k.op(k.dve, lambda e: e.tensor_tensor(out=r[:, 0:nsz], in0=r[:, 0:nsz], in1=p[:, 0:nsz], op=ALU.add),
                 reads=[rb, pb], writes=[rb])
        k.dma(k.pool, dst[t0:t0 + 128, n0:n0 + nsz], r[:, 0:nsz], rb, False)
    return setup, epi


def epi_swiglu_fm(k, dstT):
    def setup(ph):
        return dict(s=Ring(ph, 2, [128, 512], F32, "sl"), o=Ring(ph, 3, [128, 512], BF16, "eo"))

    def epi(ph, st, pts, n0, nsz, t0, tsz):
        pg, pgb = pts[0]
        pu, pub = pts[1]
        s, sb_ = st["s"].next()
        k.op(k.act, lambda e: e.activation(out=s[:, 0:tsz], in_=pg[:, 0:tsz], func=AF.Silu), reads=[pgb], writes=[sb_])
        o, ob = st["o"].next()
        k.op(k.dve, lambda e: e.tensor_tensor(out=o[:, 0:tsz], in0=s[:, 0:tsz], in1=pu[:, 0:tsz], op=ALU.mult),
             reads=[sb_, pub], writes=[ob])
        k.dma(k.pool, dstT[n0:n0 + 128, t0:t0 + tsz], o[:, 0:tsz], ob, False)
    return setup, epi


def phase_cross_attn(k, C, cqT, ckT, cv, coT, Tn):
    XH, NM = C["XH"], C["NM"]
    Th = Tn // 2
    KCH = NM // 128
    ph = Phase(k)
    kt = ph.sb([128, XH, 2 * NM], BF16, "ckt")
    ktb = ph.buf("ckt")
    for h in range(XH):
        k.dma(k.sync, kt[:, h, :], ckT[h * 128:(h + 1) * 128, :], ktb, True)
    vt = ph.sb([128, 2 * KCH, XH * 128], BF16, "cvt")
    vtb = ph.buf("cvt")
    k.dma(k.sync, vt[:], cv.rearrange("(c p) n -> p c n", p=128), vtb, True)
    qr = Ring(ph, 2, [128, 512], BF16, "cq")
    pss = Ring(ph, 2, [128, KCH, 512], F32, "pss", psum=True)
    pso = Ring(ph, 2, [128, 512], F32, "pso", psum=True)
    psd = Ring(ph, 2, [128, 512], F32, "psd", psum=True)
    pr = Ring(ph, 2, [128, KCH, 512], BF16, "pT")
    rd = Ring(ph, 2, [128, 512], F32, "rden")
    orr = Ring(ph, 3, [128, 512], BF16, "co")
    scale = 1.0 / math.sqrt(128.0)
    TB = min(512, Th)
    for h in range(XH):
        for hs in range(2):
            for tb in range(Th // TB):
                t0 = hs * Th + tb * TB
                q, qb = qr.next()
                k.dma(k.sync, q[:, 0:TB], cqT[h * 128:(h + 1) * 128, t0:t0 + TB], qb, True)
                s, sb_ = pss.next()
                for kc in range(KCH):
                    k.op(k.pe, lambda e: e.matmul(s[:, kc, 0:TB], lhsT=kt[:, h, hs * NM + kc * 128:hs * NM + (kc + 1) * 128],
                                                  rhs=q[:, 0:TB], start=True, stop=True), reads=[ktb, qb], writes=[sb_])
                p, pb = pr.next()
                k.op(k.act, lambda e: e.activation(out=p[:, :, 0:TB], in_=s[:, :, 0:TB], func=AF.Exp, scale=scale),
                     reads=[sb_], writes=[pb])
                o, ob = pso.next()
                d, db = psd.next()
                for kc in range(KCH):
                    k.op(k.pe, lambda e: e.matmul(o[:, 0:TB], lhsT=vt[:, hs * KCH + kc, h * 128:(h + 1) * 128],
                                                  rhs=p[:, kc, 0:TB], start=(kc == 0), stop=(kc == KCH - 1)),
                         reads=[vtb, pb], writes=[ob])
                for kc in range(KCH):
                    k.op(k.pe, lambda e: e.matmul(d[:, 0:TB], lhsT=C["ones"][:], rhs=p[:, kc, 0:TB],
                                                  start=(kc == 0), stop=(kc == KCH - 1)),
                         reads=[C["onesb"], pb], writes=[db])
                r, rb = rd.next()
                k.op(k.dve, lambda e: e.reciprocal(out=r[:, 0:TB], in_=d[:, 0:TB]), reads=[db], writes=[rb])
                oo, oob = orr.next()
                k.op(k.dve, lambda e: e.tensor_tensor(out=oo[:, 0:TB], in0=o[:, 0:TB], in1=r[:, 0:TB], op=ALU.mult),
                     reads=[ob, rb], writes=[oob])
                k.dma(k.pool, coT[h * 128:(h + 1) * 128, t0:t0 + TB], oo[:, 0:TB], oob, False)
    ph.close()


def setup_consts(k, C, ident_ap):
    nc = k.nc
    C["es"] = ExitStack()
    C["ident"] = C["es"].enter_context(nc.sbuf_tensor("ident_sb", [128, 128], BF16))
    C["identb"] = Buf("ident")
    k.dma(k.sync, C["ident"][:], ident_ap, C["identb"], True)
    C["ones"] = C["es"].enter_context(nc.sbuf_tensor("ones_sb", [128, 128], BF16))
    C["onesb"] = Buf("ones")
    k.op(k.dve, lambda e: e.memset(C["ones"][:], 1.0), writes=[C["onesb"]])
    C["eps"] = C["es"].enter_context(nc.sbuf_tensor("eps_sb", [128, 2], F32))
    C["epsb"] = Buf("eps")
    k.op(k.dve, lambda e: e.memset(C["eps"][:], EPS), writes=[C["epsb"]])


def dense_tail(k, C, A, layer, Tn):
    D, F, XW, NM = C["D"], C["F"], C["XW"], C["NM"]
    y = A["y"]
    W = A["Wb"]
    phase_normT(k, C, y, A["g_cross"][layer], A["xnT"], Tn)
    phase_normT(k, C, A["mem"], A["g_mem"][layer], A["mnT"], 2 * NM)
    s, e = epi_headnorm_fm(k, C, A["cqT"], A["x_q_gain"][layer])
    gemm(k, C, A["xnT"], [W["x_wq%d" % layer]], Tn, "FM", e, epi_setup=s)
    s, e = epi_headnorm_fm(k, C, A["ckT"], A["x_k_gain"][layer])
    gemm(k, C, A["mnT"], [W["x_wk%d" % layer]], 2 * NM, "FM", e, epi_setup=s)
    s, e = epi_store_tm(k, A["cv"])
    gemm(k, C, A["mnT"], [W["x_wv%d" % layer]], 2 * NM, "TM", e, epi_setup=s)
    phase_cross_attn(k, C, A["cqT"], A["ckT"], A["cv"], A["coT"], Tn)
    s, e = epi_resid_tm(k, y, y)
    gemm(k, C, A["coT"], [W["x_wo%d" % layer]], Tn, "TM", e, epi_setup=s)
    phase_normT(k, C, y, A["g_ffn"][layer], A["xnT"], Tn)
    s, e = epi_swiglu_fm(k, A["hT"])
    gemm(k, C, A["xnT"], [W["gate%d" % layer], W["up%d" % layer]], Tn, "FM", e, epi_setup=s)
    s, e = epi_resid_tm(k, y, y)
    gemm(k, C, A["hT"], [W["down%d" % layer]], Tn, "TM", e, epi_setup=s, TG=512, KBmax=22)


def cmul_acc(k, E, dre, dim, sre, sim, pr, pi, npi, bufs):
    R, W = bufs
    k.op(E, lambda e: e.scalar_tensor_tensor(out=dre, in0=sre, scalar=pr, in1=dre, op0=ALU.mult, op1=ALU.add), reads=R, writes=W)
    k.op(E, lambda e: e.scalar_tensor_tensor(out=dre, in0=sim, scalar=npi, in1=dre, op0=ALU.mult, op1=ALU.add), reads=R, writes=W)
    k.op(E, lambda e: e.scalar_tensor_tensor(out=dim, in0=sim, scalar=pr, in1=dim, op0=ALU.mult, op1=ALU.add), reads=R, writes=W)
    k.op(E, lambda e: e.scalar_tensor_tensor(out=dim, in0=sre, scalar=pi, in1=dim, op0=ALU.mult, op1=ALU.add), reads=R, writes=W)


def scan_seq(k, E, h, Tn, Th, rev, PW, FL, slab, bufs, flag_ap, sv):
    hre, him = h[:, 0, :], h[:, 1, :]
    nl = Tn.bit_length() - 1
    assert (1 << nl) == Tn
    R, Wr = bufs
    k._deps(E, R, Wr)

    def sel(hh, l, i, k0=None, k1=None):
        st = 1 << l
        v = hh.rearrange("p (k i s) -> p k i s", i=2, s=st)
        ii = (1 - i) if rev else i
        ss = 0 if rev else st - 1
        if k0 is None:
            return v[:, :, ii, ss]
        return v[:, k0:k1, ii, ss]

    def cm(dre, dim, sre, sim, sc, prev):
        pr, pi, npi = sc
        STT = lambda o, i0, sca, i1: (lambda e: e.scalar_tensor_tensor(out=o, in0=i0, scalar=sca, in1=i1, op0=ALU.mult, op1=ALU.add))
        a = k.op_w(E, STT(dre, sre, pr, dre), prev)
        b = k.op_w(E, STT(dim, sim, pr, dim), prev)
        c = k.op_w(E, STT(dre, sim, npi, dre), [a])
        d = k.op_w(E, STT(dim, sre, pi, dim), [b])
        return [c, d]

    def scal(t):
        return t[0][:, slab:slab + 1], t[1][:, slab:slab + 1], t[2][:, slab:slab + 1]

    prev = []
    for l in range(nl):
        sc = scal(FL if (1 << l) == Th else PW[l])
        prev = cm(sel(hre, l, 1), sel(him, l, 1), sel(hre, l, 0), sel(him, l, 0), sc, prev)
    pos = Th if rev else Th - 1
    t1 = k.op_w(E, lambda e: e.tensor_copy(out=sv[:, :], in_=h[:, :, pos]), prev)
    t2 = k.op_w(E, lambda e: e.tensor_scalar_mul(out=h[:, :, pos], in0=h[:, :, pos], scalar1=flag_ap), [t1])
    prev = [t2]
    for l in range(nl - 2, -1, -1):
        K = Tn >> (l + 1)
        sc = scal(PW[l])
        if rev:
            prev = cm(sel(hre, l, 0, 0, K - 1), sel(him, l, 0, 0, K - 1), sel(hre, l, 1, 1, K), sel(him, l, 1, 1, K), sc, prev)
        else:
            prev = cm(sel(hre, l, 0, 1, K), sel(him, l, 0, 1, K), sel(hre, l, 1, 0, K - 1), sel(him, l, 1, 0, K - 1), sc, prev)
    t3 = k.op_w(E, lambda e: e.tensor_copy(out=h[:, :, pos], in_=sv[:, :]), prev)
    k._commit(t3, R, Wr)


def phase_s5(k, C, A, Tn):
    D, G = C["D"], C["G"]
    DC = D // 128
    NS = G
    Th = Tn // 2
    ph = Phase(k)
    names = ["lr", "li", "ls", "st", "a", "th", "mag", "r1", "r2", "sn", "cs", "lbr", "lbi", "den", "t1", "t2", "cr", "ci", "nci"]
    P = {n: ph.sb([128, NS], F32, "s5" + n) for n in names}
    pb = ph.buf("s5par")
    cst = ph.sb([128, 4], F32, "s5c")
    k.op(k.dve, lambda e: e.memset(cst[:, 0:1], -math.pi), writes=[pb])
    k.dma(k.sync, P["lr"][:], A["s5_lr"], pb, True)
    k.dma(k.sync, P["li"][:], A["s5_li"], pb, True)
    k.dma(k.sync, P["ls"][:], A["s5_ls"], pb, True)
    flag = ph.sb([128, 1], F32, "flag")
    k.dma(k.sync, flag[:], A["flag"], pb, True)
    V = k.dve
    RW = dict(reads=[pb], writes=[pb])

    def tt(o, a_, b_, op):
        k.op(V, lambda e: e.tensor_tensor(out=P[o][:], in0=P[a_][:], in1=P[b_][:], op=op), **RW)

    def ts(o, a_, s1, s2, o0, o1):
        k.op(V, lambda e: e.tensor_scalar(out=P[o][:], in0=P[a_][:], scalar1=s1, scalar2=s2, op0=o0, op1=o1), **RW)

    k.op(k.act, lambda e: e.activation(out=P["st"][:], in_=P["ls"][:], func=AF.Exp), **RW)
    k.op(V, lambda e: e.tensor_scalar_min(out=P["lr"][:], in0=P["lr"][:], scalar1=-1e-4), **RW)
    tt("a", "lr", "st", ALU.mult)
    tt("th", "li", "st", ALU.mult)
    k.op(k.act, lambda e: e.activation(out=P["mag"][:], in_=P["a"][:], func=AF.Exp), **RW)
    MAGIC = 12582912.0
    for (o, sh) in (("sn", 0.0), ("cs", 0.5 * math.pi)):
        ts("r1", "th", sh, 1.0, ALU.add, ALU.mult)
        ts("r2", "r1", 1.0 / (2 * math.pi), MAGIC, ALU.mult, ALU.add)
        ts("r2", "r2", -MAGIC, -2 * math.pi, ALU.add, ALU.mult)
        tt("r1", "r1", "r2", ALU.add)
        ts("r1", "r1", math.pi, -math.pi, ALU.min, ALU.max)
        k.op(k.act, lambda e: e.activation(out=P[o][:], in_=P["r1"][:], func=AF.Sin), **RW)
    tt("lbr", "mag", "cs", ALU.mult)
    tt("lbi", "mag", "sn", ALU.mult)
    tt("den", "lr", "lr", ALU.mult)
    tt("t1", "li", "li", ALU.mult)
    tt("den", "den", "t1", ALU.add)
    k.op(V, lambda e: e.reciprocal(out=P["den"][:], in_=P["den"][:]), **RW)
    ts("t1", "lbr", -1.0, 1.0, ALU.add, ALU.mult)
    tt("cr", "t1", "lr", ALU.mult)
    tt("t2", "lbi", "li", ALU.mult)
    tt("cr", "cr", "t2", ALU.add)
    tt("cr", "cr", "den", ALU.mult)
    tt("ci", "lbi", "lr", ALU.mult)
    tt("t2", "t1", "li", ALU.mult)
    tt("ci", "ci", "t2", ALU.subtract)
    tt("ci", "ci", "den", ALU.mult)
    ts("nci", "ci", -1.0, 1.0, ALU.mult, ALU.mult)
    nl = Tn.bit_length() - 1
    PW = []
    cur = (P["lbr"], P["lbi"])
    for l in range(nl):
        if l > 0:
            pr_ = ph.sb([128, NS], F32, "pwr")
            pi_ = ph.sb([128, NS], F32, "pwi")
            t_ = P["t1"]
            k.op(V, lambda e: e.tensor_tensor(out=pr_[:], in0=cur[0][:], in1=cur[0][:], op=ALU.mult), **RW)
            k.op(V, lambda e: e.tensor_tensor(out=t_[:], in0=cur[1][:], in1=cur[1][:], op=ALU.mult), **RW)
            k.op(V, lambda e: e.tensor_tensor(out=pr_[:], in0=pr_[:], in1=t_[:], op=ALU.subtract), **RW)
            k.op(V, lambda e: e.tensor_tensor(out=pi_[:], in0=cur[0][:], in1=cur[1][:], op=ALU.mult), **RW)
            k.op(V, lambda e: e.tensor_scalar_mul(out=pi_[:], in0=pi_[:], scalar1=2.0), **RW)
            cur = (pr_, pi_)
        npi_ = ph.sb([128, NS], F32, "pwn")
        k.op(V, lambda e: e.tensor_scalar_mul(out=npi_[:], in0=cur[1][:], scalar1=-1.0), **RW)
        PW.append((cur[0], cur[1], npi_))
    ltop = Th.bit_length() - 1
    flr = ph.sb([128, NS], F32, "flr")
    fli = ph.sb([128, NS], F32, "fli")
    nfli = ph.sb([128, NS], F32, "nfli")
    k.op(V, lambda e: e.tensor_scalar_mul(out=flr[:], in0=PW[ltop][0][:], scalar1=flag[:, 0:1]), **RW)
    k.op(V, lambda e: e.tensor_scalar_mul(out=fli[:], in0=PW[ltop][1][:], scalar1=flag[:, 0:1]), **RW)
    k.op(V, lambda e: e.tensor_scalar_mul(out=nfli[:], in0=fli[:], scalar1=-1.0), **RW)
    FL = (flr, fli, nfli)
    sv = ph.sb([128, 2], F32, "s5sv")
    dsk = ph.sb([128, DC], F32, "dsk")
    k.dma(k.sync, dsk[:], A["s5_dT"], pb, True)

    xr = Ring(ph, 1, [128, Tn], BF16, "s5x")
    hR = Ring(ph, 2, [128, 2, Tn], F32, "s5h")
    bst = Ring(ph, 2, [128, 2, 128], F32, "s5bs")
    bbf = Ring(ph, 2, [128, 2, 128], BF16, "s5bb")
    cstg = Ring(ph, 2, [128, 2, 128], F32, "s5cs")
    cbf = Ring(ph, 3, [128, 2, 128], F32, "s5cb")
    ctmp = Ring(ph, 2, [128, 2, 128], F32, "s5ct")
    yacc = Ring(ph, 2, [128, Tn], F32, "s5y")
    yo = Ring(ph, 3, [128, 512], BF16, "s5yo")
    gtmp = Ring(ph, 2, [128, 512], F32, "s5g")
    psb = Ring(ph, 4, [128, 512], F32, "s5pb", psum=True)
    psy = Ring(ph, 4, [128, 512], F32, "s5py", psum=True)
    NTB = Tn // 512 if Tn >= 512 else 1
    TB = min(512, Tn)
    E = k.dve

    def emit_out(h, hb, cb_, cbb, ya, yab):
        for tb in range(NTB):
            py, pyb = psy.next()
            k.op(k.pe, lambda e: e.matmul(py[:, 0:TB], lhsT=cb_[:, 0, :], rhs=h[:, 0, tb * TB:(tb + 1) * TB], start=True, stop=False),
                 reads=[cbb, hb], writes=[pyb])
            k.op(k.pe, lambda e: e.matmul(py[:, 0:TB], lhsT=cb_[:, 1, :], rhs=h[:, 1, tb * TB:(tb + 1) * TB], start=False, stop=True),
                 reads=[cbb, hb], writes=[pyb])
            k.op(k.dve, lambda e: e.tensor_tensor(out=ya[:, tb * TB:(tb + 1) * TB], in0=ya[:, tb * TB:(tb + 1) * TB], in1=py[:, 0:TB], op=ALU.add),
                 reads=[pyb, yab], writes=[yab])

    def emit_gelu(oc, ya, yab):
        for tb in range(NTB):
            sl = slice(tb * TB, (tb + 1) * TB)
            g, gb_ = gtmp.next()
            k.op(k.dve, lambda e: e.tensor_tensor(out=g[:, 0:TB], in0=ya[:, sl], in1=ya[:, sl], op=ALU.mult), reads=[yab], writes=[gb_])
            k.op(k.dve, lambda e: e.tensor_scalar(out=g[:, 0:TB], in0=g[:, 0:TB], scalar1=0.044715, scalar2=1.0, op0=ALU.mult, op1=ALU.add), reads=[gb_], writes=[gb_])
            k.op(k.dve, lambda e: e.tensor_tensor(out=g[:, 0:TB], in0=g[:, 0:TB], in1=ya[:, sl], op=ALU.mult), reads=[gb_, yab], writes=[gb_])
            k.op(k.act, lambda e: e.activation(out=g[:, 0:TB], in_=g[:, 0:TB], func=AF.Sigmoid, scale=1.5957691216057308), reads=[gb_], writes=[gb_])
            o, ob = yo.next()
            k.op(k.dve, lambda e: e.tensor_tensor(out=o[:, 0:TB], in0=g[:, 0:TB], in1=ya[:, sl], op=ALU.mult), reads=[gb_, yab], writes=[ob])
            k.dma(k.pool, A["s5T"][oc * 128:(oc + 1) * 128, sl], o[:, 0:TB], ob, False)

    pend = None
    pend_gelu = None
    for oc in range(DC):
        x, xb = xr.next()
        k.dma(k.sync, x[:], A["xnT"][oc * 128:(oc + 1) * 128, :], xb, True)
        ya, yab = yacc.next()
        k.op(k.dve, lambda e: e.tensor_scalar_mul(out=ya[:], in0=x[:], scalar1=dsk[:, oc:oc + 1]), reads=[xb, pb], writes=[yab])
        for d in range(2):
            for gq in range(4):
                gp = oc * 4 + gq
                slab = d * (G // 2) + gp
                bs, bsb = bst.next()
                k.dma(k.sync, bs[:, 0, :], A["s5_Bre"][:, slab * 128:(slab + 1) * 128], bsb, True)
                k.dma(k.sync, bs[:, 1, :], A["s5_Bim"][:, slab * 128:(slab + 1) * 128], bsb, True)
                bb, bbb = bbf.next()
                k.op(k.act, lambda e: e.activation(out=bb[:], in_=bs[:], func=AF.Copy), reads=[bsb], writes=[bbb])
                cs_, csb = cstg.next()
                k.dma(k.sync, cs_[:, 0, :], A["s5_Cre"][:, slab * 128:(slab + 1) * 128], csb, True)
                k.dma(k.sync, cs_[:, 1, :], A["s5_Cim"][:, slab * 128:(slab + 1) * 128], csb, True)
                ct, ctb = ctmp.next()
                cb_, cbb = cbf.next()
                crs, cis, ncis = P["cr"][:, slab:slab + 1], P["ci"][:, slab:slab + 1], P["nci"][:, slab:slab + 1]
                k.op(E, lambda e: e.tensor_scalar_mul(out=ct[:, 0, :], in0=cs_[:, 0, :], scalar1=crs), reads=[csb, pb], writes=[ctb])
                k.op(E, lambda e: e.scalar_tensor_tensor(out=cb_[:, 0, :], in0=cs_[:, 1, :], scalar=ncis, in1=ct[:, 0, :],
                                                         op0=ALU.mult, op1=ALU.add), reads=[csb, ctb, pb], writes=[cbb])
                k.op(E, lambda e: e.tensor_scalar_mul(out=ct[:, 1, :], in0=cs_[:, 0, :], scalar1=ncis), reads=[csb, pb], writes=[ctb])
                k.op(E, lambda e: e.tensor_scalar(out=ct[:, 0, :], in0=cs_[:, 1, :], scalar1=crs, scalar2=-1.0, op0=ALU.mult, op1=ALU.mult),
                     reads=[csb, pb], writes=[ctb])
                k.op(E, lambda e: e.tensor_tensor(out=cb_[:, 1, :], in0=ct[:, 1, :], in1=ct[:, 0, :], op=ALU.add), reads=[ctb], writes=[cbb])
                h, hb = hR.next()
                for part in range(2):
                    for tb in range(NTB):
                        p, pbb = psb.next()
                        k.op(k.pe, lambda e: e.matmul(p[:, 0:TB], lhsT=bb[:, part, :], rhs=x[:, tb * TB:(tb + 1) * TB], start=True, stop=True),
                             reads=[bbb, xb], writes=[pbb])
                        k.op(k.act, lambda e: e.activation(out=h[:, part, tb * TB:(tb + 1) * TB], in_=p[:, 0:TB], func=AF.Copy),
                             reads=[pbb], writes=[hb])
                bufs = ([hb, pb], [hb])
                scan_seq(k, E, h, Tn, Th, (d == 1), PW, FL, slab, bufs, flag[:, 0:1], sv)
                if pend is not None:
                    emit_out(*pend)
                if pend_gelu is not None:
                    emit_gelu(*pend_gelu)
                    pend_gelu = None
                pend = (h, hb, cb_, cbb, ya, yab)
        pend_gelu = (oc, ya, yab)
    emit_out(*pend)
    emit_gelu(*pend_gelu)
    ph.close()


def s5_host_layout(inp, G):
    P = 64
    out = {}

    def st(a):
        a = np.asarray(a).reshape(2, G // 2, 2, P)
        return np.ascontiguousarray(a.transpose(2, 3, 0, 1).reshape(128, G))

    out["s5_lr"] = st(inp["s5_lam_re"][0])
    out["s5_li"] = st(inp["s5_lam_im"][0])
    out["s5_ls"] = st(np.broadcast_to(np.asarray(inp["s5_log_step"][0])[:, :, None], (2, G, P)))
    NS = G
    for nm, src in (("s5_Bre", inp["s5_b_re"][0]), ("s5_Bim", inp["s5_b_im"][0])):
        src = np.asarray(src)
        blk = np.zeros((8, 16, NS, 2, P), np.float32)
        for d in range(2):
            for gp in range(G // 2):
                slab = d * (G // 2) + gp
                for g2 in range(2):
                    gl = 2 * (gp % 4) + g2
                    blk[gl, :, slab, g2, :] = src[d, 2 * gp + g2].T
        out[nm] = blk.reshape(128, NS * 128)
    for nm, src in (("s5_Cre", inp["s5_c_re"][0]), ("s5_Cim", inp["s5_c_im"][0])):
        src = np.asarray(src)
        blk = np.zeros((2, P, NS, 8, 16), np.float32)
        for d in range(2):
            for gp in range(G // 2):
                slab = d * (G // 2) + gp
                for g2 in range(2):
                    gl = 2 * (gp % 4) + g2
                    blk[g2, :, slab, gl, :] = src[d, 2 * gp + g2].T
        out[nm] = blk.reshape(128, NS * 128)
    dsk = np.asarray(inp["s5_d"][0]).reshape(-1)
    out["s5_dT"] = np.ascontiguousarray(dsk.reshape(-1, 128).T)
    return out


def na_host_tables(rpb, Rw, kind):
    rpb = np.asarray(rpb)
    H = rpb.shape[0]
    kc = np.arange(64)[:, None]
    qc = np.arange(64)[None, :]
    cidx = np.clip(kc - qc + 15, 0, 30)
    qstart = np.clip(qc - 8, 0, 48)
    cvalid = (kc >= qstart) & (kc < qstart + 16)
    TT = np.zeros((2, 64, H, 14, 64), np.float32)
    for kp in range(2):
        for e in range(14):
            TT[kp, :, :, e, :] = rpb[:, e + kp][:, cidx].transpose(1, 0, 2)
    CM = np.where(cvalid, 0.0, -30000.0).astype(np.float32)
    CM = np.concatenate([CM, CM], 0)
    grids = [(0, Rw)] if kind == "p" else [(0, Rw // 2), (Rw // 2, Rw // 2)]
    rm = np.full((Rw, Rw), -30000.0, np.float32)
    for base, Rg in grids:
        kr_ = min(8, Rg)
        for rr in range(Rg):
            r0 = int(np.clip(rr - kr_ // 2, 0, Rg - kr_))
            rm[base + rr, base + r0:base + r0 + kr_] = 0.0
    NP = Rw // 2
    RMq = np.full((2, NP, 7, 2, 64), -30000.0, np.float32)
    for pi in range(NP):
        r = 2 * pi
        for ci in range(7):
            kr = r - 6 + 2 * ci
            if kr < 0 or kr > Rw - 2:
                continue
            for kp in range(2):
                for qp in range(2):
                    RMq[kp, pi, ci, qp, :] = rm[r + qp, kr + kp]
    Aind = np.zeros((2, 2, 64), np.float32)
    Aind[0, 0] = 1.0
    Aind[1, 1] = 1.0
    return dict(na_TT=TT.reshape(128, H * 14 * 64), na_CM=CM,
                na_RMq=RMq.reshape(2, NP * 7 * 128).astype(ml_dtypes.bfloat16),
                na_A=Aind.reshape(2, 128).astype(ml_dtypes.bfloat16))


def phase_na(k, C, A, Tn):
    H = C["NAH"]
    Rw = Tn // 64
    NP = Rw // 2
    NT = Tn // 128
    ph = Phase(k)
    TT = ph.sb([128, H, 14, 64], F32, "naTT")
    ttb = ph.buf("naTT")
    k.dma(k.sync, TT[:], A["na_TT"].rearrange("p (h e q) -> p h e q", h=H, e=14), ttb, True)
    CM = ph.sb([128, 64], F32, "naCM")
    k.dma(k.sync, CM[:], A["na_CM"], ttb, True)
    for h in range(H):
        k.op(k.dve, lambda e: e.tensor_tensor(out=TT[:, h, :, :], in0=TT[:, h, :, :],
                                              in1=CM[:].unsqueeze(1).broadcast_to([128, 14, 64]), op=ALU.add), reads=[ttb], writes=[ttb])
        k.op(k.dve, lambda e: e.tensor_scalar_mul(out=TT[:, h, :, :], in0=TT[:, h, :, :], scalar1=math.sqrt(128.0)), reads=[ttb], writes=[ttb])
    RM = ph.sb([2, NP * 7 * 128], BF16, "naRM")
    rmb = ph.buf("naRM")
    k.dma(k.sync, RM[:], A["na_RMq"], rmb, True)
    Ai = ph.sb([2, 128], BF16, "naA")
    k.dma(k.sync, Ai[:], A["na_A"], rmb, True)
    qr = Ring(ph, 2, [128, Tn], BF16, "naq")
    kr_ = Ring(ph, 2, [128, Tn], BF16, "nak")
    vr = Ring(ph, 2, [128, NT, 128], BF16, "nav")
    og = Ring(ph, 2, [128, Tn], BF16, "nao")
    pS = Ring(ph, 2, [128, 8, 128], F32, "naS", psum=True)
    pO = Ring(ph, 2, [128, 128], F32, "naO", psum=True)
    pD = Ring(ph, 2, [128, 128], F32, "naD", psum=True)
    sS = Ring(ph, 2, [128, 8, 128], F32, "naSs")
    sP = Ring(ph, 2, [128, 8, 128], BF16, "naP")
    rD = Ring(ph, 2, [128, 128], F32, "naR")
    scale = 1.0 / math.sqrt(128.0)
    for h in range(H):
        q, qb = qr.next()
        kk, kb = kr_.next()
        v, vb = vr.next()
        o, ob = og.next()
        k.dma(k.sync, q[:], A["qT"][h * 128:(h + 1) * 128, :], qb, True)
        k.dma(k.sync, kk[:], A["kT"][h * 128:(h + 1) * 128, :], kb, True)
        k.dma(k.sync, v[:], A["v"].rearrange("(n p) c -> p n c", p=128)[:, :, h * 128:(h + 1) * 128], vb, True)
        for pi in range(NP):
            r = 2 * pi
            chunks = [ci for ci in range(7) if 0 <= r - 6 + 2 * ci <= Rw - 2]
            S, Sb = pS.next()
            for ci in chunks:
                krow = r - 6 + 2 * ci
                k.op(k.pe, lambda e: e.matmul(S[:, ci, :], lhsT=kk[:, 64 * krow:64 * krow + 128], rhs=q[:, 64 * r:64 * r + 128],
                                              start=True, stop=False), reads=[kb, qb], writes=[Sb])
                off = (pi * 7 + ci) * 128
                k.op(k.pe, lambda e: e.matmul(S[:, ci, :], lhsT=Ai[:, :], rhs=RM[:, off:off + 128], start=False, stop=True),
                     reads=[rmb], writes=[Sb])
            Ss, Ssb = sS.next()
            for ci in chunks:
                for qp in range(2):
                    dr = (r - 6 + 2 * ci) - (r + qp) + 7
                    k.op(k.dve, lambda e: e.tensor_tensor(out=Ss[:, ci, qp * 64:(qp + 1) * 64], in0=S[:, ci, qp * 64:(qp + 1) * 64],
                                                          in1=TT[:, h, dr, :], op=ALU.add), reads=[Sb, ttb], writes=[Ssb])
            c0, c1 = chunks[0], chunks[-1] + 1
            Pp, Pb = sP.next()
            k.op(k.act, lambda e: e.activation(out=Pp[:, c0:c1, :], in_=Ss[:, c0:c1, :], func=AF.Exp, scale=scale), reads=[Ssb], writes=[Pb])
            O, Ob = pO.next()
            Dn, Db = pD.next()
            for j, ci in enumerate(chunks):
                krow = r - 6 + 2 * ci
                k.op(k.pe, lambda e: e.matmul(O[:], lhsT=v[:, krow // 2, :], rhs=Pp[:, ci, :], start=(j == 0), stop=(j == len(chunks) - 1)),
                     reads=[vb, Pb], writes=[Ob])
            for j, ci in enumerate(chunks):
                k.op(k.pe, lambda e: e.matmul(Dn[:], lhsT=C["ones"][:], rhs=Pp[:, ci, :], start=(j == 0), stop=(j == len(chunks) - 1)),
                     reads=[C["onesb"], Pb], writes=[Db])
            rr, rrb = rD.next()
            k.op(k.dve, lambda e: e.reciprocal(out=rr[:], in_=Dn[:]), reads=[Db], writes=[rrb])
            k.op(k.dve, lambda e: e.tensor_tensor(out=o[:, 64 * r:64 * r + 128], in0=O[:], in1=rr[:], op=ALU.mult), reads=[Ob, rrb], writes=[ob])
        k.dma(k.pool, A["mixT"][h * 128:(h + 1) * 128, :], o[:], ob, False)
    ph.close()


def hy_host_consts(Tn, kind):
    L = Tn if kind == "p" else Tn // 2
    nseq = Tn // L
    pos = np.tile(np.arange(L), nseq)
    t = (pos / L).astype(np.float32)
    ang = 2.0 * math.pi * t[:, None].astype(np.float64) * np.arange(1, 17)
    z = np.concatenate([t[:, None], np.cos(ang), np.sin(ang)], -1).astype(np.float32)
    NT = Tn // 128
    lay = lambda a: np.ascontiguousarray(a.reshape(NT, 128).T)
    m0 = (pos != 0).astype(np.float32)
    ws = np.zeros(Tn, np.float32)
    ws[:L] = 1.0
    f = np.arange(L)
    w = math.pi * (2 * f[None, :] + 1) * np.arange(L)[:, None] / (2.0 * L)
    Cb, Sb = np.cos(w), np.sin(w)
    Cm = np.zeros((Tn, Tn), np.float32)
    Sm = np.zeros((Tn, Tn), np.float32)
    for s in range(nseq):
        Cm[s * L:(s + 1) * L, s * L:(s + 1) * L] = Cb
        Sm[s * L:(s + 1) * L, s * L:(s + 1) * L] = Sb
    bf = ml_dtypes.bfloat16
    Fm = np.concatenate([Cm, -Sm], 1).astype(bf)
    Wi = (np.concatenate([Cm.T, -Sm.T], 0) / L).astype(bf)
    return dict(hy_zT=np.ascontiguousarray(z.T), hy_ntn=lay(-t), hy_m0=lay(m0), hy_ws=lay(ws).astype(bf),
                hy_F=blk_host(Fm, pick_nb(2 * Tn)), hy_C=blk_host(Cm.astype(bf), pick_nb(Tn)),
                hy_S=blk_host(Sm.astype(bf), pick_nb(Tn)), hy_Wi=blk_host(Wi, pick_nb(Tn)))


def phase_hy_filter(k, C, A, Tn):
    HW = C["HW"]
    NT = Tn // 128
    ph = Phase(k)
    cb = ph.buf("hyc")
    ld = lambda t, src: k.dma(k.sync, t, src, cb, True)
    zT = ph.sb([33, Tn], F32, "hzT"); ld(zT[:], A["hy_zT"])
    w1 = ph.sb([33, 64], F32, "hw1"); ld(w1[:], A["hy_f_w1"])
    w2 = ph.sb([64, 64], F32, "hw2"); ld(w2[:], A["hy_f_w2"])
    w3 = ph.sb([64, 2 * HW], F32, "hw3"); ld(w3[:], A["hy_f_w3"])
    sc = ph.sb([64, 8], F32, "hsc")
    col = lambda a: a.rearrange("(p o) -> p o", o=1)
    ld(sc[:, 0:1], col(A["hy_f_freq"])); ld(sc[:, 1:2], col(A["hy_f_b1"])); ld(sc[:, 2:3], col(A["hy_f_b2"]))
    k.op(k.dve, lambda e: e.tensor_tensor(out=sc[:, 3:4], in0=sc[:, 0:1], in1=sc[:, 1:2], op=ALU.mult), reads=[cb], writes=[cb])
    k.op(k.dve, lambda e: e.tensor_tensor(out=sc[:, 4:5], in0=sc[:, 0:1], in1=sc[:, 2:3], op=ALU.mult), reads=[cb], writes=[cb])
    b3 = ph.sb([128, 2 * HW], F32, "hb3"); ld(b3[:], A["hy_f_b3"].partition_broadcast(128))
    eld = ph.sb([128, 2 * HW], F32, "held"); ld(eld[:], A["hy_log_decay"].rearrange("a b -> (a b)").partition_broadcast(128))
    k.op(k.act, lambda e: e.activation(out=eld[:], in_=eld[:], func=AF.Exp), reads=[cb], writes=[cb])
    ntn = ph.sb([128, NT], F32, "hntn"); ld(ntn[:], A["hy_ntn"])
    m0 = ph.sb([128, NT], F32, "hm0"); ld(m0[:], A["hy_m0"])
    ws = ph.sb([128, NT], BF16, "hws"); ld(ws[:], A["hy_ws"])
    h1 = ph.sb([64, Tn], F32, "hh1")
    h2 = ph.sb([64, Tn], F32, "hh2")
    hb = ph.buf("hh")
    pm = Ring(ph, 2, [64, 512], F32, "hpm", psum=True)
    tr = Ring(ph, 2, [64, 512], F32, "htr")
    MAGIC = 12582912.0
    TB = min(512, Tn)
    for (wt, Kd, src, dst, fbc) in ((w1, 33, zT, h1, 3), (w2, 64, h1, h2, 4)):
        for tb in range(Tn // TB):
            p, pb = pm.next()
            k.op(k.pe, lambda e: e.matmul(p[:, 0:TB], lhsT=wt[0:Kd, :], rhs=src[0:Kd, tb * TB:(tb + 1) * TB], start=True, stop=True),
                 reads=[cb, hb], writes=[pb])
            t1, t1b = tr.next()
            t2, t2b = tr.next()
            k.op(k.dve, lambda e: e.tensor_scalar(out=t1[:, 0:TB], in0=p[:, 0:TB], scalar1=sc[:, 0:1], scalar2=sc[:, fbc:fbc + 1],
                                                  op0=ALU.mult, op1=ALU.add), reads=[pb, cb], writes=[t1b])
            k.op(k.dve, lambda e: e.tensor_scalar(out=t2[:, 0:TB], in0=t1[:, 0:TB], scalar1=1.0 / (2 * math.pi), scalar2=MAGIC,
                                                  op0=ALU.mult, op1=ALU.add), reads=[t1b], writes=[t2b])
            k.op(k.dve, lambda e: e.tensor_scalar(out=t2[:, 0:TB], in0=t2[:, 0:TB], scalar1=-MAGIC, scalar2=-2 * math.pi,
                                                  op0=ALU.add, op1=ALU.mult), reads=[t2b], writes=[t2b])
            k.op(k.dve, lambda e: e.tensor_tensor(out=t1[:, 0:TB], in0=t1[:, 0:TB], in1=t2[:, 0:TB], op=ALU.add), reads=[t1b, t2b], writes=[t1b])
            k.op(k.dve, lambda e: e.tensor_scalar(out=t1[:, 0:TB], in0=t1[:, 0:TB], scalar1=math.pi, scalar2=-math.pi,
                                                  op0=ALU.min, op1=ALU.max), reads=[t1b], writes=[t1b])
            k.op(k.act, lambda e: e.activation(out=dst[:, tb * TB:(tb + 1) * TB], in_=t1[:, 0:TB], func=AF.Sin), reads=[t1b], writes=[hb])
    pf = Ring(ph, 4, [128, 512], F32, "hpf", psum=True)
    pnr = Ring(ph, 2, [128, 4], F32, "hpn", psum=True)
    nacc = ph.sb([128, HW // 128], F32, "hnacc")
    pnb = ph.buf("hnacc")
    k.op(k.dve, lambda e: e.memset(nacc[:], 0.0), writes=[pnb])
    fr = Ring(ph, 4, [128, 512], F32, "hfr")
    dr = Ring(ph, 2, [128, 512], F32, "hdr")
    ar = Ring(ph, 2, [128, 512], BF16, "har")
    orr = Ring(ph, 4, [128, 512], BF16, "hor")
    CB = min(512, HW)
    for n in range(NT):
        for cbk in range(HW // CB):
            fd = []
            for d in range(2):
                c0 = d * HW + cbk * CB
                p, pb = pf.next()
                k.op(k.pe, lambda e: e.matmul(p[:, 0:CB], lhsT=h2[:, n * 128:(n + 1) * 128], rhs=w3[:, c0:c0 + CB], start=True, stop=True),
                     reads=[hb, cb], writes=[pb])
                dc, dcb = dr.next()
                k.op(k.act, lambda e: e.activation(out=dc[:, 0:CB], in_=eld[:, c0:c0 + CB], func=AF.Exp, scale=ntn[:, n:n + 1]),
                     reads=[cb], writes=[dcb])
                f, fb = fr.next()
                k.op(k.dve, lambda e: e.tensor_tensor(out=f[:, 0:CB], in0=p[:, 0:CB], in1=b3[:, c0:c0 + CB], op=ALU.add), reads=[pb, cb], writes=[fb])
                k.op(k.dve, lambda e: e.tensor_tensor(out=f[:, 0:CB], in0=f[:, 0:CB], in1=dc[:, 0:CB], op=ALU.mult), reads=[fb, dcb], writes=[fb])
                fd.append((f, fb))
            (ff, ffb), (fw, fwb) = fd
            a1, a1b = dr.next()
            k.op(k.act, lambda e: e.activation(out=a1[:, 0:CB], in_=ff[:, 0:CB], func=AF.Abs), reads=[ffb], writes=[a1b])
            a2, a2b = dr.next()
            k.op(k.act, lambda e: e.activation(out=a2[:, 0:CB], in_=fw[:, 0:CB], func=AF.Abs), reads=[fwb], writes=[a2b])
            ab_, abb = ar.next()
            k.op(k.dve, lambda e: e.tensor_tensor(out=ab_[:, 0:CB], in0=a1[:, 0:CB], in1=a2[:, 0:CB], op=ALU.add), reads=[a1b, a2b], writes=[abb])
            pq, pqb = pnr.next()
            nj = CB // 128
            for j in range(nj):
                k.op(k.pe, lambda e: e.matmul(pq[:, j:j + 1], lhsT=ab_[:, j * 128:(j + 1) * 128], rhs=ws[:, n:n + 1],
                                              start=True, stop=True), reads=[abb, cb], writes=[pqb])
            k.op(k.dve, lambda e: e.tensor_tensor(out=nacc[:, cbk * nj:(cbk + 1) * nj], in0=nacc[:, cbk * nj:(cbk + 1) * nj],
                                                  in1=pq[:, 0:nj], op=ALU.add), reads=[pqb, pnb], writes=[pnb])
            o1, o1b = orr.next()
            k.op(k.dve, lambda e: e.scalar_tensor_tensor(out=o1[:, 0:CB], in0=fw[:, 0:CB], scalar=m0[:, n:n + 1], in1=ff[:, 0:CB],
                                                         op0=ALU.mult, op1=ALU.add), reads=[ffb, fwb, cb], writes=[o1b])
            o2, o2b = orr.next()
            k.op(k.dve, lambda e: e.scalar_tensor_tensor(out=o2[:, 0:CB], in0=fw[:, 0:CB], scalar=m0[:, n:n + 1], in1=ff[:, 0:CB],
                                                         op0=ALU.mult, op1=ALU.subtract), reads=[ffb, fwb, cb], writes=[o2b])
            k.dma(k.pool, A["hy_hs"][n * 128:(n + 1) * 128, cbk * CB:(cbk + 1) * CB], o1[:, 0:CB], o1b, False)
            k.dma(k.pool, A["hy_hd"][n * 128:(n + 1) * 128, cbk * CB:(cbk + 1) * CB], o2[:, 0:CB], o2b, False)
    rn = ph.sb([128, HW // 128], F32, "hrn")
    rnb = ph.buf("hrn")
    k.op(k.dve, lambda e: e.reciprocal(out=rn[:], in_=nacc[:]), reads=[pnb], writes=[rnb])
    k.dma(k.pool, A["hy_rn"], rn[:], rnb, False)
    ph.close()


def phase_hy_conv(k, C, A, Tn):
    HW = C["HW"]
    NT = Tn // 128
    Th = Tn // 2
    NC = HW // 128
    ph = Phase(k)
    cb = ph.buf("hcc")
    cw = ph.sb([128, 3 * NC, 3], F32, "hcw")
    k.dma(k.sync, cw[:], A["hy_cwT"].rearrange("p (c j) -> p c j", j=3), cb, True)
    cbias = ph.sb([128, 3 * NC], F32, "hcb")
    k.dma(k.sync, cbias[:], A["hy_cbT"], cb, True)
    flag = ph.sb([128, 1], F32, "hfl")
    k.dma(k.sync, flag[:], A["flag"], cb, True)
    cwf = ph.sb([128, 3 * NC, 3], F32, "hcwf")
    k.op(k.dve, lambda e: e.tensor_scalar_mul(out=cwf[:], in0=cw[:], scalar1=flag[:, 0:1]), reads=[cb], writes=[cb])
    zr = Ring(ph, 4, [128, Tn], BF16, "hz")
    zc = Ring(ph, 4, [128, Tn], F32, "hzc")
    ur = Ring(ph, 2, [128, Tn], BF16, "hu")
    xr = Ring(ph, 2, [128, Tn], BF16, "hx0")
    pt = Ring(ph, 2, [128, 4, 128], BF16, "hpt", psum=True)
    us = Ring(ph, 2, [128, NT, 128], BF16, "hus")
    for c in range(NC):
        outs = []
        for part in range(3):
            cc = part * NC + c
            z, zb = zr.next()
            k.dma(k.sync, z[:], A["zT"][cc * 128:(cc + 1) * 128, :], zb, True)
            o, ob = zc.next()
            w0, w1, w2 = cw[:, cc, 0:1], cw[:, cc, 1:2], cw[:, cc, 2:3]
            k.op(k.dve, lambda e: e.tensor_scalar(out=o[:], in0=z[:], scalar1=w1, scalar2=cbias[:, cc:cc + 1], op0=ALU.mult, op1=ALU.add),
                 reads=[zb, cb], writes=[ob])
            for a in (0, Th):
                k.op(k.dve, lambda e: e.scalar_tensor_tensor(out=o[:, a + 1:a + Th], in0=z[:, a:a + Th - 1], scalar=w0, in1=o[:, a + 1:a + Th],
                                                             op0=ALU.mult, op1=ALU.add), reads=[zb, cb, ob], writes=[ob])
                k.op(k.dve, lambda e: e.scalar_tensor_tensor(out=o[:, a:a + Th - 1], in0=z[:, a + 1:a + Th], scalar=w2, in1=o[:, a:a + Th - 1],
                                                             op0=ALU.mult, op1=ALU.add), reads=[zb, cb, ob], writes=[ob])
            k.op(k.dve, lambda e: e.scalar_tensor_tensor(out=o[:, Th:Th + 1], in0=z[:, Th - 1:Th], scalar=cwf[:, cc, 0:1], in1=o[:, Th:Th + 1],
                                                         op0=ALU.mult, op1=ALU.add), reads=[zb, cb, ob], writes=[ob])
            k.op(k.dve, lambda e: e.scalar_tensor_tensor(out=o[:, Th - 1:Th], in0=z[:, Th:Th + 1], scalar=cwf[:, cc, 2:3], in1=o[:, Th - 1:Th],
                                                         op0=ALU.mult, op1=ALU.add), reads=[zb, cb, ob], writes=[ob])
            outs.append((o, ob))
        (x0, x0b), (x1, x1b), (vv, vvb) = outs
        xo, xob = xr.next()
        k.op(k.act, lambda e: e.activation(out=xo[:], in_=x0[:], func=AF.Copy), reads=[x0b], writes=[xob])
        k.dma(k.pool, A["hy_x0T"][c * 128:(c + 1) * 128, :], xo[:], xob, False)
        u, ub = ur.next()
        k.op(k.dve, lambda e: e.tensor_tensor(out=u[:], in0=vv[:], in1=x1[:], op=ALU.mult), reads=[vvb, x1b], writes=[ub])
        k.dma(k.pool, A["hy_uT"][c * 128:(c + 1) * 128, :], u[:], ub, False)
        s, sb_ = us.next()
        for n4 in range(0, NT, 4):
            p, pb = pt.next()
            for j in range(min(4, NT - n4)):
                n = n4 + j
                k.op(k.pe, lambda e: e.transpose(out=p[:, j, :], in_=u[:, n * 128:(n + 1) * 128], identity=C["ident"][:]),
                     reads=[ub, C["identb"]], writes=[pb])
            k.op(k.act, lambda e: e.activation(out=s[:, n4:n4 + 4, :], in_=p[:], func=AF.Copy), reads=[pb], writes=[sb_])
        k.dma(k.pool, A["hy_u"].rearrange("(n p) c -> p n c", p=128)[:, :, c * 128:(c + 1) * 128], s[:], sb_, False)
    ph.close()


def phase_hy_mul(k, C, A, Tn):
    HW = C["HW"]
    ph = Phase(k)
    CB = min(512, HW)
    ir = Ring(ph, 8, [128, 512], BF16, "hmi")
    tr = Ring(ph, 4, [128, 512], F32, "hmt")
    orr = Ring(ph, 4, [128, 512], BF16, "hmo")
    for fi in range(Tn // 128):
        for cbk in range(HW // CB):
            cs = slice(cbk * CB, (cbk + 1) * CB)
            t = []
            for (src, r0) in ((A["hy_Uf"], fi * 128), (A["hy_Uf"], Tn + fi * 128), (A["hy_Tf"], fi * 128), (A["hy_Tf"], Tn + fi * 128)):
                x, xb = ir.next()
                k.dma(k.sync, x[:, 0:CB], src[r0:r0 + 128, cs], xb, True)
                t.append((x, xb))
            (ur_, urb), (ui, uib), (tr_, trb), (ti, tib) = t
            a, ab_ = tr.next()
            b, bb_ = tr.next()
            k.op(k.dve, lambda e: e.tensor_tensor(out=a[:, 0:CB], in0=ur_[:, 0:CB], in1=tr_[:, 0:CB], op=ALU.mult), reads=[urb, trb], writes=[ab_])
            k.op(k.dve, lambda e: e.tensor_tensor(out=b[:, 0:CB], in0=ui[:, 0:CB], in1=ti[:, 0:CB], op=ALU.mult), reads=[uib, tib], writes=[bb_])
            o1, o1b = orr.next()
            k.op(k.dve, lambda e: e.tensor_tensor(out=o1[:, 0:CB], in0=a[:, 0:CB], in1=b[:, 0:CB], op=ALU.subtract), reads=[ab_, bb_], writes=[o1b])
            a2, a2b = tr.next()
            b2, b2b = tr.next()
            k.op(k.dve, lambda e: e.tensor_tensor(out=a2[:, 0:CB], in0=ur_[:, 0:CB], in1=ti[:, 0:CB], op=ALU.mult), reads=[urb, tib], writes=[a2b])
            k.op(k.dve, lambda e: e.tensor_tensor(out=b2[:, 0:CB], in0=ui[:, 0:CB], in1=tr_[:, 0:CB], op=ALU.mult), reads=[uib, trb], writes=[b2b])
            o2, o2b = orr.next()
            k.op(k.dve, lambda e: e.tensor_tensor(out=o2[:, 0:CB], in0=a2[:, 0:CB], in1=b2[:, 0:CB], op=ALU.add), reads=[a2b, b2b], writes=[o2b])
            k.dma(k.pool, A["hy_Yf"][fi * 128:(fi + 1) * 128, cs], o1[:, 0:CB], o1b, False)
            k.dma(k.pool, A["hy_Yf"][Tn + fi * 128:Tn + (fi + 1) * 128, cs], o2[:, 0:CB], o2b, False)
    ph.close()


def epi_hy_final(k, C, A):
    HW, NAW = C["HW"], C["NAW"]

    def setup(ph):
        b = ph.buf("hfs")
        rn = ph.sb([128, HW // 128], F32, "hfrn")
        k.dma(k.sync, rn[:], A["hy_rn"], b, True)
        bi = ph.sb([128, HW // 128], F32, "hfbi")
        k.dma(k.sync, bi[:], A["hy_biasT"], b, True)
        return dict(b=b, rn=rn, bi=bi, u=Ring(ph, 3, [128, 512], BF16, "hfu"), x=Ring(ph, 3, [128, 512], BF16, "hfx"),
                    y=Ring(ph, 2, [128, 512], F32, "hfy"), o=Ring(ph, 3, [128, 512], BF16, "hfo"))

    def epi(ph, st, pts, n0, nsz, t0, tsz):
        p, pb = pts[0]
        c = t0 // 128
        u, ub = st["u"].next()
        k.dma(k.pool, u[:, 0:nsz], A["hy_uT"][t0:t0 + 128, n0:n0 + nsz], ub, True)
        x, xb = st["x"].next()
        k.dma(k.pool, x[:, 0:nsz], A["hy_x0T"][t0:t0 + 128, n0:n0 + nsz], xb, True)
        y, yb = st["y"].next()
        k.op(k.dve, lambda e: e.tensor_scalar_mul(out=y[:, 0:nsz], in0=p[:, 0:nsz], scalar1=st["rn"][:, c:c + 1]), reads=[pb, st["b"]], writes=[yb])
        k.op(k.dve, lambda e: e.scalar_tensor_tensor(out=y[:, 0:nsz], in0=u[:, 0:nsz], scalar=st["bi"][:, c:c + 1], in1=y[:, 0:nsz],
                                                     op0=ALU.mult, op1=ALU.add), reads=[ub, yb, st["b"]], writes=[yb])
        o, ob = st["o"].next()
        k.op(k.dve, lambda e: e.tensor_tensor(out=o[:, 0:nsz], in0=y[:, 0:nsz], in1=x[:, 0:nsz], op=ALU.mult), reads=[yb, xb], writes=[ob])
        k.dma(k.pool, A["mixT"][NAW + t0:NAW + t0 + 128, n0:n0 + nsz], o[:, 0:nsz], ob, False)
    return setup, epi


def hyena_all(k, C, A, Tn):
    HW = C["HW"]
    phase_hy_filter(k, C, A, Tn)
    phase_hy_conv(k, C, A, Tn)
    s, e = epi_store_fm(k, A["hy_Uf"])
    gemm(k, C, A["hy_u"], [A["hy_F"]], HW, "FM", e, epi_setup=s)
    s, e = epi_store_fm(k, A["hy_Tf"], row0=0)
    gemm(k, C, A["hy_hs"], [A["hy_C"]], HW, "FM", e, epi_setup=s)
    s, e = epi_store_fm(k, A["hy_Tf"], row0=Tn)
    gemm(k, C, A["hy_hd"], [A["hy_S"]], HW, "FM", e, epi_setup=s)
    phase_hy_mul(k, C, A, Tn)
    s, e = epi_hy_final(k, C, A)
    gemm(k, C, A["hy_Yf"], [A["hy_Wi"]], HW, "TM", e, epi_setup=s, TG=512, KBmax=32)


SCR_KIND = "Internal"
PHASES = None
DBG = {}
CFG = dict(D=4096, F=11008, XH=4, XW=512, NM=256, NAH=16, NAW=2048, HW=2048, G=256, T=4096)


def build_program(cfg, in_shapes):
    nc = bass.Bass("TRN2", target_bir_lowering=False)
    k = KB(nc)
    C = dict(cfg)
    D, F, XW, NM, NAW, HW, T = C["D"], C["F"], C["XW"], C["NM"], C["NAW"], C["HW"], C["T"]
    A = {}
    for name, (shape, dt) in in_shapes.items():
        A[name] = nc.dram_tensor(name, list(shape), BF16 if dt == "bf16" else F32, kind="ExternalInput").ap()
    A["y"] = nc.dram_tensor("y", [T, D], F32, kind="ExternalOutput").ap()

    def scr(name, shape, dt=BF16):
        A[name] = nc.dram_tensor("scr_" + name, list(shape), dt, kind=SCR_KIND).ap()

    scr("xnT", [D, T]); scr("mnT", [D, 2 * NM]); scr("cqT", [XW, T]); scr("ckT", [XW, 2 * NM]); scr("cv", [2 * NM, XW])
    scr("coT", [XW, T]); scr("hT", [F, T]); scr("qT", [NAW, T]); scr("kT", [NAW, T]); scr("v", [T, NAW]); scr("zT", [3 * HW, T])
    scr("mixT", [NAW + HW, T]); scr("s5T", [D, T])
    scr("hy_hs", [T, HW]); scr("hy_hd", [T, HW]); scr("hy_rn", [128, HW // 128], F32); scr("hy_x0T", [HW, T]); scr("hy_uT", [HW, T])
    scr("hy_u", [T, HW]); scr("hy_Uf", [2 * T, HW]); scr("hy_Tf", [2 * T, HW]); scr("hy_Yf", [2 * T, HW])
    setup_consts(k, C, A["ident"])
    for n in ["g_mix", "g_cross", "g_mem", "g_ffn", "x_wq", "x_wk", "x_wv", "x_wo", "x_q_gain", "x_k_gain", "w_ffn_gate", "w_ffn_up", "w_ffn_down"]:
        A[n] = [A[n][0], A[n][1]]
    w_in = A["w_in"]
    W = {}
    items = []

    def wb(name, src, dual=False):
        Kd, N = src.shape
        NB = pick_nb(N, dual)
        W[name] = nc.dram_tensor("wb_" + name, [N // NB, 128, Kd // 128, NB], BF16, kind="Internal").ap()
        items.append((src, W[name]))

    wb("q", w_in[:, 0:NAW]); wb("k", w_in[:, NAW:2 * NAW]); wb("v", w_in[:, 2 * NAW:3 * NAW]); wb("z", w_in[:, 3 * NAW:3 * NAW + 3 * HW])
    wb("w_out", A["w_out"])
    for l in range(2):
        wb("x_wq%d" % l, A["x_wq"][l]); wb("x_wk%d" % l, A["x_wk"][l]); wb("x_wv%d" % l, A["x_wv"][l]); wb("x_wo%d" % l, A["x_wo"][l])
        wb("gate%d" % l, A["w_ffn_gate"][l], True); wb("up%d" % l, A["w_ffn_up"][l], True); wb("down%d" % l, A["w_ffn_down"][l])
    wb("glu_v", A["w_glu"][:, 0:D], True); wb("glu_g", A["w_glu"][:, D:2 * D], True)
    A["Wb"] = W
    on = lambda n: (PHASES is None) or (n in PHASES)
    if on("cast"):
        phase_cast_weights(k, items)
    if on("l0proj"):
        phase_normT(k, C, A["x"], A["g_mix"][0], A["xnT"], T)
        s, e = epi_headnorm_fm(k, C, A["qT"], A["na_q_gain"])
        gemm(k, C, A["xnT"], [W["q"]], T, "FM", e, epi_setup=s)
        s, e = epi_headnorm_fm(k, C, A["kT"], A["na_k_gain"])
        gemm(k, C, A["xnT"], [W["k"]], T, "FM", e, epi_setup=s)
        s, e = epi_store_tm(k, A["v"])
        gemm(k, C, A["xnT"], [W["v"]], T, "TM", e, epi_setup=s)
        s, e = epi_store_fm(k, A["zT"])
        gemm(k, C, A["xnT"], [W["z"]], T, "FM", e, epi_setup=s)
    if on("na"):
        phase_na(k, C, A, T)
    if on("hy"):
        hyena_all(k, C, A, T)
    if on("wout"):
        s, e = epi_resid_tm(k, A["x"], A["y"])
        gemm(k, C, A["mixT"], [W["w_out"]], T, "TM", e, epi_setup=s)
    if on("tail0"):
        dense_tail(k, C, A, 0, T)
    if on("s5"):
        phase_normT(k, C, A["y"], A["g_mix"][1], A["xnT"], T)
        phase_s5(k, C, A, T)
    if on("glu"):
        s, e = epi_resid_tm(k, A["y"], A["y"], glu=True)
        gemm(k, C, A["s5T"], [W["glu_v"], W["glu_g"]], T, "TM", e, epi_setup=s)
    if on("tail1"):
        dense_tail(k, C, A, 1, T)
    k.barrier()
    return nc


def host_inputs(cfg, inputs, x_core, mem_core, kind):
    bf = ml_dtypes.bfloat16
    T, HW, G = cfg["T"], cfg["HW"], cfg["G"]
    f = lambda n: np.ascontiguousarray(np.asarray(inputs[n]))
    im = {"x": x_core, "mem": mem_core, "ident": np.eye(128).astype(bf),
          "flag": np.full((128, 1), 1.0 if kind == "p" else 0.0, np.float32)}
    for n in ["g_mix", "g_cross", "g_mem", "g_ffn", "x_wq", "x_wk", "x_wv", "x_wo", "x_q_gain", "x_k_gain",
              "w_ffn_gate", "w_ffn_up", "w_ffn_down"]:
        im[n] = f(n)
    for n in ["w_in", "na_q_gain", "na_k_gain", "w_out", "w_glu", "hy_f_w1", "hy_f_w2", "hy_f_w3", "hy_f_freq", "hy_f_b1", "hy_f_b2",
              "hy_f_b3", "hy_log_decay"]:
        im[n] = f(n)[0]
    cw = f("hy_conv_w")[0]
    im["hy_cwT"] = np.ascontiguousarray(cw.reshape(3, 3 * HW // 128, 128).transpose(2, 1, 0).reshape(128, -1))
    im["hy_cbT"] = np.ascontiguousarray(f("hy_conv_b")[0].reshape(-1, 128).T)
    im["hy_biasT"] = np.ascontiguousarray(f("hy_bias")[0].reshape(-1, 128).T)
    im.update(hy_host_consts(T, kind))
    im.update(na_host_tables(f("na_rpb")[0], T // 64, kind))
    im.update(s5_host_layout(inputs, G))
    return im


_CACHE = {}


def run_cfg(cfg, inputs, cores):
    base = {}
    ims = []
    for (x, m, kd) in cores:
        if kd not in base:
            base[kd] = host_inputs(cfg, inputs, x, m, kd)
        im = dict(base[kd])
        im["x"] = x
        im["mem"] = m
        ims.append(im)
    shapes = {n: (a.shape, "bf16" if a.dtype == ml_dtypes.bfloat16 else "f32") for n, a in ims[0].items()}
    nc = build_program(cfg, shapes)
    res = run_bass_kernel_spmd(nc, ims, core_ids=list(range(len(ims))))
    DBG["res"] = res.results
    return [r["y"] for r in res.results]


def kernel(**inputs):
    cfg = CFG
    T, D, NM = cfg["T"], cfg["D"], cfg["NM"]
    xp = np.asarray(inputs["x_prompt"])
    xs = np.asarray(inputs["x_sample"])
    mp = np.asarray(inputs["mem_prompt"])
    ms = np.asarray(inputs["mem_sample"])
    cores = []
    for b in range(2):
        cores.append((np.ascontiguousarray(xp[b]), np.ascontiguousarray(np.concatenate([mp[b], mp[b]], 0)), "p"))
    for j in range(4):
        cores.append((np.ascontiguousarray(xs[2 * j:2 * j + 2].reshape(T, D)), np.ascontiguousarray(ms[2 * j:2 * j + 2].reshape(2 * NM, D)), "s"))
    cores.append(cores[2])
    cores.append(cores[3])
    ys = run_cfg(cfg, inputs, cores)
    y_prompt = np.stack([ys[0], ys[1]], 0).astype(np.float32)
    y_sample = np.concatenate([ys[2 + j].reshape(2, T // 2, D) for j in range(4)], 0).astype(np.float32)
    return (y_prompt, y_sample)
```

```python
import math
from contextlib import ExitStack

import numpy as np
import ml_dtypes
import concourse.bass as bass
import concourse.mybir as mybir
from concourse.bass_utils import run_bass_kernel_spmd

F32 = mybir.dt.float32
BF16 = mybir.dt.bfloat16
AF = mybir.ActivationFunctionType
ALU = mybir.AluOpType
AX = mybir.AxisListType
EPS = 1e-6


class Tok:
    __slots__ = ("sem", "val")

    def __init__(self, sem, val):
        self.sem = sem
        self.val = val


class Buf:
    def __init__(self, name=""):
        self.name = name
        self.lw = None
        self.rd = {}
        self.ds = None
        self.ds2 = None


class Eng:
    def __init__(self, nc, name, e):
        self.name = name
        self.e = e
        self.sem = nc.alloc_semaphore("sem_" + name)
        self.cnt = 0
        self.seen = {}


class DSem:
    def __init__(self, nc, i):
        self.sem = nc.alloc_semaphore("dsem%d" % i)
        self.cnt = 0


class KB:
    def __init__(self, nc):
        self.nc = nc
        self.pe = Eng(nc, "pe", nc.tensor)
        self.act = Eng(nc, "act", nc.scalar)
        self.dve = Eng(nc, "dve", nc.vector)
        self.pool = Eng(nc, "pool", nc.gpsimd)
        self.sync = Eng(nc, "sync", nc.sync)
        self.engs = [self.pe, self.act, self.dve, self.pool, self.sync]
        self.dfree = []
        self.dfree2 = []
        self.dall = []
        self.pending = {}
        self.nds = 0

    def get_ds(self, store=False):
        fl = self.dfree2 if store else self.dfree
        if fl:
            return fl.pop()
        d = DSem(self.nc, self.nds)
        self.nds += 1
        self.dall.append(d)
        return d

    def _wait(self, E, tok):
        if tok is None:
            return
        if E is self.pe and tok.sem is E.sem:
            return
        key = id(tok.sem)
        if E.seen.get(key, 0) >= tok.val:
            return
        E.e.wait_ge(tok.sem, tok.val)
        E.seen[key] = tok.val

    def _deps(self, E, reads, writes):
        for b in reads:
            self._wait(E, b.lw)
        for b in writes:
            self._wait(E, b.lw)
            for t in b.rd.values():
                self._wait(E, t)

    def _commit(self, tok, reads, writes):
        for b in reads:
            key = id(tok.sem)
            o = b.rd.get(key)
            if o is None or o.val < tok.val:
                b.rd[key] = tok
        for b in writes:
            b.lw = tok
            b.rd = {}

    def op(self, E, fn, reads=(), writes=()):
        self._deps(E, reads, writes)
        ins = fn(E.e)
        E.cnt += 1
        ins.then_inc(E.sem, 1)
        tok = Tok(E.sem, E.cnt)
        self._commit(tok, reads, writes)
        return tok

    def op_w(self, E, fn, waits):
        for t in waits:
            self._wait(E, t)
        ins = fn(E.e)
        E.cnt += 1
        ins.then_inc(E.sem, 1)
        return Tok(E.sem, E.cnt)

    def dma(self, Q, out, in_, sb, load, extra_reads=(), extra_writes=(), **kw):
        reads = list(extra_reads) + ([] if load else [sb])
        writes = list(extra_writes) + ([sb] if load else [])
        self._deps(Q, reads, writes)
        if load:
            if sb.ds is None:
                sb.ds = self.get_ds()
            ds = sb.ds
        else:
            if sb.ds2 is None:
                sb.ds2 = self.get_ds(True)
            ds = sb.ds2
        ins = Q.e.dma_start(out=out, in_=in_, **kw)
        ds.cnt += 16
        ins.then_inc(ds.sem, 16)
        tok = Tok(ds.sem, ds.cnt)
        self.pending[id(ds.sem)] = tok
        self._commit(tok, reads, writes)
        return tok

    def release(self, bufs):
        for b in bufs:
            if b.ds is not None:
                self.dfree.append(b.ds)
                b.ds = None
            if b.ds2 is not None:
                self.dfree2.append(b.ds2)
                b.ds2 = None

    def barrier(self):
        toks = [Tok(E.sem, E.cnt) for E in self.engs if E.cnt > 0]
        toks += list(self.pending.values())
        for E in self.engs:
            for t in toks:
                if t.sem is not E.sem:
                    self._wait(E, t)
        self.pending = {}


class Phase:
    uid = 0

    def __init__(self, k):
        self.k = k
        self.es = ExitStack()
        self.bufs = []
        self.n = 0

    def sb(self, shape, dt, name=None):
        Phase.uid += 1
        t = self.es.enter_context(self.k.nc.sbuf_tensor("%s_%d" % (name or "sb", Phase.uid), list(shape), dt))
        return t

    def ps(self, shape, dt, name=None):
        Phase.uid += 1
        t = self.es.enter_context(self.k.nc.psum_tensor("%s_%d" % (name or "ps", Phase.uid), list(shape), dt))
        return t

    def buf(self, name=""):
        b = Buf(name)
        self.bufs.append(b)
        return b

    def close(self):
        self.k.barrier()
        self.k.release(self.bufs)
        self.es.close()


class Ring:
    def __init__(self, ph, n, shape, dt, name, psum=False):
        self.t = [(ph.ps if psum else ph.sb)(shape, dt, name) for _ in range(n)]
        self.b = [ph.buf(name + str(i)) for i in range(n)]
        self.i = 0
        self.n = n

    def next(self):
        j = self.i % self.n
        self.i += 1
        return self.t[j], self.b[j]


def phase_normT(k, C, src, g_ap, dstT, Tn):
    D = C["D"]
    DC = D // 128
    ph = Phase(k)
    gbc = ph.sb([128, D], F32, "gbc")
    gb = ph.buf("gbc")
    k.dma(k.sync, gbc[:], g_ap.partition_broadcast(128), gb, True)
    xr = Ring(ph, 2, [128, D], F32, "xt")
    xn = Ring(ph, 2, [128, D], BF16, "xn")
    junk = ph.sb([128, D], BF16, "junk")
    jb = ph.buf("junk")
    ssr = Ring(ph, 4, [128, 2], F32, "ss")
    st = Ring(ph, 2, [128, DC, 512], BF16, "stage")
    pt = Ring(ph, 4, [128, 8, 128], BF16, "pst", psum=True)
    TQ = min(512, Tn)
    nj = TQ // 128
    for tq in range(Tn // TQ):
        stg, stb = st.next()
        for j in range(nj):
            t0 = tq * TQ + j * 128
            xt, xb = xr.next()
            k.dma(k.sync, xt[:], src[t0:t0 + 128, :], xb, True)
            ss, sb_ = ssr.next()
            k.op(k.act, lambda e: e.activation(out=junk[:], in_=xt[:], func=AF.Square, accum_out=ss[:, 0:1]),
                 reads=[xb], writes=[jb, sb_])
            k.op(k.act, lambda e: e.activation(out=ss[:, 1:2], in_=ss[:, 0:1], func=AF.Sqrt, scale=1.0 / D, bias=C["eps"][:, 0:1]),
                 reads=[sb_, C["epsb"]], writes=[sb_])
            k.op(k.dve, lambda e: e.reciprocal(out=ss[:, 1:2], in_=ss[:, 1:2]), reads=[sb_], writes=[sb_])
            xnt, xnb = xn.next()
            k.op(k.dve, lambda e: e.scalar_tensor_tensor(out=xnt[:], in0=xt[:], scalar=ss[:, 1:2], in1=gbc[:],
                                                         op0=ALU.mult, op1=ALU.mult),
                 reads=[xb, sb_, gb], writes=[xnb])
            G8 = min(8, DC)
            for c8 in range(DC // G8):
                p, pb = pt.next()
                for c in range(G8):
                    cc = c8 * G8 + c
                    k.op(k.pe, lambda e: e.transpose(out=p[:, c, :], in_=xnt[:, cc * 128:(cc + 1) * 128],
                                                     identity=C["ident"][:]),
                         reads=[xnb, C["identb"]], writes=[pb])
                E = k.act if (c8 % 2 == 0) else k.dve
                if E is k.act:
                    k.op(E, lambda e: e.activation(out=stg[:, c8 * G8:(c8 + 1) * G8, j * 128:(j + 1) * 128],
                                                   in_=p[:, 0:G8, :], func=AF.Copy), reads=[pb], writes=[stb])
                else:
                    k.op(E, lambda e: e.tensor_copy(out=stg[:, c8 * G8:(c8 + 1) * G8, j * 128:(j + 1) * 128],
                                                    in_=p[:, 0:G8, :]), reads=[pb], writes=[stb])
        k.dma(k.pool, dstT.rearrange("(c p) t -> p c t", p=128)[:, :, tq * TQ:(tq + 1) * TQ], stg[:, :, 0:TQ],
              stb, False)
    ph.close()


def pick_nb(N, dual=False):
    nb = 256 if dual else 512
    nb = min(nb, N)
    while N % nb != 0:
        nb -= 128
    return nb


def blk_host(W, NB):
    Kd, N = W.shape
    return np.ascontiguousarray(W.reshape(Kd // 128, 128, N // NB, NB).transpose(2, 1, 0, 3))


def phase_cast_weights(k, items):
    ph = Phase(k)
    sr = Ring(ph, 3, [128, 2048], F32, "cws")
    br = Ring(ph, 3, [128, 2048], BF16, "cwb")
    ci = 0
    for (src, dst) in items:
        NBLK, _, KC, NB = dst.shape
        N = NBLK * NB
        CW = (2048 // NB) * NB
        for kc in range(KC):
            for c0 in range(0, N, CW):
                w = min(CW, N - c0)
                st_, sb_ = sr.next()
                k.dma(k.sync, st_[:, 0:w], src[kc * 128:(kc + 1) * 128, c0:c0 + w], sb_, True)
                bt, bb = br.next()
                if ci % 3 == 2:
                    k.op(k.act, lambda e: e.activation(out=bt[:, 0:w], in_=st_[:, 0:w], func=AF.Copy), reads=[sb_], writes=[bb])
                else:
                    k.op(k.dve, lambda e: e.tensor_copy(out=bt[:, 0:w], in_=st_[:, 0:w]), reads=[sb_], writes=[bb])
                ci += 1
                k.dma(k.pool, dst[c0 // NB:(c0 + w) // NB, :, kc, :].rearrange("b p n -> p b n"),
                      bt[:, 0:w].rearrange("p (b n) -> p b n", n=NB), bb, False)
    ph.close()


def gemm(k, C, actT, ws, Tn, mode, epi, TG=1024, KBmax=32, epi_setup=None):
    Kd = actT.shape[0]
    NBLK, _, KC, NB = ws[0].shape
    assert Kd == KC * 128
    nw = len(ws)
    TG = min(TG, Tn)
    nkb = (KC + KBmax - 1) // KBmax
    KBs = [KC // nkb + (1 if i < KC % nkb else 0) for i in range(nkb)]
    kb0 = [sum(KBs[:i]) for i in range(nkb)]
    KBm = max(KBs)
    ph = Phase(k)
    act = Ring(ph, 1 if KC * TG * 2 > 40000 else 2, [128, KC, TG], BF16, "act")
    wb = Ring(ph, 2 if nw * KBm * NB * 2 > 40000 else 3, [128, nw, KBm, NB], BF16, "wb")
    psr = [Ring(ph, 4 if nw == 2 else 6, [128, 512], F32, "ps%d" % i, psum=True) for i in range(nw)]
    est = epi_setup(ph) if epi_setup else None
    for tg in range(Tn // TG):
        at, ab = act.next()
        for kc in range(KC):
            k.dma(k.sync, at[:, kc, :], actT[kc * 128:(kc + 1) * 128, tg * TG:(tg + 1) * TG], ab, True)
        for nb in range(NBLK):
            if mode == "FM":
                tiles = [(m, tb) for tb in range(TG // 512 if TG >= 512 else 1) for m in range(NB // 128)]
            else:
                tiles = [(0, tt) for tt in range(TG // 128)]
            maxt = psr[0].n
            if nkb > 1:
                assert len(tiles) <= maxt
            for s0 in range(0, len(tiles), maxt):
                sub = tiles[s0:s0 + maxt]
                pts = [[psr[i].next() for i in range(nw)] for _ in sub]
                for kbi in range(nkb):
                    if s0 == 0 or nkb > 1:
                        wt, wbuf = wb.next()
                        for wi in range(nw):
                            for q0 in range(0, KBs[kbi], 8):
                                q1 = min(q0 + 8, KBs[kbi])
                                k.dma(k.sync, wt[:, wi, q0:q1, :], ws[wi][nb, :, kb0[kbi] + q0:kb0[kbi] + q1, :], wbuf, True)
                    for ti, (m, tb) in enumerate(sub):
                        for wi in range(nw):
                            p, pb = pts[ti][wi]
                            for kk in range(KBs[kbi]):
                                kc = kb0[kbi] + kk
                                first = (kbi == 0 and kk == 0)
                                last = (kbi == nkb - 1 and kk == KBs[kbi] - 1)
                                if mode == "FM":
                                    tsz = min(512, TG)
                                    k.op(k.pe, lambda e: e.matmul(p[:, 0:tsz], lhsT=wt[:, wi, kk, m * 128:(m + 1) * 128],
                                                                  rhs=at[:, kc, tb * 512:tb * 512 + tsz],
                                                                  start=first, stop=last),
                                         reads=[wbuf, ab], writes=[pb])
                                else:
                                    k.op(k.pe, lambda e: e.matmul(p[:, 0:NB], lhsT=at[:, kc, tb * 128:(tb + 1) * 128],
                                                                  rhs=wt[:, wi, kk, :], start=first, stop=last),
                                         reads=[wbuf, ab], writes=[pb])
                for ti, (m, tb) in enumerate(sub):
                    if mode == "FM":
                        epi(ph, est, pts[ti], nb * NB + m * 128, 128, tg * TG + tb * 512, min(512, TG))
                    else:
                        epi(ph, est, pts[ti], nb * NB, NB, tg * TG + tb * 128, 128)
    ph.close()


def epi_store_fm(k, dstT, row0=0):
    def setup(ph):
        return Ring(ph, 3, [128, 512], BF16, "eo")

    def epi(ph, st, pts, n0, nsz, t0, tsz):
        p, pb = pts[0]
        o, ob = st.next()
        k.op(k.act, lambda e: e.activation(out=o[:, 0:tsz], in_=p[:, 0:tsz], func=AF.Copy), reads=[pb], writes=[ob])
        k.dma(k.pool, dstT[row0 + n0:row0 + n0 + 128, t0:t0 + tsz], o[:, 0:tsz], ob, False)
    return setup, epi


def epi_store_tm(k, dst, col0=0):
    def setup(ph):
        return Ring(ph, 3, [128, 512], BF16, "eo")

    def epi(ph, st, pts, n0, nsz, t0, tsz):
        p, pb = pts[0]
        o, ob = st.next()
        k.op(k.act, lambda e: e.activation(out=o[:, 0:nsz], in_=p[:, 0:nsz], func=AF.Copy), reads=[pb], writes=[ob])
        k.dma(k.pool, dst[t0:t0 + 128, col0 + n0:col0 + n0 + nsz], o[:, 0:nsz], ob, False)
    return setup, epi


def epi_headnorm_fm(k, C, dstT, gain_ap, row0=0):
    def setup(ph):
        g = ph.sb([128, 2], F32, "hg")
        gb = ph.buf("hg")
        k.dma(k.sync, g[:, 0:1], gain_ap.rearrange("(p o) -> p o", o=1), gb, True)
        k.op(k.dve, lambda e: e.tensor_copy(out=g[:, 1:2], in_=g[:, 0:1]), reads=[gb], writes=[gb])
        return dict(g=g, gb=gb, sq=Ring(ph, 2, [128, 512], BF16, "sq"), ps=Ring(ph, 2, [128, 512], F32, "pss", psum=True),
                    rs=Ring(ph, 2, [128, 512], F32, "rs"), o=Ring(ph, 3, [128, 512], BF16, "eo"))

    def epi(ph, st, pts, n0, nsz, t0, tsz):
        p, pb = pts[0]
        sq, sqb = st["sq"].next()
        k.op(k.act, lambda e: e.activation(out=sq[:, 0:tsz], in_=p[:, 0:tsz], func=AF.Square), reads=[pb], writes=[sqb])
        ps2, ps2b = st["ps"].next()
        k.op(k.pe, lambda e: e.matmul(ps2[:, 0:tsz], lhsT=C["ones"][:], rhs=sq[:, 0:tsz], start=True, stop=True),
             reads=[sqb, C["onesb"]], writes=[ps2b])
        rs, rsb = st["rs"].next()
        k.op(k.act, lambda e: e.activation(out=rs[:, 0:tsz], in_=ps2[:, 0:tsz], func=AF.Sqrt, scale=1.0 / 128.0,
                                           bias=C["eps"][:, 0:1]), reads=[ps2b, C["epsb"]], writes=[rsb])
        k.op(k.dve, lambda e: e.reciprocal(out=rs[:, 0:tsz], in_=rs[:, 0:tsz]), reads=[rsb], writes=[rsb])
        o, ob = st["o"].next()
        k.op(k.dve, lambda e: e.scalar_tensor_tensor(out=o[:, 0:tsz], in0=p[:, 0:tsz], scalar=st["g"][:, 1:2],
                                                     in1=rs[:, 0:tsz], op0=ALU.mult, op1=ALU.mult),
             reads=[pb, rsb, st["gb"]], writes=[ob])
        k.dma(k.pool, dstT[row0 + n0:row0 + n0 + 128, t0:t0 + tsz], o[:, 0:tsz], ob, False)
    return setup, epi


def epi_resid_tm(k, src, dst, glu=False):
    def setup(ph):
        return dict(r=Ring(ph, 3, [128, 512], F32, "res"), sg=Ring(ph, 2, [128, 512], F32, "sg"))

    def epi(ph, st, pts, n0, nsz, t0, tsz):
        r, rb = st["r"].next()
        k.dma(k.pool, r[:, 0:nsz], src[t0:t0 + 128, n0:n0 + nsz], rb, True)
        p, pb = pts[0]
        if glu:
            p2, pb2 = pts[1]
            sg, sgb = st["sg"].next()
            k.op(k.act, lambda e: e.activation(out=sg[:, 0:nsz], in_=p2[:, 0:nsz], func=AF.Sigmoid),
                 reads=[pb2], writes=[sgb])
            k.op(k.dve, lambda e: e.tensor_tensor(out=sg[:, 0:nsz], in0=sg[:, 0:nsz], in1=p[:, 0:nsz], op=ALU.mult),
                 reads=[pb, sgb], writes=[sgb])
            k.op(k.dve, lambda e: e.tensor_tensor(out=r[:, 0:nsz], in0=r[:, 0:nsz], in1=sg[:, 0:nsz], op=ALU.add),
                 reads=[rb, sgb], writes=[rb])
        else:
            k.op(k.dve, lambda e: e.tensor_tensor(out=r[:, 0:nsz], in0=r[:, 0:nsz], in1=p[:, 0:nsz], op=ALU.add),
                 reads=[rb, pb], writes=[rb])
        k.dma(k.pool, dst[t0:t0 + 128, n0:n0 + nsz], r[:, 0:nsz], rb, False)
    return setup, epi


def epi_swiglu_fm(k, dstT):
    def setup(ph):
        return dict(s=Ring(ph, 2, [128, 512], F32, "sl"), o=Ring(ph, 3, [128, 512], BF16, "eo"))

    def epi(ph, st, pts, n0, nsz, t0, tsz):
        pg, pgb = pts[0]
        pu, pub = pts[1]
        s, sb_ = st["s"].next()
        k.op(k.act, lambda e: e.activation(out=s[:, 0:tsz], in_=pg[:, 0:tsz], func=AF.Silu), reads=[pgb], writes=[sb_])
        o, ob = st["o"].next()
        k.op(k.dve, lambda e: e.tensor_tensor(out=o[:, 0:tsz], in0=s[:, 0:tsz], in1=pu[:, 0:tsz], op=ALU.mult),
             reads=[sb_, pub], writes=[ob])
        k.dma(k.pool, dstT[n0:n0 + 128, t0:t0 + tsz], o[:, 0:tsz], ob, False)
    return setup, epi


def phase_cross_attn(k, C, cqT, ckT, cv, coT, Tn):
    XH, NM = C["XH"], C["NM"]
    Th = Tn // 2
    KCH = NM // 128
    ph = Phase(k)
    kt = ph.sb([128, XH, 2 * NM], BF16, "ckt")
    ktb = ph.buf("ckt")
    for h in range(XH):
        k.dma(k.sync, kt[:, h, :], ckT[h * 128:(h + 1) * 128, :], ktb, True)
    vt = ph.sb([128, 2 * KCH, XH * 128], BF16, "cvt")
    vtb = ph.buf("cvt")
    k.dma(k.sync, vt[:], cv.rearrange("(c p) n -> p c n", p=128), vtb, True)
    qr = Ring(ph, 2, [128, 512], BF16, "cq")
    pss = Ring(ph, 2, [128, KCH, 512], F32, "pss", psum=True)
    pso = Ring(ph, 2, [128, 512], F32, "pso", psum=True)
    psd = Ring(ph, 2, [128, 512], F32, "psd", psum=True)
    pr = Ring(ph, 2, [128, KCH, 512], BF16, "pT")
    rd = Ring(ph, 2, [128, 512], F32, "rden")
    orr = Ring(ph, 3, [128, 512], BF16, "co")
    scale = 1.0 / math.sqrt(128.0)
    TB = min(512, Th)
    for h in range(XH):
        for hs in range(2):
            for tb in range(Th // TB):
                t0 = hs * Th + tb * TB
                q, qb = qr.next()
                k.dma(k.sync, q[:, 0:TB], cqT[h * 128:(h + 1) * 128, t0:t0 + TB], qb, True)
                s, sb_ = pss.next()
                for kc in range(KCH):
                    k.op(k.pe, lambda e: e.matmul(s[:, kc, 0:TB], lhsT=kt[:, h, hs * NM + kc * 128:hs * NM + (kc + 1) * 128],
                                                  rhs=q[:, 0:TB], start=True, stop=True), reads=[ktb, qb], writes=[sb_])
                p, pb = pr.next()
                k.op(k.act, lambda e: e.activation(out=p[:, :, 0:TB], in_=s[:, :, 0:TB], func=AF.Exp, scale=scale),
                     reads=[sb_], writes=[pb])
                o, ob = pso.next()
                d, db = psd.next()
                for kc in range(KCH):
                    k.op(k.pe, lambda e: e.matmul(o[:, 0:TB], lhsT=vt[:, hs * KCH + kc, h * 128:(h + 1) * 128],
                                                  rhs=p[:, kc, 0:TB], start=(kc == 0), stop=(kc == KCH - 1)),
                         reads=[vtb, pb], writes=[ob])
                for kc in range(KCH):
                    k.op(k.pe, lambda e: e.matmul(d[:, 0:TB], lhsT=C["ones"][:], rhs=p[:, kc, 0:TB],
                                                  start=(kc == 0), stop=(kc == KCH - 1)),
                         reads=[C["onesb"], pb], writes=[db])
                r, rb = rd.next()
                k.op(k.dve, lambda e: e.reciprocal(out=r[:, 0:TB], in_=d[:, 0:TB]), reads=[db], writes=[rb])
                oo, oob = orr.next()
                k.op(k.dve, lambda e: e.tensor_tensor(out=oo[:, 0:TB], in0=o[:, 0:TB], in1=r[:, 0:TB], op=ALU.mult),
                     reads=[ob, rb], writes=[oob])
                k.dma(k.pool, coT[h * 128:(h + 1) * 128, t0:t0 + TB], oo[:, 0:TB], oob, False)
    ph.close()


def setup_consts(k, C, ident_ap):
    nc = k.nc
    C["es"] = ExitStack()
    C["ident"] = C["es"].enter_context(nc.sbuf_tensor("ident_sb", [128, 128], BF16))
    C["identb"] = Buf("ident")
    k.dma(k.sync, C["ident"][:], ident_ap, C["identb"], True)
    C["ones"] = C["es"].enter_context(nc.sbuf_tensor("ones_sb", [128, 128], BF16))
    C["onesb"] = Buf("ones")
    k.op(k.dve, lambda e: e.memset(C["ones"][:], 1.0), writes=[C["onesb"]])
    C["eps"] = C["es"].enter_context(nc.sbuf_tensor("eps_sb", [128, 2], F32))
    C["epsb"] = Buf("eps")
    k.op(k.dve, lambda e: e.memset(C["eps"][:], EPS), writes=[C["epsb"]])


def dense_tail(k, C, A, layer, Tn):
    D, F, XW, NM = C["D"], C["F"], C["XW"], C["NM"]
    y = A["y"]
    W = A["Wb"]
    phase_normT(k, C, y, A["g_cross"][layer], A["xnT"], Tn)
    phase_normT(k, C, A["mem"], A["g_mem"][layer], A["mnT"], 2 * NM)
    s, e = epi_headnorm_fm(k, C, A["cqT"], A["x_q_gain"][layer])
    gemm(k, C, A["xnT"], [W["x_wq%d" % layer]], Tn, "FM", e, epi_setup=s)
    s, e = epi_headnorm_fm(k, C, A["ckT"], A["x_k_gain"][layer])
    gemm(k, C, A["mnT"], [W["x_wk%d" % layer]], 2 * NM, "FM", e, epi_setup=s)
    s, e = epi_store_tm(k, A["cv"])
    gemm(k, C, A["mnT"], [W["x_wv%d" % layer]], 2 * NM, "TM", e, epi_setup=s)
    phase_cross_attn(k, C, A["cqT"], A["ckT"], A["cv"], A["coT"], Tn)
    s, e = epi_resid_tm(k, y, y)
    gemm(k, C, A["coT"], [W["x_wo%d" % layer]], Tn, "TM", e, epi_setup=s)
    phase_normT(k, C, y, A["g_ffn"][layer], A["xnT"], Tn)
    s, e = epi_swiglu_fm(k, A["hT"])
    gemm(k, C, A["xnT"], [W["gate%d" % layer], W["up%d" % layer]], Tn, "FM", e, epi_setup=s)
    s, e = epi_resid_tm(k, y, y)
    gemm(k, C, A["hT"], [W["down%d" % layer]], Tn, "TM", e, epi_setup=s, TG=512, KBmax=22)


def cmul_acc(k, E, dre, dim, sre, sim, pr, pi, npi, bufs):
    R, W = bufs
    k.op(E, lambda e: e.scalar_tensor_tensor(out=dre, in0=sre, scalar=pr, in1=dre, op0=ALU.mult, op1=ALU.add), reads=R, writes=W)
    k.op(E, lambda e: e.scalar_tensor_tensor(out=dre, in0=sim, scalar=npi, in1=dre, op0=ALU.mult, op1=ALU.add), reads=R, writes=W)
    k.op(E, lambda e: e.scalar_tensor_tensor(out=dim, in0=sim, scalar=pr, in1=dim, op0=ALU.mult, op1=ALU.add), reads=R, writes=W)
    k.op(E, lambda e: e.scalar_tensor_tensor(out=dim, in0=sre, scalar=pi, in1=dim, op0=ALU.mult, op1=ALU.add), reads=R, writes=W)


def scan_seq(k, E, h, Tn, Th, rev, PW, FL, slab, bufs, flag_ap, sv):
    hre, him = h[:, 0, :], h[:, 1, :]
    nl = Tn.bit_length() - 1
    assert (1 << nl) == Tn
    R, Wr = bufs
    k._deps(E, R, Wr)

    def sel(hh, l, i, k0=None, k1=None):
        st = 1 << l
        v = hh.rearrange("p (k i s) -> p k i s", i=2, s=st)
        ii = (1 - i) if rev else i
        ss = 0 if rev else st - 1
        if k0 is None:
            return v[:, :, ii, ss]
        return v[:, k0:k1, ii, ss]

    def sel2(l, i, k0=None, k1=None):
        st = 1 << l
        v = h[:, :, :].rearrange("p c (k i s) -> p c k i s", i=2, s=st)
        ii = (1 - i) if rev else i
        ss = 0 if rev else st - 1
        if k0 is None:
            return v[:, :, :, ii, ss]
        return v[:, :, k0:k1, ii, ss]

    def cm(l, kd, ks, sc, prev):
        pr, pi, npi = sc
        STT = lambda o, i0, sca, i1: (lambda e: e.scalar_tensor_tensor(out=o, in0=i0, scalar=sca, in1=i1, op0=ALU.mult, op1=ALU.add))
        d2, s2 = sel2(l, *kd), sel2(l, *ks)
        dre, dim = sel(hre, l, *kd), sel(him, l, *kd)
        sre, sim = sel(hre, l, *ks), sel(him, l, *ks)
        a = k.op_w(E, STT(d2, s2, pr, d2), prev)
        c = k.op_w(E, STT(dre, sim, npi, dre), [a])
        d = k.op_w(E, STT(dim, sre, pi, dim), [a])
        return [c, d]

    def scal(t):
        return t[0][:, slab:slab + 1], t[1][:, slab:slab + 1], t[2][:, slab:slab + 1]

    prev = []
    for l in range(nl):
        sc = scal(FL if (1 << l) == Th else PW[l])
        prev = cm(l, (1,), (0,), sc, prev)
    pos = Th if rev else Th - 1
    t1 = k.op_w(E, lambda e: e.tensor_copy(out=sv[:, :], in_=h[:, :, pos]), prev)
    t2 = k.op_w(E, lambda e: e.tensor_scalar_mul(out=h[:, :, pos], in0=h[:, :, pos], scalar1=flag_ap), [t1])
    prev = [t2]
    for l in range(nl - 2, -1, -1):
        K = Tn >> (l + 1)
        sc = scal(PW[l])
        if rev:
            prev = cm(l, (0, 0, K - 1), (1, 1, K), sc, prev)
        else:
            prev = cm(l, (0, 1, K), (1, 0, K - 1), sc, prev)
    t3 = k.op_w(E, lambda e: e.tensor_copy(out=h[:, :, pos], in_=sv[:, :]), prev)
    k._commit(t3, R, Wr)


def phase_s5(k, C, A, Tn):
    D, G = C["D"], C["G"]
    DC = D // 128
    NS = G
    Th = Tn // 2
    ph = Phase(k)
    names = ["lr", "li", "ls", "st", "a", "th", "mag", "r1", "r2", "sn", "cs", "lbr", "lbi", "den", "t1", "t2", "cr", "ci", "nci"]
    P = {n: ph.sb([128, NS], F32, "s5" + n) for n in names}
    pb = ph.buf("s5par")
    cst = ph.sb([128, 4], F32, "s5c")
    k.op(k.dve, lambda e: e.memset(cst[:, 0:1], -math.pi), writes=[pb])
    k.dma(k.sync, P["lr"][:], A["s5_lr"], pb, True)
    k.dma(k.sync, P["li"][:], A["s5_li"], pb, True)
    k.dma(k.sync, P["ls"][:], A["s5_ls"], pb, True)
    flag = ph.sb([128, 1], F32, "flag")
    k.dma(k.sync, flag[:], A["flag"], pb, True)
    V = k.dve
    RW = dict(reads=[pb], writes=[pb])

    def tt(o, a_, b_, op):
        k.op(V, lambda e: e.tensor_tensor(out=P[o][:], in0=P[a_][:], in1=P[b_][:], op=op), **RW)

    def ts(o, a_, s1, s2, o0, o1):
        k.op(V, lambda e: e.tensor_scalar(out=P[o][:], in0=P[a_][:], scalar1=s1, scalar2=s2, op0=o0, op1=o1), **RW)

    k.op(k.act, lambda e: e.activation(out=P["st"][:], in_=P["ls"][:], func=AF.Exp), **RW)
    k.op(V, lambda e: e.tensor_scalar_min(out=P["lr"][:], in0=P["lr"][:], scalar1=-1e-4), **RW)
    tt("a", "lr", "st", ALU.mult)
    tt("th", "li", "st", ALU.mult)
    k.op(k.act, lambda e: e.activation(out=P["mag"][:], in_=P["a"][:], func=AF.Exp), **RW)
    MAGIC = 12582912.0
    for (o, sh) in (("sn", 0.0), ("cs", 0.5 * math.pi)):
        ts("r1", "th", sh, 1.0, ALU.add, ALU.mult)
        ts("r2", "r1", 1.0 / (2 * math.pi), MAGIC, ALU.mult, ALU.add)
        ts("r2", "r2", -MAGIC, -2 * math.pi, ALU.add, ALU.mult)
        tt("r1", "r1", "r2", ALU.add)
        ts("r1", "r1", math.pi, -math.pi, ALU.min, ALU.max)
        k.op(k.act, lambda e: e.activation(out=P[o][:], in_=P["r1"][:], func=AF.Sin), **RW)
    tt("lbr", "mag", "cs", ALU.mult)
    tt("lbi", "mag", "sn", ALU.mult)
    tt("den", "lr", "lr", ALU.mult)
    tt("t1", "li", "li", ALU.mult)
    tt("den", "den", "t1", ALU.add)
    k.op(V, lambda e: e.reciprocal(out=P["den"][:], in_=P["den"][:]), **RW)
    ts("t1", "lbr", -1.0, 1.0, ALU.add, ALU.mult)
    tt("cr", "t1", "lr", ALU.mult)
    tt("t2", "lbi", "li", ALU.mult)
    tt("cr", "cr", "t2", ALU.add)
    tt("cr", "cr", "den", ALU.mult)
    tt("ci", "lbi", "lr", ALU.mult)
    tt("t2", "t1", "li", ALU.mult)
    tt("ci", "ci", "t2", ALU.subtract)
    tt("ci", "ci", "den", ALU.mult)
    ts("nci", "ci", -1.0, 1.0, ALU.mult, ALU.mult)
    nl = Tn.bit_length() - 1
    PW = []
    cur = (P["lbr"], P["lbi"])
    for l in range(nl):
        if l > 0:
            pr_ = ph.sb([128, NS], F32, "pwr")
            pi_ = ph.sb([128, NS], F32, "pwi")
            t_ = P["t1"]
            k.op(V, lambda e: e.tensor_tensor(out=pr_[:], in0=cur[0][:], in1=cur[0][:], op=ALU.mult), **RW)
            k.op(V, lambda e: e.tensor_tensor(out=t_[:], in0=cur[1][:], in1=cur[1][:], op=ALU.mult), **RW)
            k.op(V, lambda e: e.tensor_tensor(out=pr_[:], in0=pr_[:], in1=t_[:], op=ALU.subtract), **RW)
            k.op(V, lambda e: e.tensor_tensor(out=pi_[:], in0=cur[0][:], in1=cur[1][:], op=ALU.mult), **RW)
            k.op(V, lambda e: e.tensor_scalar_mul(out=pi_[:], in0=pi_[:], scalar1=2.0), **RW)
            cur = (pr_, pi_)
        npi_ = ph.sb([128, NS], F32, "pwn")
        k.op(V, lambda e: e.tensor_scalar_mul(out=npi_[:], in0=cur[1][:], scalar1=-1.0), **RW)
        PW.append((cur[0], cur[1], npi_))
    ltop = Th.bit_length() - 1
    flr = ph.sb([128, NS], F32, "flr")
    fli = ph.sb([128, NS], F32, "fli")
    nfli = ph.sb([128, NS], F32, "nfli")
    k.op(V, lambda e: e.tensor_scalar_mul(out=flr[:], in0=PW[ltop][0][:], scalar1=flag[:, 0:1]), **RW)
    k.op(V, lambda e: e.tensor_scalar_mul(out=fli[:], in0=PW[ltop][1][:], scalar1=flag[:, 0:1]), **RW)
    k.op(V, lambda e: e.tensor_scalar_mul(out=nfli[:], in0=fli[:], scalar1=-1.0), **RW)
    FL = (flr, fli, nfli)
    sv = ph.sb([128, 2], F32, "s5sv")
    dsk = ph.sb([128, DC], F32, "dsk")
    k.dma(k.sync, dsk[:], A["s5_dT"], pb, True)

    xr = Ring(ph, 1, [128, Tn], BF16, "s5x")
    hR = Ring(ph, 2, [128, 2, Tn], F32, "s5h")
    bst = Ring(ph, 2, [128, 2, 128], F32, "s5bs")
    bbf = Ring(ph, 2, [128, 2, 128], BF16, "s5bb")
    cstg = Ring(ph, 2, [128, 2, 128], F32, "s5cs")
    cbf = Ring(ph, 3, [128, 2, 128], F32, "s5cb")
    ctmp = Ring(ph, 2, [128, 2, 128], F32, "s5ct")
    yacc = Ring(ph, 2, [128, Tn], F32, "s5y")
    yo = Ring(ph, 3, [128, 512], BF16, "s5yo")
    gtmp = Ring(ph, 2, [128, 512], F32, "s5g")
    psb = Ring(ph, 4, [128, 512], F32, "s5pb", psum=True)
    psy = Ring(ph, 4, [128, 512], F32, "s5py", psum=True)
    NTB = Tn // 512 if Tn >= 512 else 1
    TB = min(512, Tn)
    E = k.dve

    def emit_out(h, hb, cb_, cbb, ya, yab):
        for tb in range(NTB):
            py, pyb = psy.next()
            k.op(k.pe, lambda e: e.matmul(py[:, 0:TB], lhsT=cb_[:, 0, :], rhs=h[:, 0, tb * TB:(tb + 1) * TB], start=True, stop=False),
                 reads=[cbb, hb], writes=[pyb])
            k.op(k.pe, lambda e: e.matmul(py[:, 0:TB], lhsT=cb_[:, 1, :], rhs=h[:, 1, tb * TB:(tb + 1) * TB], start=False, stop=True),
                 reads=[cbb, hb], writes=[pyb])
            k.op(k.dve, lambda e: e.tensor_tensor(out=ya[:, tb * TB:(tb + 1) * TB], in0=ya[:, tb * TB:(tb + 1) * TB], in1=py[:, 0:TB], op=ALU.add),
                 reads=[pyb, yab], writes=[yab])

    def emit_gelu(oc, ya, yab):
        for tb in range(NTB):
            sl = slice(tb * TB, (tb + 1) * TB)
            g, gb_ = gtmp.next()
            k.op(k.dve, lambda e: e.tensor_tensor(out=g[:, 0:TB], in0=ya[:, sl], in1=ya[:, sl], op=ALU.mult), reads=[yab], writes=[gb_])
            k.op(k.dve, lambda e: e.tensor_scalar(out=g[:, 0:TB], in0=g[:, 0:TB], scalar1=0.044715, scalar2=1.0, op0=ALU.mult, op1=ALU.add), reads=[gb_], writes=[gb_])
            k.op(k.dve, lambda e: e.tensor_tensor(out=g[:, 0:TB], in0=g[:, 0:TB], in1=ya[:, sl], op=ALU.mult), reads=[gb_, yab], writes=[gb_])
            k.op(k.act, lambda e: e.activation(out=g[:, 0:TB], in_=g[:, 0:TB], func=AF.Sigmoid, scale=1.5957691216057308), reads=[gb_], writes=[gb_])
            o, ob = yo.next()
            k.op(k.dve, lambda e: e.tensor_tensor(out=o[:, 0:TB], in0=g[:, 0:TB], in1=ya[:, sl], op=ALU.mult), reads=[gb_, yab], writes=[ob])
            k.dma(k.pool, A["s5T"][oc * 128:(oc + 1) * 128, sl], o[:, 0:TB], ob, False)

    pend = None
    pend_gelu = None
    for oc in range(DC):
        x, xb = xr.next()
        k.dma(k.sync, x[:], A["xnT"][oc * 128:(oc + 1) * 128, :], xb, True)
        ya, yab = yacc.next()
        k.op(k.dve, lambda e: e.tensor_scalar_mul(out=ya[:], in0=x[:], scalar1=dsk[:, oc:oc + 1]), reads=[xb, pb], writes=[yab])
        for d in range(2):
            for gq in range(4):
                gp = oc * 4 + gq
                slab = d * (G // 2) + gp
                bs, bsb = bst.next()
                k.dma(k.sync, bs[:, 0, :], A["s5_Bre"][:, slab * 128:(slab + 1) * 128], bsb, True)
                k.dma(k.sync, bs[:, 1, :], A["s5_Bim"][:, slab * 128:(slab + 1) * 128], bsb, True)
                bb, bbb = bbf.next()
                k.op(k.act, lambda e: e.activation(out=bb[:], in_=bs[:], func=AF.Copy), reads=[bsb], writes=[bbb])
                cs_, csb = cstg.next()
                k.dma(k.sync, cs_[:, 0, :], A["s5_Cre"][:, slab * 128:(slab + 1) * 128], csb, True)
                k.dma(k.sync, cs_[:, 1, :], A["s5_Cim"][:, slab * 128:(slab + 1) * 128], csb, True)
                ct, ctb = ctmp.next()
                cb_, cbb = cbf.next()
                crs, cis, ncis = P["cr"][:, slab:slab + 1], P["ci"][:, slab:slab + 1], P["nci"][:, slab:slab + 1]
                k.op(E, lambda e: e.tensor_scalar_mul(out=ct[:, 0, :], in0=cs_[:, 0, :], scalar1=crs), reads=[csb, pb], writes=[ctb])
                k.op(E, lambda e: e.scalar_tensor_tensor(out=cb_[:, 0, :], in0=cs_[:, 1, :], scalar=ncis, in1=ct[:, 0, :],
                                                         op0=ALU.mult, op1=ALU.add), reads=[csb, ctb, pb], writes=[cbb])
                k.op(E, lambda e: e.tensor_scalar_mul(out=ct[:, 1, :], in0=cs_[:, 0, :], scalar1=ncis), reads=[csb, pb], writes=[ctb])
                k.op(E, lambda e: e.tensor_scalar(out=ct[:, 0, :], in0=cs_[:, 1, :], scalar1=crs, scalar2=-1.0, op0=ALU.mult, op1=ALU.mult),
                     reads=[csb, pb], writes=[ctb])
                k.op(E, lambda e: e.tensor_tensor(out=cb_[:, 1, :], in0=ct[:, 1, :], in1=ct[:, 0, :], op=ALU.add), reads=[ctb], writes=[cbb])
                h, hb = hR.next()
                for part in range(2):
                    for tb in range(NTB):
                        p, pbb = psb.next()
                        k.op(k.pe, lambda e: e.matmul(p[:, 0:TB], lhsT=bb[:, part, :], rhs=x[:, tb * TB:(tb + 1) * TB], start=True, stop=True),
                             reads=[bbb, xb], writes=[pbb])
                        k.op(k.act, lambda e: e.activation(out=h[:, part, tb * TB:(tb + 1) * TB], in_=p[:, 0:TB], func=AF.Copy),
                             reads=[pbb], writes=[hb])
                bufs = ([hb, pb], [hb])
                scan_seq(k, E, h, Tn, Th, (d == 1), PW, FL, slab, bufs, flag[:, 0:1], sv)
                if pend is not None:
                    emit_out(*pend)
                if pend_gelu is not None:
                    emit_gelu(*pend_gelu)
                    pend_gelu = None
                pend = (h, hb, cb_, cbb, ya, yab)
        pend_gelu = (oc, ya, yab)
    emit_out(*pend)
    emit_gelu(*pend_gelu)
    ph.close()


def s5_host_layout(inp, G):
    P = 64
    out = {}

    def st(a):
        a = np.asarray(a).reshape(2, G // 2, 2, P)
        return np.ascontiguousarray(a.transpose(2, 3, 0, 1).reshape(128, G))

    out["s5_lr"] = st(inp["s5_lam_re"][0])
    out["s5_li"] = st(inp["s5_lam_im"][0])
    out["s5_ls"] = st(np.broadcast_to(np.asarray(inp["s5_log_step"][0])[:, :, None], (2, G, P)))
    NS = G
    for nm, src in (("s5_Bre", inp["s5_b_re"][0]), ("s5_Bim", inp["s5_b_im"][0])):
        src = np.asarray(src)
        blk = np.zeros((8, 16, NS, 2, P), np.float32)
        for d in range(2):
            for gp in range(G // 2):
                slab = d * (G // 2) + gp
                for g2 in range(2):
                    gl = 2 * (gp % 4) + g2
                    blk[gl, :, slab, g2, :] = src[d, 2 * gp + g2].T
        out[nm] = blk.reshape(128, NS * 128)
    for nm, src in (("s5_Cre", inp["s5_c_re"][0]), ("s5_Cim", inp["s5_c_im"][0])):
        src = np.asarray(src)
        blk = np.zeros((2, P, NS, 8, 16), np.float32)
        for d in range(2):
            for gp in range(G // 2):
                slab = d * (G // 2) + gp
                for g2 in range(2):
                    gl = 2 * (gp % 4) + g2
                    blk[g2, :, slab, gl, :] = src[d, 2 * gp + g2].T
        out[nm] = blk.reshape(128, NS * 128)
    dsk = np.asarray(inp["s5_d"][0]).reshape(-1)
    out["s5_dT"] = np.ascontiguousarray(dsk.reshape(-1, 128).T)
    return out


def na_host_tables(rpb, Rw, kind):
    rpb = np.asarray(rpb)
    H = rpb.shape[0]
    kc = np.arange(64)[:, None]
    qc = np.arange(64)[None, :]
    cidx = np.clip(kc - qc + 15, 0, 30)
    qstart = np.clip(qc - 8, 0, 48)
    cvalid = (kc >= qstart) & (kc < qstart + 16)
    TT = np.zeros((2, 64, H, 14, 64), np.float32)
    for kp in range(2):
        for e in range(14):
            TT[kp, :, :, e, :] = rpb[:, e + kp][:, cidx].transpose(1, 0, 2)
    CM = np.where(cvalid, 0.0, -30000.0).astype(np.float32)
    CM = np.concatenate([CM, CM], 0)
    grids = [(0, Rw)] if kind == "p" else [(0, Rw // 2), (Rw // 2, Rw // 2)]
    rm = np.full((Rw, Rw), -30000.0, np.float32)
    for base, Rg in grids:
        kr_ = min(8, Rg)
        for rr in range(Rg):
            r0 = int(np.clip(rr - kr_ // 2, 0, Rg - kr_))
            rm[base + rr, base + r0:base + r0 + kr_] = 0.0
    NP = Rw // 2
    RMq = np.full((2, NP, 7, 2, 64), -30000.0, np.float32)
    for pi in range(NP):
        r = 2 * pi
        for ci in range(7):
            kr = r - 6 + 2 * ci
            if kr < 0 or kr > Rw - 2:
                continue
            for kp in range(2):
                for qp in range(2):
                    RMq[kp, pi, ci, qp, :] = rm[r + qp, kr + kp]
    Aind = np.zeros((2, 2, 64), np.float32)
    Aind[0, 0] = 1.0
    Aind[1, 1] = 1.0
    return dict(na_TT=TT.reshape(128, H * 14 * 64), na_CM=CM,
                na_RMq=RMq.reshape(2, NP * 7 * 128).astype(ml_dtypes.bfloat16),
                na_A=Aind.reshape(2, 128).astype(ml_dtypes.bfloat16))


def phase_na(k, C, A, Tn):
    H = C["NAH"]
    Rw = Tn // 64
    NP = Rw // 2
    NT = Tn // 128
    ph = Phase(k)
    TT = ph.sb([128, H, 14, 64], F32, "naTT")
    ttb = ph.buf("naTT")
    k.dma(k.sync, TT[:], A["na_TT"].rearrange("p (h e q) -> p h e q", h=H, e=14), ttb, True)
    CM = ph.sb([128, 64], F32, "naCM")
    k.dma(k.sync, CM[:], A["na_CM"], ttb, True)
    for h in range(H):
        k.op(k.dve, lambda e: e.tensor_tensor(out=TT[:, h, :, :], in0=TT[:, h, :, :],
                                              in1=CM[:].unsqueeze(1).broadcast_to([128, 14, 64]), op=ALU.add), reads=[ttb], writes=[ttb])
        k.op(k.dve, lambda e: e.tensor_scalar_mul(out=TT[:, h, :, :], in0=TT[:, h, :, :], scalar1=math.sqrt(128.0)), reads=[ttb], writes=[ttb])
    RM = ph.sb([2, NP * 7 * 128], BF16, "naRM")
    rmb = ph.buf("naRM")
    k.dma(k.sync, RM[:], A["na_RMq"], rmb, True)
    Ai = ph.sb([2, 128], BF16, "naA")
    k.dma(k.sync, Ai[:], A["na_A"], rmb, True)
    qr = Ring(ph, 2, [128, Tn], BF16, "naq")
    kr_ = Ring(ph, 2, [128, Tn], BF16, "nak")
    vr = Ring(ph, 2, [128, NT, 128], BF16, "nav")
    og = Ring(ph, 2, [128, Tn], BF16, "nao")
    pS = Ring(ph, 2, [128, 8, 128], F32, "naS", psum=True)
    pO = Ring(ph, 2, [128, 128], F32, "naO", psum=True)
    pD = Ring(ph, 2, [128, 128], F32, "naD", psum=True)
    sS = Ring(ph, 2, [128, 8, 128], F32, "naSs")
    sP = Ring(ph, 2, [128, 8, 128], BF16, "naP")
    rD = Ring(ph, 2, [128, 128], F32, "naR")
    scale = 1.0 / math.sqrt(128.0)
    for h in range(H):
        q, qb = qr.next()
        kk, kb = kr_.next()
        v, vb = vr.next()
        o, ob = og.next()
        k.dma(k.sync, q[:], A["qT"][h * 128:(h + 1) * 128, :], qb, True)
        k.dma(k.sync, kk[:], A["kT"][h * 128:(h + 1) * 128, :], kb, True)
        k.dma(k.sync, v[:], A["v"].rearrange("(n p) c -> p n c", p=128)[:, :, h * 128:(h + 1) * 128], vb, True)
        for pi in range(NP):
            r = 2 * pi
            chunks = [ci for ci in range(7) if 0 <= r - 6 + 2 * ci <= Rw - 2]
            S, Sb = pS.next()
            for ci in chunks:
                krow = r - 6 + 2 * ci
                k.op(k.pe, lambda e: e.matmul(S[:, ci, :], lhsT=kk[:, 64 * krow:64 * krow + 128], rhs=q[:, 64 * r:64 * r + 128],
                                              start=True, stop=False), reads=[kb, qb], writes=[Sb])
                off = (pi * 7 + ci) * 128
                k.op(k.pe, lambda e: e.matmul(S[:, ci, :], lhsT=Ai[:, :], rhs=RM[:, off:off + 128], start=False, stop=True),
                     reads=[rmb], writes=[Sb])
            Ss, Ssb = sS.next()
            c0, c1 = chunks[0], chunks[-1] + 1
            TT5 = TT[:, h, :, :].rearrange("p (a b) q -> p a b q", b=2)
            for qp in range(2):
                k.op(k.dve, lambda e: e.tensor_tensor(out=Ss[:, c0:c1, qp * 64:(qp + 1) * 64], in0=S[:, c0:c1, qp * 64:(qp + 1) * 64],
                                                      in1=TT5[:, c0:c1, 1 - qp, :], op=ALU.add), reads=[Sb, ttb], writes=[Ssb])
            Pp, Pb = sP.next()
            k.op(k.act, lambda e: e.activation(out=Pp[:, c0:c1, :], in_=Ss[:, c0:c1, :], func=AF.Exp, scale=scale), reads=[Ssb], writes=[Pb])
            O, Ob = pO.next()
            Dn, Db = pD.next()
            for j, ci in enumerate(chunks):
                krow = r - 6 + 2 * ci
                k.op(k.pe, lambda e: e.matmul(O[:], lhsT=v[:, krow // 2, :], rhs=Pp[:, ci, :], start=(j == 0), stop=(j == len(chunks) - 1)),
                     reads=[vb, Pb], writes=[Ob])
            for j, ci in enumerate(chunks):
                k.op(k.pe, lambda e: e.matmul(Dn[:], lhsT=C["ones"][:], rhs=Pp[:, ci, :], start=(j == 0), stop=(j == len(chunks) - 1)),
                     reads=[C["onesb"], Pb], writes=[Db])
            rr, rrb = rD.next()
            k.op(k.dve, lambda e: e.reciprocal(out=rr[:], in_=Dn[:]), reads=[Db], writes=[rrb])
            k.op(k.dve, lambda e: e.tensor_tensor(out=o[:, 64 * r:64 * r + 128], in0=O[:], in1=rr[:], op=ALU.mult), reads=[Ob, rrb], writes=[ob])
        k.dma(k.pool, A["mixT"][h * 128:(h + 1) * 128, :], o[:], ob, False)
    ph.close()


def hy_host_consts(Tn, kind):
    L = Tn if kind == "p" else Tn // 2
    nseq = Tn // L
    pos = np.tile(np.arange(L), nseq)
    t = (pos / L).astype(np.float32)
    ang = 2.0 * math.pi * t[:, None].astype(np.float64) * np.arange(1, 17)
    z = np.concatenate([t[:, None], np.cos(ang), np.sin(ang)], -1).astype(np.float32)
    NT = Tn // 128
    lay = lambda a: np.ascontiguousarray(a.reshape(NT, 128).T)
    m0 = (pos != 0).astype(np.float32)
    ws = np.zeros(Tn, np.float32)
    ws[:L] = 1.0
    f = np.arange(L)
    w = math.pi * (2 * f[None, :] + 1) * np.arange(L)[:, None] / (2.0 * L)
    Cb, Sb = np.cos(w), np.sin(w)
    Cm = np.zeros((Tn, Tn), np.float32)
    Sm = np.zeros((Tn, Tn), np.float32)
    for s in range(nseq):
        Cm[s * L:(s + 1) * L, s * L:(s + 1) * L] = Cb
        Sm[s * L:(s + 1) * L, s * L:(s + 1) * L] = Sb
    bf = ml_dtypes.bfloat16
    Fm = np.concatenate([Cm, -Sm], 1).astype(bf)
    Wi = (np.concatenate([Cm.T, -Sm.T], 0) / L).astype(bf)
    return dict(hy_zT=np.ascontiguousarray(z.T), hy_ntn=lay(-t), hy_m0=lay(m0), hy_ws=lay(ws).astype(bf),
                hy_F=blk_host(Fm, pick_nb(2 * Tn)), hy_C=blk_host(Cm.astype(bf), pick_nb(Tn)),
                hy_S=blk_host(Sm.astype(bf), pick_nb(Tn)), hy_Wi=blk_host(Wi, pick_nb(Tn)))


def phase_hy_filter(k, C, A, Tn):
    HW = C["HW"]
    NT = Tn // 128
    ph = Phase(k)
    cb = ph.buf("hyc")
    ld = lambda t, src: k.dma(k.sync, t, src, cb, True)
    zT = ph.sb([33, Tn], F32, "hzT"); ld(zT[:], A["hy_zT"])
    w1 = ph.sb([33, 64], F32, "hw1"); ld(w1[:], A["hy_f_w1"])
    w2 = ph.sb([64, 64], F32, "hw2"); ld(w2[:], A["hy_f_w2"])
    w3 = ph.sb([64, 2 * HW], F32, "hw3"); ld(w3[:], A["hy_f_w3"])
    sc = ph.sb([64, 8], F32, "hsc")
    col = lambda a: a.rearrange("(p o) -> p o", o=1)
    ld(sc[:, 0:1], col(A["hy_f_freq"])); ld(sc[:, 1:2], col(A["hy_f_b1"])); ld(sc[:, 2:3], col(A["hy_f_b2"]))
    k.op(k.dve, lambda e: e.tensor_tensor(out=sc[:, 3:4], in0=sc[:, 0:1], in1=sc[:, 1:2], op=ALU.mult), reads=[cb], writes=[cb])
    k.op(k.dve, lambda e: e.tensor_tensor(out=sc[:, 4:5], in0=sc[:, 0:1], in1=sc[:, 2:3], op=ALU.mult), reads=[cb], writes=[cb])
    b3 = ph.sb([128, 2 * HW], F32, "hb3"); ld(b3[:], A["hy_f_b3"].partition_broadcast(128))
    eld = ph.sb([128, 2 * HW], F32, "held"); ld(eld[:], A["hy_log_decay"].rearrange("a b -> (a b)").partition_broadcast(128))
    k.op(k.act, lambda e: e.activation(out=eld[:], in_=eld[:], func=AF.Exp), reads=[cb], writes=[cb])
    ntn = ph.sb([128, NT], F32, "hntn"); ld(ntn[:], A["hy_ntn"])
    m0 = ph.sb([128, NT], F32, "hm0"); ld(m0[:], A["hy_m0"])
    ws = ph.sb([128, NT], BF16, "hws"); ld(ws[:], A["hy_ws"])
    h1 = ph.sb([64, Tn], F32, "hh1")
    h2 = ph.sb([64, Tn], F32, "hh2")
    hb = ph.buf("hh")
    pm = Ring(ph, 2, [64, 512], F32, "hpm", psum=True)
    tr = Ring(ph, 2, [64, 512], F32, "htr")
    MAGIC = 12582912.0
    TB = min(512, Tn)
    for (wt, Kd, src, dst, fbc) in ((w1, 33, zT, h1, 3), (w2, 64, h1, h2, 4)):
        for tb in range(Tn // TB):
            p, pb = pm.next()
            k.op(k.pe, lambda e: e.matmul(p[:, 0:TB], lhsT=wt[0:Kd, :], rhs=src[0:Kd, tb * TB:(tb + 1) * TB], start=True, stop=True),
                 reads=[cb, hb], writes=[pb])
            t1, t1b = tr.next()
            t2, t2b = tr.next()
            k.op(k.dve, lambda e: e.tensor_scalar(out=t1[:, 0:TB], in0=p[:, 0:TB], scalar1=sc[:, 0:1], scalar2=sc[:, fbc:fbc + 1],
                                                  op0=ALU.mult, op1=ALU.add), reads=[pb, cb], writes=[t1b])
            k.op(k.dve, lambda e: e.tensor_scalar(out=t2[:, 0:TB], in0=t1[:, 0:TB], scalar1=1.0 / (2 * math.pi), scalar2=MAGIC,
                                                  op0=ALU.mult, op1=ALU.add), reads=[t1b], writes=[t2b])
            k.op(k.dve, lambda e: e.tensor_scalar(out=t2[:, 0:TB], in0=t2[:, 0:TB], scalar1=-MAGIC, scalar2=-2 * math.pi,
                                                  op0=ALU.add, op1=ALU.mult), reads=[t2b], writes=[t2b])
            k.op(k.dve, lambda e: e.tensor_tensor(out=t1[:, 0:TB], in0=t1[:, 0:TB], in1=t2[:, 0:TB], op=ALU.add), reads=[t1b, t2b], writes=[t1b])
            k.op(k.dve, lambda e: e.tensor_scalar(out=t1[:, 0:TB], in0=t1[:, 0:TB], scalar1=math.pi, scalar2=-math.pi,
                                                  op0=ALU.min, op1=ALU.max), reads=[t1b], writes=[t1b])
            k.op(k.act, lambda e: e.activation(out=dst[:, tb * TB:(tb + 1) * TB], in_=t1[:, 0:TB], func=AF.Sin), reads=[t1b], writes=[hb])
    pf = Ring(ph, 4, [128, 512], F32, "hpf", psum=True)
    pnr = Ring(ph, 2, [128, 4], F32, "hpn", psum=True)
    nacc = ph.sb([128, HW // 128], F32, "hnacc")
    pnb = ph.buf("hnacc")
    k.op(k.dve, lambda e: e.memset(nacc[:], 0.0), writes=[pnb])
    fr = Ring(ph, 4, [128, 512], F32, "hfr")
    dr = Ring(ph, 2, [128, 512], F32, "hdr")
    ar = Ring(ph, 2, [128, 512], BF16, "har")
    orr = Ring(ph, 4, [128, 512], BF16, "hor")
    CB = min(512, HW)
    for n in range(NT):
        for cbk in range(HW // CB):
            fd = []
            for d in range(2):
                c0 = d * HW + cbk * CB
                p, pb = pf.next()
                k.op(k.pe, lambda e: e.matmul(p[:, 0:CB], lhsT=h2[:, n * 128:(n + 1) * 128], rhs=w3[:, c0:c0 + CB], start=True, stop=True),
                     reads=[hb, cb], writes=[pb])
                dc, dcb = dr.next()
                k.op(k.act, lambda e: e.activation(out=dc[:, 0:CB], in_=eld[:, c0:c0 + CB], func=AF.Exp, scale=ntn[:, n:n + 1]),
                     reads=[cb], writes=[dcb])
                f, fb = fr.next()
                k.op(k.dve, lambda e: e.tensor_tensor(out=f[:, 0:CB], in0=p[:, 0:CB], in1=b3[:, c0:c0 + CB], op=ALU.add), reads=[pb, cb], writes=[fb])
                k.op(k.dve, lambda e: e.tensor_tensor(out=f[:, 0:CB], in0=f[:, 0:CB], in1=dc[:, 0:CB], op=ALU.mult), reads=[fb, dcb], writes=[fb])
                fd.append((f, fb))
            (ff, ffb), (fw, fwb) = fd
            a1, a1b = dr.next()
            k.op(k.act, lambda e: e.activation(out=a1[:, 0:CB], in_=ff[:, 0:CB], func=AF.Abs), reads=[ffb], writes=[a1b])
            a2, a2b = dr.next()
            k.op(k.act, lambda e: e.activation(out=a2[:, 0:CB], in_=fw[:, 0:CB], func=AF.Abs), reads=[fwb], writes=[a2b])
            ab_, abb = ar.next()
            k.op(k.dve, lambda e: e.tensor_tensor(out=ab_[:, 0:CB], in0=a1[:, 0:CB], in1=a2[:, 0:CB], op=ALU.add), reads=[a1b, a2b], writes=[abb])
            pq, pqb = pnr.next()
            nj = CB // 128
            for j in range(nj):
                k.op(k.pe, lambda e: e.matmul(pq[:, j:j + 1], lhsT=ab_[:, j * 128:(j + 1) * 128], rhs=ws[:, n:n + 1],
                                              start=True, stop=True), reads=[abb, cb], writes=[pqb])
            k.op(k.dve, lambda e: e.tensor_tensor(out=nacc[:, cbk * nj:(cbk + 1) * nj], in0=nacc[:, cbk * nj:(cbk + 1) * nj],
                                                  in1=pq[:, 0:nj], op=ALU.add), reads=[pqb, pnb], writes=[pnb])
            o1, o1b = orr.next()
            k.op(k.dve, lambda e: e.scalar_tensor_tensor(out=o1[:, 0:CB], in0=fw[:, 0:CB], scalar=m0[:, n:n + 1], in1=ff[:, 0:CB],
                                                         op0=ALU.mult, op1=ALU.add), reads=[ffb, fwb, cb], writes=[o1b])
            o2, o2b = orr.next()
            k.op(k.dve, lambda e: e.scalar_tensor_tensor(out=o2[:, 0:CB], in0=fw[:, 0:CB], scalar=m0[:, n:n + 1], in1=ff[:, 0:CB],
                                                         op0=ALU.mult, op1=ALU.subtract), reads=[ffb, fwb, cb], writes=[o2b])
            k.dma(k.pool, A["hy_hs"][n * 128:(n + 1) * 128, cbk * CB:(cbk + 1) * CB], o1[:, 0:CB], o1b, False)
            k.dma(k.pool, A["hy_hd"][n * 128:(n + 1) * 128, cbk * CB:(cbk + 1) * CB], o2[:, 0:CB], o2b, False)
    rn = ph.sb([128, HW // 128], F32, "hrn")
    rnb = ph.buf("hrn")
    k.op(k.dve, lambda e: e.reciprocal(out=rn[:], in_=nacc[:]), reads=[pnb], writes=[rnb])
    k.dma(k.pool, A["hy_rn"], rn[:], rnb, False)
    ph.close()


def phase_hy_conv(k, C, A, Tn):
    HW = C["HW"]
    NT = Tn // 128
    Th = Tn // 2
    NC = HW // 128
    ph = Phase(k)
    cb = ph.buf("hcc")
    cw = ph.sb([128, 3 * NC, 3], F32, "hcw")
    k.dma(k.sync, cw[:], A["hy_cwT"].rearrange("p (c j) -> p c j", j=3), cb, True)
    cbias = ph.sb([128, 3 * NC], F32, "hcb")
    k.dma(k.sync, cbias[:], A["hy_cbT"], cb, True)
    flag = ph.sb([128, 1], F32, "hfl")
    k.dma(k.sync, flag[:], A["flag"], cb, True)
    cwf = ph.sb([128, 3 * NC, 3], F32, "hcwf")
    k.op(k.dve, lambda e: e.tensor_scalar_mul(out=cwf[:], in0=cw[:], scalar1=flag[:, 0:1]), reads=[cb], writes=[cb])
    zr = Ring(ph, 4, [128, Tn], BF16, "hz")
    zc = Ring(ph, 4, [128, Tn], F32, "hzc")
    ur = Ring(ph, 2, [128, Tn], BF16, "hu")
    xr = Ring(ph, 2, [128, Tn], BF16, "hx0")
    pt = Ring(ph, 2, [128, 4, 128], BF16, "hpt", psum=True)
    us = Ring(ph, 2, [128, NT, 128], BF16, "hus")
    for c in range(NC):
        outs = []
        for part in range(3):
            cc = part * NC + c
            z, zb = zr.next()
            k.dma(k.sync, z[:], A["zT"][cc * 128:(cc + 1) * 128, :], zb, True)
            o, ob = zc.next()
            w0, w1, w2 = cw[:, cc, 0:1], cw[:, cc, 1:2], cw[:, cc, 2:3]
            k.op(k.dve, lambda e: e.tensor_scalar(out=o[:], in0=z[:], scalar1=w1, scalar2=cbias[:, cc:cc + 1], op0=ALU.mult, op1=ALU.add),
                 reads=[zb, cb], writes=[ob])
            for a in (0, Th):
                k.op(k.dve, lambda e: e.scalar_tensor_tensor(out=o[:, a + 1:a + Th], in0=z[:, a:a + Th - 1], scalar=w0, in1=o[:, a + 1:a + Th],
                                                             op0=ALU.mult, op1=ALU.add), reads=[zb, cb, ob], writes=[ob])
                k.op(k.dve, lambda e: e.scalar_tensor_tensor(out=o[:, a:a + Th - 1], in0=z[:, a + 1:a + Th], scalar=w2, in1=o[:, a:a + Th - 1],
                                                             op0=ALU.mult, op1=ALU.add), reads=[zb, cb, ob], writes=[ob])
            k.op(k.dve, lambda e: e.scalar_tensor_tensor(out=o[:, Th:Th + 1], in0=z[:, Th - 1:Th], scalar=cwf[:, cc, 0:1], in1=o[:, Th:Th + 1],
                                                         op0=ALU.mult, op1=ALU.add), reads=[zb, cb, ob], writes=[ob])
            k.op(k.dve, lambda e: e.scalar_tensor_tensor(out=o[:, Th - 1:Th], in0=z[:, Th:Th + 1], scalar=cwf[:, cc, 2:3], in1=o[:, Th - 1:Th],
                                                         op0=ALU.mult, op1=ALU.add), reads=[zb, cb, ob], writes=[ob])
            outs.append((o, ob))
        (x0, x0b), (x1, x1b), (vv, vvb) = outs
        xo, xob = xr.next()
        k.op(k.act, lambda e: e.activation(out=xo[:], in_=x0[:], func=AF.Copy), reads=[x0b], writes=[xob])
        k.dma(k.pool, A["hy_x0T"][c * 128:(c + 1) * 128, :], xo[:], xob, False)
        u, ub = ur.next()
        k.op(k.dve, lambda e: e.tensor_tensor(out=u[:], in0=vv[:], in1=x1[:], op=ALU.mult), reads=[vvb, x1b], writes=[ub])
        k.dma(k.pool, A["hy_uT"][c * 128:(c + 1) * 128, :], u[:], ub, False)
        s, sb_ = us.next()
        for n4 in range(0, NT, 4):
            p, pb = pt.next()
            for j in range(min(4, NT - n4)):
                n = n4 + j
                k.op(k.pe, lambda e: e.transpose(out=p[:, j, :], in_=u[:, n * 128:(n + 1) * 128], identity=C["ident"][:]),
                     reads=[ub, C["identb"]], writes=[pb])
            k.op(k.act, lambda e: e.activation(out=s[:, n4:n4 + 4, :], in_=p[:], func=AF.Copy), reads=[pb], writes=[sb_])
        k.dma(k.pool, A["hy_u"].rearrange("(n p) c -> p n c", p=128)[:, :, c * 128:(c + 1) * 128], s[:], sb_, False)
    ph.close()


def phase_hy_mul(k, C, A, Tn):
    HW = C["HW"]
    ph = Phase(k)
    CB = min(512, HW)
    ir = Ring(ph, 8, [128, 512], BF16, "hmi")
    tr = Ring(ph, 4, [128, 512], F32, "hmt")
    orr = Ring(ph, 4, [128, 512], BF16, "hmo")
    for fi in range(Tn // 128):
        for cbk in range(HW // CB):
            cs = slice(cbk * CB, (cbk + 1) * CB)
            t = []
            for (src, r0) in ((A["hy_Uf"], fi * 128), (A["hy_Uf"], Tn + fi * 128), (A["hy_Tf"], fi * 128), (A["hy_Tf"], Tn + fi * 128)):
                x, xb = ir.next()
                k.dma(k.sync, x[:, 0:CB], src[r0:r0 + 128, cs], xb, True)
                t.append((x, xb))
            (ur_, urb), (ui, uib), (tr_, trb), (ti, tib) = t
            a, ab_ = tr.next()
            b, bb_ = tr.next()
            k.op(k.dve, lambda e: e.tensor_tensor(out=a[:, 0:CB], in0=ur_[:, 0:CB], in1=tr_[:, 0:CB], op=ALU.mult), reads=[urb, trb], writes=[ab_])
            k.op(k.dve, lambda e: e.tensor_tensor(out=b[:, 0:CB], in0=ui[:, 0:CB], in1=ti[:, 0:CB], op=ALU.mult), reads=[uib, tib], writes=[bb_])
            o1, o1b = orr.next()
            k.op(k.dve, lambda e: e.tensor_tensor(out=o1[:, 0:CB], in0=a[:, 0:CB], in1=b[:, 0:CB], op=ALU.subtract), reads=[ab_, bb_], writes=[o1b])
            a2, a2b = tr.next()
            b2, b2b = tr.next()
            k.op(k.dve, lambda e: e.tensor_tensor(out=a2[:, 0:CB], in0=ur_[:, 0:CB], in1=ti[:, 0:CB], op=ALU.mult), reads=[urb, tib], writes=[a2b])
            k.op(k.dve, lambda e: e.tensor_tensor(out=b2[:, 0:CB], in0=ui[:, 0:CB], in1=tr_[:, 0:CB], op=ALU.mult), reads=[uib, trb], writes=[b2b])
            o2, o2b = orr.next()
            k.op(k.dve, lambda e: e.tensor_tensor(out=o2[:, 0:CB], in0=a2[:, 0:CB], in1=b2[:, 0:CB], op=ALU.add), reads=[a2b, b2b], writes=[o2b])
            k.dma(k.pool, A["hy_Yf"][fi * 128:(fi + 1) * 128, cs], o1[:, 0:CB], o1b, False)
            k.dma(k.pool, A["hy_Yf"][Tn + fi * 128:Tn + (fi + 1) * 128, cs], o2[:, 0:CB], o2b, False)
    ph.close()


def epi_hy_final(k, C, A):
    HW, NAW = C["HW"], C["NAW"]

    def setup(ph):
        b = ph.buf("hfs")
        rn = ph.sb([128, HW // 128], F32, "hfrn")
        k.dma(k.sync, rn[:], A["hy_rn"], b, True)
        bi = ph.sb([128, HW // 128], F32, "hfbi")
        k.dma(k.sync, bi[:], A["hy_biasT"], b, True)
        return dict(b=b, rn=rn, bi=bi, u=Ring(ph, 3, [128, 512], BF16, "hfu"), x=Ring(ph, 3, [128, 512], BF16, "hfx"),
                    y=Ring(ph, 2, [128, 512], F32, "hfy"), o=Ring(ph, 3, [128, 512], BF16, "hfo"))

    def epi(ph, st, pts, n0, nsz, t0, tsz):
        p, pb = pts[0]
        c = t0 // 128
        u, ub = st["u"].next()
        k.dma(k.pool, u[:, 0:nsz], A["hy_uT"][t0:t0 + 128, n0:n0 + nsz], ub, True)
        x, xb = st["x"].next()
        k.dma(k.pool, x[:, 0:nsz], A["hy_x0T"][t0:t0 + 128, n0:n0 + nsz], xb, True)
        y, yb = st["y"].next()
        k.op(k.dve, lambda e: e.tensor_scalar_mul(out=y[:, 0:nsz], in0=p[:, 0:nsz], scalar1=st["rn"][:, c:c + 1]), reads=[pb, st["b"]], writes=[yb])
        k.op(k.dve, lambda e: e.scalar_tensor_tensor(out=y[:, 0:nsz], in0=u[:, 0:nsz], scalar=st["bi"][:, c:c + 1], in1=y[:, 0:nsz],
                                                     op0=ALU.mult, op1=ALU.add), reads=[ub, yb, st["b"]], writes=[yb])
        o, ob = st["o"].next()
        k.op(k.dve, lambda e: e.tensor_tensor(out=o[:, 0:nsz], in0=y[:, 0:nsz], in1=x[:, 0:nsz], op=ALU.mult), reads=[yb, xb], writes=[ob])
        k.dma(k.pool, A["mixT"][NAW + t0:NAW + t0 + 128, n0:n0 + nsz], o[:, 0:nsz], ob, False)
    return setup, epi


def hyena_all(k, C, A, Tn):
    HW = C["HW"]
    phase_hy_filter(k, C, A, Tn)
    phase_hy_conv(k, C, A, Tn)
    s, e = epi_store_fm(k, A["hy_Uf"])
    gemm(k, C, A["hy_u"], [A["hy_F"]], HW, "FM", e, epi_setup=s)
    s, e = epi_store_fm(k, A["hy_Tf"], row0=0)
    gemm(k, C, A["hy_hs"], [A["hy_C"]], HW, "FM", e, epi_setup=s)
    s, e = epi_store_fm(k, A["hy_Tf"], row0=Tn)
    gemm(k, C, A["hy_hd"], [A["hy_S"]], HW, "FM", e, epi_setup=s)
    phase_hy_mul(k, C, A, Tn)
    s, e = epi_hy_final(k, C, A)
    gemm(k, C, A["hy_Yf"], [A["hy_Wi"]], HW, "TM", e, epi_setup=s, TG=512, KBmax=32)


SCR_KIND = "Internal"
PHASES = None
DBG = {}
CFG = dict(D=4096, F=11008, XH=4, XW=512, NM=256, NAH=16, NAW=2048, HW=2048, G=256, T=4096)


def build_program(cfg, in_shapes):
    nc = bass.Bass("TRN2", target_bir_lowering=False)
    k = KB(nc)
    C = dict(cfg)
    D, F, XW, NM, NAW, HW, T = C["D"], C["F"], C["XW"], C["NM"], C["NAW"], C["HW"], C["T"]
    A = {}
    for name, (shape, dt) in in_shapes.items():
        A[name] = nc.dram_tensor(name, list(shape), BF16 if dt == "bf16" else F32, kind="ExternalInput").ap()
    A["y"] = nc.dram_tensor("y", [T, D], F32, kind="ExternalOutput").ap()

    def scr(name, shape, dt=BF16):
        A[name] = nc.dram_tensor("scr_" + name, list(shape), dt, kind=SCR_KIND).ap()

    scr("xnT", [D, T]); scr("mnT", [D, 2 * NM]); scr("cqT", [XW, T]); scr("ckT", [XW, 2 * NM]); scr("cv", [2 * NM, XW])
    scr("coT", [XW, T]); scr("hT", [F, T]); scr("qT", [NAW, T]); scr("kT", [NAW, T]); scr("v", [T, NAW]); scr("zT", [3 * HW, T])
    scr("mixT", [NAW + HW, T]); scr("s5T", [D, T])
    scr("hy_hs", [T, HW]); scr("hy_hd", [T, HW]); scr("hy_rn", [128, HW // 128], F32); scr("hy_x0T", [HW, T]); scr("hy_uT", [HW, T])
    scr("hy_u", [T, HW]); scr("hy_Uf", [2 * T, HW]); scr("hy_Tf", [2 * T, HW]); scr("hy_Yf", [2 * T, HW])
    setup_consts(k, C, A["ident"])
    for n in ["g_mix", "g_cross", "g_mem", "g_ffn", "x_wq", "x_wk", "x_wv", "x_wo", "x_q_gain", "x_k_gain", "w_ffn_gate", "w_ffn_up", "w_ffn_down"]:
        A[n] = [A[n][0], A[n][1]]
    w_in = A["w_in"]
    W = {}
    items = []

    def wb(name, src, dual=False):
        Kd, N = src.shape
        NB = pick_nb(N, dual)
        W[name] = nc.dram_tensor("wb_" + name, [N // NB, 128, Kd // 128, NB], BF16, kind="Internal").ap()
        items.append((src, W[name]))

    wb("q", w_in[:, 0:NAW]); wb("k", w_in[:, NAW:2 * NAW]); wb("v", w_in[:, 2 * NAW:3 * NAW]); wb("z", w_in[:, 3 * NAW:3 * NAW + 3 * HW])
    wb("w_out", A["w_out"])
    for l in range(2):
        wb("x_wq%d" % l, A["x_wq"][l]); wb("x_wk%d" % l, A["x_wk"][l]); wb("x_wv%d" % l, A["x_wv"][l]); wb("x_wo%d" % l, A["x_wo"][l])
        wb("gate%d" % l, A["w_ffn_gate"][l], True); wb("up%d" % l, A["w_ffn_up"][l], True); wb("down%d" % l, A["w_ffn_down"][l])
    wb("glu_v", A["w_glu"][:, 0:D], True); wb("glu_g", A["w_glu"][:, D:2 * D], True)
    A["Wb"] = W
    on = lambda n: (PHASES is None) or (n in PHASES)
    if on("cast"):
        phase_cast_weights(k, items)
    if on("l0proj"):
        phase_normT(k, C, A["x"], A["g_mix"][0], A["xnT"], T)
        s, e = epi_headnorm_fm(k, C, A["qT"], A["na_q_gain"])
        gemm(k, C, A["xnT"], [W["q"]], T, "FM", e, epi_setup=s)
        s, e = epi_headnorm_fm(k, C, A["kT"], A["na_k_gain"])
        gemm(k, C, A["xnT"], [W["k"]], T, "FM", e, epi_setup=s)
        s, e = epi_store_tm(k, A["v"])
        gemm(k, C, A["xnT"], [W["v"]], T, "TM", e, epi_setup=s)
        s, e = epi_store_fm(k, A["zT"])
        gemm(k, C, A["xnT"], [W["z"]], T, "FM", e, epi_setup=s)
    if on("na"):
        phase_na(k, C, A, T)
    if on("hy"):
        hyena_all(k, C, A, T)
    if on("wout"):
        s, e = epi_resid_tm(k, A["x"], A["y"])
        gemm(k, C, A["mixT"], [W["w_out"]], T, "TM", e, epi_setup=s)
    if on("tail0"):
        dense_tail(k, C, A, 0, T)
    if on("s5"):
        phase_normT(k, C, A["y"], A["g_mix"][1], A["xnT"], T)
        phase_s5(k, C, A, T)
    if on("glu"):
        s, e = epi_resid_tm(k, A["y"], A["y"], glu=True)
        gemm(k, C, A["s5T"], [W["glu_v"], W["glu_g"]], T, "TM", e, epi_setup=s)
    if on("tail1"):
        dense_tail(k, C, A, 1, T)
    k.barrier()
    return nc


def host_inputs(cfg, inputs, x_core, mem_core, kind):
    bf = ml_dtypes.bfloat16
    T, HW, G = cfg["T"], cfg["HW"], cfg["G"]
    f = lambda n: np.ascontiguousarray(np.asarray(inputs[n]))
    im = {"x": x_core, "mem": mem_core, "ident": np.eye(128).astype(bf),
          "flag": np.full((128, 1), 1.0 if kind == "p" else 0.0, np.float32)}
    for n in ["g_mix", "g_cross", "g_mem", "g_ffn", "x_wq", "x_wk", "x_wv", "x_wo", "x_q_gain", "x_k_gain",
              "w_ffn_gate", "w_ffn_up", "w_ffn_down"]:
        im[n] = f(n)
    for n in ["w_in", "na_q_gain", "na_k_gain", "w_out", "w_glu", "hy_f_w1", "hy_f_w2", "hy_f_w3", "hy_f_freq", "hy_f_b1", "hy_f_b2",
              "hy_f_b3", "hy_log_decay"]:
        im[n] = f(n)[0]
    cw = f("hy_conv_w")[0]
    im["hy_cwT"] = np.ascontiguousarray(cw.reshape(3, 3 * HW // 128, 128).transpose(2, 1, 0).reshape(128, -1))
    im["hy_cbT"] = np.ascontiguousarray(f("hy_conv_b")[0].reshape(-1, 128).T)
    im["hy_biasT"] = np.ascontiguousarray(f("hy_bias")[0].reshape(-1, 128).T)
    im.update(hy_host_consts(T, kind))
    im.update(na_host_tables(f("na_rpb")[0], T // 64, kind))
    im.update(s5_host_layout(inputs, G))
    return im


_CACHE = {}


def run_cfg(cfg, inputs, cores):
    base = {}
    ims = []
    for (x, m, kd) in cores:
        if kd not in base:
            base[kd] = host_inputs(cfg, inputs, x, m, kd)
        im = dict(base[kd])
        im["x"] = x
        im["mem"] = m
        ims.append(im)
    shapes = {n: (a.shape, "bf16" if a.dtype == ml_dtypes.bfloat16 else "f32") for n, a in ims[0].items()}
    nc = build_program(cfg, shapes)
    res = run_bass_kernel_spmd(nc, ims, core_ids=list(range(len(ims))))
    DBG["res"] = res.results
    return [r["y"] for r in res.results]


def kernel(**inputs):
    cfg = CFG
    T, D, NM = cfg["T"], cfg["D"], cfg["NM"]
    xp = np.asarray(inputs["x_prompt"])
    xs = np.asarray(inputs["x_sample"])
    mp = np.asarray(inputs["mem_prompt"])
    ms = np.asarray(inputs["mem_sample"])
    cores = []
    for b in range(2):
        cores.append((np.ascontiguousarray(xp[b]), np.ascontiguousarray(np.concatenate([mp[b], mp[b]], 0)), "p"))
    for j in range(4):
        cores.append((np.ascontiguousarray(xs[2 * j:2 * j + 2].reshape(T, D)), np.ascontiguousarray(ms[2 * j:2 * j + 2].reshape(2 * NM, D)), "s"))
    cores.append(cores[2])
    cores.append(cores[3])
    ys = run_cfg(cfg, inputs, cores)
    y_prompt = np.stack([ys[0], ys[1]], 0).astype(np.float32)
    y_sample = np.concatenate([ys[2 + j].reshape(2, T // 2, D) for j in range(4)], 0).astype(np.float32)
    return (y_prompt, y_sample)
```

```python
import math
from contextlib import ExitStack

import numpy as np
import ml_dtypes
import concourse.bass as bass
import concourse.mybir as mybir
from concourse.bass_utils import run_bass_kernel_spmd

F32 = mybir.dt.float32
BF16 = mybir.dt.bfloat16
AF = mybir.ActivationFunctionType
ALU = mybir.AluOpType
AX = mybir.AxisListType
EPS = 1e-6


class Tok:
    __slots__ = ("sem", "val")

    def __init__(self, sem, val):
        self.sem = sem
        self.val = val


class Buf:
    def __init__(self, name=""):
        self.name = name
        self.lw = None
        self.rd = {}
        self.ds = None
        self.ds2 = None


class Eng:
    def __init__(self, nc, name, e):
        self.name = name
        self.e = e
        self.sem = nc.alloc_semaphore("sem_" + name)
        self.cnt = 0
        self.seen = {}


class DSem:
    def __init__(self, nc, i):
        self.sem = nc.alloc_semaphore("dsem%d" % i)
        self.cnt = 0


class KB:
    def __init__(self, nc):
        self.nc = nc
        self.pe = Eng(nc, "pe", nc.tensor)
        self.act = Eng(nc, "act", nc.scalar)
        self.dve = Eng(nc, "dve", nc.vector)
        self.pool = Eng(nc, "pool", nc.gpsimd)
        self.sync = Eng(nc, "sync", nc.sync)
        self.engs = [self.pe, self.act, self.dve, self.pool, self.sync]
        self.dfree = []
        self.dfree2 = []
        self.dall = []
        self.pending = {}
        self.nds = 0

    def get_ds(self, store=False):
        fl = self.dfree2 if store else self.dfree
        if fl:
            return fl.pop()
        d = DSem(self.nc, self.nds)
        self.nds += 1
        self.dall.append(d)
        return d

    def _wait(self, E, tok):
        if tok is None:
            return
        if E is self.pe and tok.sem is E.sem:
            return
        key = id(tok.sem)
        if E.seen.get(key, 0) >= tok.val:
            return
        E.e.wait_ge(tok.sem, tok.val)
        E.seen[key] = tok.val

    def _deps(self, E, reads, writes):
        for b in reads:
            self._wait(E, b.lw)
        for b in writes:
            self._wait(E, b.lw)
            for t in b.rd.values():
                self._wait(E, t)

    def _commit(self, tok, reads, writes):
        for b in reads:
            key = id(tok.sem)
            o = b.rd.get(key)
            if o is None or o.val < tok.val:
                b.rd[key] = tok
        for b in writes:
            b.lw = tok
            b.rd = {}

    def op(self, E, fn, reads=(), writes=()):
        self._deps(E, reads, writes)
        ins = fn(E.e)
        E.cnt += 1
        ins.then_inc(E.sem, 1)
        tok = Tok(E.sem, E.cnt)
        self._commit(tok, reads, writes)
        return tok

    def op_w(self, E, fn, waits):
        for t in waits:
            self._wait(E, t)
        ins = fn(E.e)
        E.cnt += 1
        ins.then_inc(E.sem, 1)
        return Tok(E.sem, E.cnt)

    def dma(self, Q, out, in_, sb, load, extra_reads=(), extra_writes=(), **kw):
        reads = list(extra_reads) + ([] if load else [sb])
        writes = list(extra_writes) + ([sb] if load else [])
        self._deps(Q, reads, writes)
        if load:
            if sb.ds is None:
                sb.ds = self.get_ds()
            ds = sb.ds
        else:
            if sb.ds2 is None:
                sb.ds2 = self.get_ds(True)
            ds = sb.ds2
        ins = Q.e.dma_start(out=out, in_=in_, **kw)
        ds.cnt += 16
        ins.then_inc(ds.sem, 16)
        tok = Tok(ds.sem, ds.cnt)
        self.pending[id(ds.sem)] = tok
        self._commit(tok, reads, writes)
        return tok

    def release(self, bufs):
        for b in bufs:
            if b.ds is not None:
                self.dfree.append(b.ds)
                b.ds = None
            if b.ds2 is not None:
                self.dfree2.append(b.ds2)
                b.ds2 = None

    def barrier(self):
        toks = [Tok(E.sem, E.cnt) for E in self.engs if E.cnt > 0]
        toks += list(self.pending.values())
        for E in self.engs:
            for t in toks:
                if t.sem is not E.sem:
                    self._wait(E, t)
        self.pending = {}


class Phase:
    uid = 0

    def __init__(self, k):
        self.k = k
        self.es = ExitStack()
        self.bufs = []
        self.n = 0

    def sb(self, shape, dt, name=None):
        Phase.uid += 1
        t = self.es.enter_context(self.k.nc.sbuf_tensor("%s_%d" % (name or "sb", Phase.uid), list(shape), dt))
        return t

    def ps(self, shape, dt, name=None):
        Phase.uid += 1
        t = self.es.enter_context(self.k.nc.psum_tensor("%s_%d" % (name or "ps", Phase.uid), list(shape), dt))
        return t

    def buf(self, name=""):
        b = Buf(name)
        self.bufs.append(b)
        return b

    def close(self):
        self.k.barrier()
        self.k.release(self.bufs)
        self.es.close()


class Ring:
    def __init__(self, ph, n, shape, dt, name, psum=False):
        self.t = [(ph.ps if psum else ph.sb)(shape, dt, name) for _ in range(n)]
        self.b = [ph.buf(name + str(i)) for i in range(n)]
        self.i = 0
        self.n = n

    def next(self):
        j = self.i % self.n
        self.i += 1
        return self.t[j], self.b[j]


def phase_normT(k, C, src, g_ap, dstT, Tn):
    D = C["D"]
    DC = D // 128
    ph = Phase(k)
    gbc = ph.sb([128, D], F32, "gbc")
    gb = ph.buf("gbc")
    k.dma(k.sync, gbc[:], g_ap.partition_broadcast(128), gb, True)
    xr = Ring(ph, 2, [128, D], F32, "xt")
    xn = Ring(ph, 2, [128, D], BF16, "xn")
    junk = ph.sb([128, D], BF16, "junk")
    jb = ph.buf("junk")
    ssr = Ring(ph, 4, [128, 2], F32, "ss")
    st = Ring(ph, 2, [128, DC, 512], BF16, "stage")
    pt = Ring(ph, 4, [128, 8, 128], BF16, "pst", psum=True)
    TQ = min(512, Tn)
    nj = TQ // 128
    for tq in range(Tn // TQ):
        stg, stb = st.next()
        for j in range(nj):
            t0 = tq * TQ + j * 128
            xt, xb = xr.next()
            k.dma(k.sync, xt[:], src[t0:t0 + 128, :], xb, True)
            ss, sb_ = ssr.next()
            k.op(k.act, lambda e: e.activation(out=junk[:], in_=xt[:], func=AF.Square, accum_out=ss[:, 0:1]),
                 reads=[xb], writes=[jb, sb_])
            k.op(k.act, lambda e: e.activation(out=ss[:, 1:2], in_=ss[:, 0:1], func=AF.Sqrt, scale=1.0 / D, bias=C["eps"][:, 0:1]),
                 reads=[sb_, C["epsb"]], writes=[sb_])
            k.op(k.dve, lambda e: e.reciprocal(out=ss[:, 1:2], in_=ss[:, 1:2]), reads=[sb_], writes=[sb_])
            xnt, xnb = xn.next()
            k.op(k.dve, lambda e: e.scalar_tensor_tensor(out=xnt[:], in0=xt[:], scalar=ss[:, 1:2], in1=gbc[:],
                                                         op0=ALU.mult, op1=ALU.mult),
                 reads=[xb, sb_, gb], writes=[xnb])
            G8 = min(8, DC)
            for c8 in range(DC // G8):
                p, pb = pt.next()
                for c in range(G8):
                    cc = c8 * G8 + c
                    k.op(k.pe, lambda e: e.transpose(out=p[:, c, :], in_=xnt[:, cc * 128:(cc + 1) * 128],
                                                     identity=C["ident"][:]),
                         reads=[xnb, C["identb"]], writes=[pb])
                E = k.act if (c8 % 2 == 0) else k.dve
                if E is k.act:
                    k.op(E, lambda e: e.activation(out=stg[:, c8 * G8:(c8 + 1) * G8, j * 128:(j + 1) * 128],
                                                   in_=p[:, 0:G8, :], func=AF.Copy), reads=[pb], writes=[stb])
                else:
                    k.op(E, lambda e: e.tensor_copy(out=stg[:, c8 * G8:(c8 + 1) * G8, j * 128:(j + 1) * 128],
                                                    in_=p[:, 0:G8, :]), reads=[pb], writes=[stb])
        k.dma(k.pool, dstT.rearrange("(c p) t -> p c t", p=128)[:, :, tq * TQ:(tq + 1) * TQ], stg[:, :, 0:TQ],
              stb, False)
    ph.close()


def pick_nb(N, dual=False):
    nb = 256 if dual else 512
    nb = min(nb, N)
    while N % nb != 0:
        nb -= 128
    return nb


def blk_host(W, NB):
    Kd, N = W.shape
    return np.ascontiguousarray(W.reshape(Kd // 128, 128, N // NB, NB).transpose(2, 1, 0, 3))


def phase_cast_weights(k, items):
    ph = Phase(k)
    sr = Ring(ph, 3, [128, 2048], F32, "cws")
    br = Ring(ph, 3, [128, 2048], BF16, "cwb")
    ci = 0
    for (src, dst) in items:
        NBLK, _, KC, NB = dst.shape
        N = NBLK * NB
        CW = (2048 // NB) * NB
        for kc in range(KC):
            for c0 in range(0, N, CW):
                w = min(CW, N - c0)
                st_, sb_ = sr.next()
                k.dma(k.sync, st_[:, 0:w], src[kc * 128:(kc + 1) * 128, c0:c0 + w], sb_, True)
                bt, bb = br.next()
                if ci % 3 == 2:
                    k.op(k.act, lambda e: e.activation(out=bt[:, 0:w], in_=st_[:, 0:w], func=AF.Copy), reads=[sb_], writes=[bb])
                else:
                    k.op(k.dve, lambda e: e.tensor_copy(out=bt[:, 0:w], in_=st_[:, 0:w]), reads=[sb_], writes=[bb])
                ci += 1
                k.dma(k.pool, dst[c0 // NB:(c0 + w) // NB, :, kc, :].rearrange("b p n -> p b n"),
                      bt[:, 0:w].rearrange("p (b n) -> p b n", n=NB), bb, False)
    ph.close()


def gemm(k, C, actT, ws, Tn, mode, epi, TG=1024, KBmax=32, epi_setup=None):
    Kd = actT.shape[0]
    NBLK, _, KC, NB = ws[0].shape
    assert Kd == KC * 128
    nw = len(ws)
    TG = min(TG, Tn)
    nkb = (KC + KBmax - 1) // KBmax
    KBs = [KC // nkb + (1 if i < KC % nkb else 0) for i in range(nkb)]
    kb0 = [sum(KBs[:i]) for i in range(nkb)]
    KBm = max(KBs)
    ph = Phase(k)
    act = Ring(ph, 1 if KC * TG * 2 > 40000 else 2, [128, KC, TG], BF16, "act")
    wb = Ring(ph, 2 if nw * KBm * NB * 2 > 40000 else 3, [128, nw, KBm, NB], BF16, "wb")
    psr = [Ring(ph, 4 if nw == 2 else 6, [128, 512], F32, "ps%d" % i, psum=True) for i in range(nw)]
    est = epi_setup(ph) if epi_setup else None
    for tg in range(Tn // TG):
        at, ab = act.next()
        for kc in range(KC):
            k.dma(k.sync, at[:, kc, :], actT[kc * 128:(kc + 1) * 128, tg * TG:(tg + 1) * TG], ab, True)
        for nb in range(NBLK):
            if mode == "FM":
                tiles = [(m, tb) for tb in range(TG // 512 if TG >= 512 else 1) for m in range(NB // 128)]
            else:
                tiles = [(0, tt) for tt in range(TG // 128)]
            maxt = psr[0].n
            if nkb > 1:
                assert len(tiles) <= maxt
            for s0 in range(0, len(tiles), maxt):
                sub = tiles[s0:s0 + maxt]
                pts = [[psr[i].next() for i in range(nw)] for _ in sub]
                for kbi in range(nkb):
                    if s0 == 0 or nkb > 1:
                        wt, wbuf = wb.next()
                        for wi in range(nw):
                            for q0 in range(0, KBs[kbi], 8):
                                q1 = min(q0 + 8, KBs[kbi])
                                k.dma(k.sync, wt[:, wi, q0:q1, :], ws[wi][nb, :, kb0[kbi] + q0:kb0[kbi] + q1, :], wbuf, True)
                    for ti, (m, tb) in enumerate(sub):
                        for wi in range(nw):
                            p, pb = pts[ti][wi]
                            for kk in range(KBs[kbi]):
                                kc = kb0[kbi] + kk
                                first = (kbi == 0 and kk == 0)
                                last = (kbi == nkb - 1 and kk == KBs[kbi] - 1)
                                if mode == "FM":
                                    tsz = min(512, TG)
                                    k.op(k.pe, lambda e: e.matmul(p[:, 0:tsz], lhsT=wt[:, wi, kk, m * 128:(m + 1) * 128],
                                                                  rhs=at[:, kc, tb * 512:tb * 512 + tsz],
                                                                  start=first, stop=last),
                                         reads=[wbuf, ab], writes=[pb])
                                else:
                                    k.op(k.pe, lambda e: e.matmul(p[:, 0:NB], lhsT=at[:, kc, tb * 128:(tb + 1) * 128],
                                                                  rhs=wt[:, wi, kk, :], start=first, stop=last),
                                         reads=[wbuf, ab], writes=[pb])
                for ti, (m, tb) in enumerate(sub):
                    if mode == "FM":
                        epi(ph, est, pts[ti], nb * NB + m * 128, 128, tg * TG + tb * 512, min(512, TG))
                    else:
                        epi(ph, est, pts[ti], nb * NB, NB, tg * TG + tb * 128, 128)
    ph.close()


def epi_store_fm(k, dstT, row0=0):
    def setup(ph):
        return Ring(ph, 3, [128, 512], BF16, "eo")

    def epi(ph, st, pts, n0, nsz, t0, tsz):
        p, pb = pts[0]
        o, ob = st.next()
        k.op(k.act, lambda e: e.activation(out=o[:, 0:tsz], in_=p[:, 0:tsz], func=AF.Copy), reads=[pb], writes=[ob])
        k.dma(k.pool, dstT[row0 + n0:row0 + n0 + 128, t0:t0 + tsz], o[:, 0:tsz], ob, False)
    return setup, epi


def epi_store_tm(k, dst, col0=0):
    def setup(ph):
        return Ring(ph, 3, [128, 512], BF16, "eo")

    def epi(ph, st, pts, n0, nsz, t0, tsz):
        p, pb = pts[0]
        o, ob = st.next()
        k.op(k.act, lambda e: e.activation(out=o[:, 0:nsz], in_=p[:, 0:nsz], func=AF.Copy), reads=[pb], writes=[ob])
        k.dma(k.pool, dst[t0:t0 + 128, col0 + n0:col0 + n0 + nsz], o[:, 0:nsz], ob, False)
    return setup, epi


def epi_headnorm_fm(k, C, dstT, gain_ap, row0=0):
    def setup(ph):
        g = ph.sb([128, 2], F32, "hg")
        gb = ph.buf("hg")
        k.dma(k.sync, g[:, 0:1], gain_ap.rearrange("(p o) -> p o", o=1), gb, True)
        k.op(k.dve, lambda e: e.tensor_copy(out=g[:, 1:2], in_=g[:, 0:1]), reads=[gb], writes=[gb])
        return dict(g=g, gb=gb, sq=Ring(ph, 2, [128, 512], BF16, "sq"), ps=Ring(ph, 2, [128, 512], F32, "pss", psum=True),
                    rs=Ring(ph, 2, [128, 512], F32, "rs"), o=Ring(ph, 3, [128, 512], BF16, "eo"))

    def epi(ph, st, pts, n0, nsz, t0, tsz):
        p, pb = pts[0]
        sq, sqb = st["sq"].next()
        k.op(k.act, lambda e: e.activation(out=sq[:, 0:tsz], in_=p[:, 0:tsz], func=AF.Square), reads=[pb], writes=[sqb])
        ps2, ps2b = st["ps"].next()
        k.op(k.pe, lambda e: e.matmul(ps2[:, 0:tsz], lhsT=C["ones"][:], rhs=sq[:, 0:tsz], start=True, stop=True),
             reads=[sqb, C["onesb"]], writes=[ps2b])
        rs, rsb = st["rs"].next()
        k.op(k.act, lambda e: e.activation(out=rs[:, 0:tsz], in_=ps2[:, 0:tsz], func=AF.Sqrt, scale=1.0 / 128.0,
                                           bias=C["eps"][:, 0:1]), reads=[ps2b, C["epsb"]], writes=[rsb])
        k.op(k.dve, lambda e: e.reciprocal(out=rs[:, 0:tsz], in_=rs[:, 0:tsz]), reads=[rsb], writes=[rsb])
        o, ob = st["o"].next()
        k.op(k.dve, lambda e: e.scalar_tensor_tensor(out=o[:, 0:tsz], in0=p[:, 0:tsz], scalar=st["g"][:, 1:2],
                                                     in1=rs[:, 0:tsz], op0=ALU.mult, op1=ALU.mult),
             reads=[pb, rsb, st["gb"]], writes=[ob])
        k.dma(k.pool, dstT[row0 + n0:row0 + n0 + 128, t0:t0 + tsz], o[:, 0:tsz], ob, False)
    return setup, epi


def epi_resid_tm(k, src, dst, glu=False):
    def setup(ph):
        return dict(r=Ring(ph, 3, [128, 512], F32, "res"), sg=Ring(ph, 2, [128, 512], F32, "sg"))

    def epi(ph, st, pts, n0, nsz, t0, tsz):
        r, rb = st["r"].next()
        k.dma(k.pool, r[:, 0:nsz], src[t0:t0 + 128, n0:n0 + nsz], rb, True)
        p, pb = pts[0]
        if glu:
            p2, pb2 = pts[1]
            sg, sgb = st["sg"].next()
            k.op(k.act, lambda e: e.activation(out=sg[:, 0:nsz], in_=p2[:, 0:nsz], func=AF.Sigmoid),
                 reads=[pb2], writes=[sgb])
            k.op(k.dve, lambda e: e.tensor_tensor(out=sg[:, 0:nsz], in0=sg[:, 0:nsz], in1=p[:, 0:nsz], op=ALU.mult),
                 reads=[pb, sgb], writes=[sgb])
            k.op(k.dve, lambda e: e.tensor_tensor(out=r[:, 0:nsz], in0=r[:, 0:nsz], in1=sg[:, 0:nsz], op=ALU.add),
                 reads=[rb, sgb], writes=[rb])
        else:
            k.op(k.dve, lambda e: e.tensor_tensor(out=r[:, 0:nsz], in0=r[:, 0:nsz], in1=p[:, 0:nsz], op=ALU.add),
                 reads=[rb, pb], writes=[rb])
        k.dma(k.pool, dst[t0:t0 + 128, n0:n0 + nsz], r[:, 0:nsz], rb, False)
    return setup, epi


def epi_swiglu_fm(k, dstT):
    def setup(ph):
        return dict(s=Ring(ph, 2, [128, 512], F32, "sl"), o=Ring(ph, 3, [128, 512], BF16, "eo"))

    def epi(ph, st, pts, n0, nsz, t0, tsz):
        pg, pgb = pts[0]
        pu, pub = pts[1]
        s, sb_ = st["s"].next()
        k.op(k.act, lambda e: e.activation(out=s[:, 0:tsz], in_=pg[:, 0:tsz], func=AF.Silu), reads=[pgb], writes=[sb_])
        o, ob = st["o"].next()
        k.op(k.dve, lambda e: e.tensor_tensor(out=o[:, 0:tsz], in0=s[:, 0:tsz], in1=pu[:, 0:tsz], op=ALU.mult),
             reads=[sb_, pub], writes=[ob])
        k.dma(k.pool, dstT[n0:n0 + 128, t0:t0 + tsz], o[:, 0:tsz], ob, False)
    return setup, epi


def phase_cross_attn(k, C, cqT, ckT, cv, coT, Tn):
    XH, NM = C["XH"], C["NM"]
    Th = Tn // 2
    KCH = NM // 128
    ph = Phase(k)
    kt = ph.sb([128, XH, 2 * NM], BF16, "ckt")
    ktb = ph.buf("ckt")
    for h in range(XH):
        k.dma(k.sync, kt[:, h, :], ckT[h * 128:(h + 1) * 128, :], ktb, True)
    vt = ph.sb([128, 2 * KCH, XH * 128], BF16, "cvt")
    vtb = ph.buf("cvt")
    k.dma(k.sync, vt[:], cv.rearrange("(c p) n -> p c n", p=128), vtb, True)
    qr = Ring(ph, 2, [128, 512], BF16, "cq")
    pss = Ring(ph, 2, [128, KCH, 512], F32, "pss", psum=True)
    pso = Ring(ph, 2, [128, 512], F32, "pso", psum=True)
    psd = Ring(ph, 2, [128, 512], F32, "psd", psum=True)
    pr = Ring(ph, 2, [128, KCH, 512], BF16, "pT")
    rd = Ring(ph, 2, [128, 512], F32, "rden")
    orr = Ring(ph, 3, [128, 512], BF16, "co")
    scale = 1.0 / math.sqrt(128.0)
    TB = min(512, Th)
    for h in range(XH):
        for hs in range(2):
            for tb in range(Th // TB):
                t0 = hs * Th + tb * TB
                q, qb = qr.next()
                k.dma(k.sync, q[:, 0:TB], cqT[h * 128:(h + 1) * 128, t0:t0 + TB], qb, True)
                s, sb_ = pss.next()
                for kc in range(KCH):
                    k.op(k.pe, lambda e: e.matmul(s[:, kc, 0:TB], lhsT=kt[:, h, hs * NM + kc * 128:hs * NM + (kc + 1) * 128],
                                                  rhs=q[:, 0:TB], start=True, stop=True), reads=[ktb, qb], writes=[sb_])
                p, pb = pr.next()
                k.op(k.act, lambda e: e.activation(out=p[:, :, 0:TB], in_=s[:, :, 0:TB], func=AF.Exp, scale=scale),
                     reads=[sb_], writes=[pb])
                o, ob = pso.next()
                d, db = psd.next()
                for kc in range(KCH):
                    k.op(k.pe, lambda e: e.matmul(o[:, 0:TB], lhsT=vt[:, hs * KCH + kc, h * 128:(h + 1) * 128],
                                                  rhs=p[:, kc, 0:TB], start=(kc == 0), stop=(kc == KCH - 1)),
                         reads=[vtb, pb], writes=[ob])
                for kc in range(KCH):
                    k.op(k.pe, lambda e: e.matmul(d[:, 0:TB], lhsT=C["ones"][:], rhs=p[:, kc, 0:TB],
                                                  start=(kc == 0), stop=(kc == KCH - 1)),
                         reads=[C["onesb"], pb], writes=[db])
                r, rb = rd.next()
                k.op(k.dve, lambda e: e.reciprocal(out=r[:, 0:TB], in_=d[:, 0:TB]), reads=[db], writes=[rb])
                oo, oob = orr.next()
                k.op(k.dve, lambda e: e.tensor_tensor(out=oo[:, 0:TB], in0=o[:, 0:TB], in1=r[:, 0:TB], op=ALU.mult),
                     reads=[ob, rb], writes=[oob])
                k.dma(k.pool, coT[h * 128:(h + 1) * 128, t0:t0 + TB], oo[:, 0:TB], oob, False)
    ph.close()


def setup_consts(k, C, ident_ap):
    nc = k.nc
    C["es"] = ExitStack()
    C["ident"] = C["es"].enter_context(nc.sbuf_tensor("ident_sb", [128, 128], BF16))
    C["identb"] = Buf("ident")
    k.dma(k.sync, C["ident"][:], ident_ap, C["identb"], True)
    C["ones"] = C["es"].enter_context(nc.sbuf_tensor("ones_sb", [128, 128], BF16))
    C["onesb"] = Buf("ones")
    k.op(k.dve, lambda e: e.memset(C["ones"][:], 1.0), writes=[C["onesb"]])
    C["eps"] = C["es"].enter_context(nc.sbuf_tensor("eps_sb", [128, 2], F32))
    C["epsb"] = Buf("eps")
    k.op(k.dve, lambda e: e.memset(C["eps"][:], EPS), writes=[C["epsb"]])


def dense_tail(k, C, A, layer, Tn):
    D, F, XW, NM = C["D"], C["F"], C["XW"], C["NM"]
    y = A["y"]
    W = A["Wb"]
    phase_normT(k, C, y, A["g_cross"][layer], A["xnT"], Tn)
    phase_normT(k, C, A["mem"], A["g_mem"][layer], A["mnT"], 2 * NM)
    s, e = epi_headnorm_fm(k, C, A["cqT"], A["x_q_gain"][layer])
    gemm(k, C, A["xnT"], [W["x_wq%d" % layer]], Tn, "FM", e, epi_setup=s)
    s, e = epi_headnorm_fm(k, C, A["ckT"], A["x_k_gain"][layer])
    gemm(k, C, A["mnT"], [W["x_wk%d" % layer]], 2 * NM, "FM", e, epi_setup=s)
    s, e = epi_store_tm(k, A["cv"])
    gemm(k, C, A["mnT"], [W["x_wv%d" % layer]], 2 * NM, "TM", e, epi_setup=s)
    phase_cross_attn(k, C, A["cqT"], A["ckT"], A["cv"], A["coT"], Tn)
    s, e = epi_resid_tm(k, y, y)
    gemm(k, C, A["coT"], [W["x_wo%d" % layer]], Tn, "TM", e, epi_setup=s)
    phase_normT(k, C, y, A["g_ffn"][layer], A["xnT"], Tn)
    s, e = epi_swiglu_fm(k, A["hT"])
    gemm(k, C, A["xnT"], [W["gate%d" % layer], W["up%d" % layer]], Tn, "FM", e, epi_setup=s)
    s, e = epi_resid_tm(k, y, y)
    gemm(k, C, A["hT"], [W["down%d" % layer]], Tn, "TM", e, epi_setup=s, TG=512, KBmax=22)


def cmul_acc(k, E, dre, dim, sre, sim, pr, pi, npi, bufs):
    R, W = bufs
    k.op(E, lambda e: e.scalar_tensor_tensor(out=dre, in0=sre, scalar=pr, in1=dre, op0=ALU.mult, op1=ALU.add), reads=R, writes=W)
    k.op(E, lambda e: e.scalar_tensor_tensor(out=dre, in0=sim, scalar=npi, in1=dre, op0=ALU.mult, op1=ALU.add), reads=R, writes=W)
    k.op(E, lambda e: e.scalar_tensor_tensor(out=dim, in0=sim, scalar=pr, in1=dim, op0=ALU.mult, op1=ALU.add), reads=R, writes=W)
    k.op(E, lambda e: e.scalar_tensor_tensor(out=dim, in0=sre, scalar=pi, in1=dim, op0=ALU.mult, op1=ALU.add), reads=R, writes=W)


def scan_seq(k, E, h, Tn, Th, rev, PW, FL, slab, bufs, flag_ap, sv):
    hre, him = h[:, 0, :], h[:, 1, :]
    nl = Tn.bit_length() - 1
    assert (1 << nl) == Tn
    R, Wr = bufs
    k._deps(E, R, Wr)

    def sel(hh, l, i, k0=None, k1=None):
        st = 1 << l
        v = hh.rearrange("p (k i s) -> p k i s", i=2, s=st)
        ii = (1 - i) if rev else i
        ss = 0 if rev else st - 1
        if k0 is None:
            return v[:, :, ii, ss]
        return v[:, k0:k1, ii, ss]

    def sel2(l, i, k0=None, k1=None):
        st = 1 << l
        v = h[:, :, :].rearrange("p c (k i s) -> p c k i s", i=2, s=st)
        ii = (1 - i) if rev else i
        ss = 0 if rev else st - 1
        if k0 is None:
            return v[:, :, :, ii, ss]
        return v[:, :, k0:k1, ii, ss]

    def cm(l, kd, ks, sc, prev):
        pr, pi, npi = sc
        STT = lambda o, i0, sca, i1: (lambda e: e.scalar_tensor_tensor(out=o, in0=i0, scalar=sca, in1=i1, op0=ALU.mult, op1=ALU.add))
        d2, s2 = sel2(l, *kd), sel2(l, *ks)
        dre, dim = sel(hre, l, *kd), sel(him, l, *kd)
        sre, sim = sel(hre, l, *ks), sel(him, l, *ks)
        a = k.op_w(E, STT(d2, s2, pr, d2), prev)
        c = k.op_w(E, STT(dre, sim, npi, dre), [a])
        d = k.op_w(E, STT(dim, sre, pi, dim), [a])
        return [c, d]

    def scal(t):
        return t[0][:, slab:slab + 1], t[1][:, slab:slab + 1], t[2][:, slab:slab + 1]

    prev = []
    for l in range(nl):
        sc = scal(FL if (1 << l) == Th else PW[l])
        prev = cm(l, (1,), (0,), sc, prev)
    pos = Th if rev else Th - 1
    t1 = k.op_w(E, lambda e: e.tensor_copy(out=sv[:, :], in_=h[:, :, pos]), prev)
    t2 = k.op_w(E, lambda e: e.tensor_scalar_mul(out=h[:, :, pos], in0=h[:, :, pos], scalar1=flag_ap), [t1])
    prev = [t2]
    for l in range(nl - 2, -1, -1):
        K = Tn >> (l + 1)
        sc = scal(PW[l])
        if rev:
            prev = cm(l, (0, 0, K - 1), (1, 1, K), sc, prev)
        else:
            prev = cm(l, (0, 1, K), (1, 0, K - 1), sc, prev)
    t3 = k.op_w(E, lambda e: e.tensor_copy(out=h[:, :, pos], in_=sv[:, :]), prev)
    k._commit(t3, R, Wr)


def phase_s5(k, C, A, Tn):
    D, G = C["D"], C["G"]
    DC = D // 128
    NS = G
    Th = Tn // 2
    ph = Phase(k)
    names = ["lr", "li", "ls", "st", "a", "th", "mag", "r1", "r2", "sn", "cs", "lbr", "lbi", "den", "t1", "t2", "cr", "ci", "nci"]
    P = {n: ph.sb([128, NS], F32, "s5" + n) for n in names}
    pb = ph.buf("s5par")
    cst = ph.sb([128, 4], F32, "s5c")
    k.op(k.dve, lambda e: e.memset(cst[:, 0:1], -math.pi), writes=[pb])
    k.dma(k.sync, P["lr"][:], A["s5_lr"], pb, True)
    k.dma(k.sync, P["li"][:], A["s5_li"], pb, True)
    k.dma(k.sync, P["ls"][:], A["s5_ls"], pb, True)
    flag = ph.sb([128, 1], F32, "flag")
    k.dma(k.sync, flag[:], A["flag"], pb, True)
    V = k.dve
    RW = dict(reads=[pb], writes=[pb])

    def tt(o, a_, b_, op):
        k.op(V, lambda e: e.tensor_tensor(out=P[o][:], in0=P[a_][:], in1=P[b_][:], op=op), **RW)

    def ts(o, a_, s1, s2, o0, o1):
        k.op(V, lambda e: e.tensor_scalar(out=P[o][:], in0=P[a_][:], scalar1=s1, scalar2=s2, op0=o0, op1=o1), **RW)

    k.op(k.act, lambda e: e.activation(out=P["st"][:], in_=P["ls"][:], func=AF.Exp), **RW)
    k.op(V, lambda e: e.tensor_scalar_min(out=P["lr"][:], in0=P["lr"][:], scalar1=-1e-4), **RW)
    tt("a", "lr", "st", ALU.mult)
    tt("th", "li", "st", ALU.mult)
    k.op(k.act, lambda e: e.activation(out=P["mag"][:], in_=P["a"][:], func=AF.Exp), **RW)
    MAGIC = 12582912.0
    for (o, sh) in (("sn", 0.0), ("cs", 0.5 * math.pi)):
        ts("r1", "th", sh, 1.0, ALU.add, ALU.mult)
        ts("r2", "r1", 1.0 / (2 * math.pi), MAGIC, ALU.mult, ALU.add)
        ts("r2", "r2", -MAGIC, -2 * math.pi, ALU.add, ALU.mult)
        tt("r1", "r1", "r2", ALU.add)
        ts("r1", "r1", math.pi, -math.pi, ALU.min, ALU.max)
        k.op(k.act, lambda e: e.activation(out=P[o][:], in_=P["r1"][:], func=AF.Sin), **RW)
    tt("lbr", "mag", "cs", ALU.mult)
    tt("lbi", "mag", "sn", ALU.mult)
    tt("den", "lr", "lr", ALU.mult)
    tt("t1", "li", "li", ALU.mult)
    tt("den", "den", "t1", ALU.add)
    k.op(V, lambda e: e.reciprocal(out=P["den"][:], in_=P["den"][:]), **RW)
    ts("t1", "lbr", -1.0, 1.0, ALU.add, ALU.mult)
    tt("cr", "t1", "lr", ALU.mult)
    tt("t2", "lbi", "li", ALU.mult)
    tt("cr", "cr", "t2", ALU.add)
    tt("cr", "cr", "den", ALU.mult)
    tt("ci", "lbi", "lr", ALU.mult)
    tt("t2", "t1", "li", ALU.mult)
    tt("ci", "ci", "t2", ALU.subtract)
    tt("ci", "ci", "den", ALU.mult)
    ts("nci", "ci", -1.0, 1.0, ALU.mult, ALU.mult)
    nl = Tn.bit_length() - 1
    PW = []
    cur = (P["lbr"], P["lbi"])
    for l in range(nl):
        if l > 0:
            pr_ = ph.sb([128, NS], F32, "pwr")
            pi_ = ph.sb([128, NS], F32, "pwi")
            t_ = P["t1"]
            k.op(V, lambda e: e.tensor_tensor(out=pr_[:], in0=cur[0][:], in1=cur[0][:], op=ALU.mult), **RW)
            k.op(V, lambda e: e.tensor_tensor(out=t_[:], in0=cur[1][:], in1=cur[1][:], op=ALU.mult), **RW)
            k.op(V, lambda e: e.tensor_tensor(out=pr_[:], in0=pr_[:], in1=t_[:], op=ALU.subtract), **RW)
            k.op(V, lambda e: e.tensor_tensor(out=pi_[:], in0=cur[0][:], in1=cur[1][:], op=ALU.mult), **RW)
            k.op(V, lambda e: e.tensor_scalar_mul(out=pi_[:], in0=pi_[:], scalar1=2.0), **RW)
            cur = (pr_, pi_)
        npi_ = ph.sb([128, NS], F32, "pwn")
        k.op(V, lambda e: e.tensor_scalar_mul(out=npi_[:], in0=cur[1][:], scalar1=-1.0), **RW)
        PW.append((cur[0], cur[1], npi_))
    ltop = Th.bit_length() - 1
    flr = ph.sb([128, NS], F32, "flr")
    fli = ph.sb([128, NS], F32, "fli")
    nfli = ph.sb([128, NS], F32, "nfli")
    k.op(V, lambda e: e.tensor_scalar_mul(out=flr[:], in0=PW[ltop][0][:], scalar1=flag[:, 0:1]), **RW)
    k.op(V, lambda e: e.tensor_scalar_mul(out=fli[:], in0=PW[ltop][1][:], scalar1=flag[:, 0:1]), **RW)
    k.op(V, lambda e: e.tensor_scalar_mul(out=nfli[:], in0=fli[:], scalar1=-1.0), **RW)
    FL = (flr, fli, nfli)
    sv = ph.sb([128, 2], F32, "s5sv")
    dsk = ph.sb([128, DC], F32, "dsk")
    k.dma(k.sync, dsk[:], A["s5_dT"], pb, True)

    xr = Ring(ph, 1, [128, Tn], BF16, "s5x")
    hR = Ring(ph, 1, [128, 2, Tn], F32, "s5h")
    hB = Ring(ph, 1, [128, 2, Tn], BF16, "s5hb")
    bst = Ring(ph, 2, [128, 2, 128], F32, "s5bs")
    bbf = Ring(ph, 2, [128, 2, 128], BF16, "s5bb")
    cstg = Ring(ph, 2, [128, 2, 128], F32, "s5cs")
    cbf = Ring(ph, 2, [128, 2, 128], BF16, "s5cb")
    ctmp = Ring(ph, 2, [128, 2, 128], F32, "s5ct")
    yacc = Ring(ph, 1, [128, Tn], F32, "s5y")
    yo = Ring(ph, 1, [128, Tn], BF16, "s5yo")
    gtmp = Ring(ph, 2, [128, 512], F32, "s5g")
    psb = Ring(ph, 4, [128, 512], F32, "s5pb", psum=True)
    psy = Ring(ph, 3, [128, 512], F32, "s5py", psum=True)
    NTB = Tn // 512 if Tn >= 512 else 1
    TB = min(512, Tn)
    si = 0
    for oc in range(DC):
        x, xb = xr.next()
        k.dma(k.sync, x[:], A["xnT"][oc * 128:(oc + 1) * 128, :], xb, True)
        ya, yab = yacc.next()
        k.op(k.dve, lambda e: e.tensor_scalar_mul(out=ya[:], in0=x[:], scalar1=dsk[:, oc:oc + 1]), reads=[xb, pb], writes=[yab])
        for d in range(2):
            for gq in range(4):
                gp = oc * 4 + gq
                slab = d * (G // 2) + gp
                E = k.dve
                si += 1
                bs, bsb = bst.next()
                k.dma(k.sync, bs[:, 0, :], A["s5_Bre"][:, slab * 128:(slab + 1) * 128], bsb, True)
                k.dma(k.sync, bs[:, 1, :], A["s5_Bim"][:, slab * 128:(slab + 1) * 128], bsb, True)
                bb, bbb = bbf.next()
                k.op(k.act, lambda e: e.activation(out=bb[:], in_=bs[:], func=AF.Copy), reads=[bsb], writes=[bbb])
                cs_, csb = cstg.next()
                k.dma(k.sync, cs_[:, 0, :], A["s5_Cre"][:, slab * 128:(slab + 1) * 128], csb, True)
                k.dma(k.sync, cs_[:, 1, :], A["s5_Cim"][:, slab * 128:(slab + 1) * 128], csb, True)
                ct, ctb = ctmp.next()
                cb_, cbb = cbf.next()
                crs, cis, ncis = P["cr"][:, slab:slab + 1], P["ci"][:, slab:slab + 1], P["nci"][:, slab:slab + 1]
                k.op(E, lambda e: e.tensor_scalar_mul(out=ct[:, 0, :], in0=cs_[:, 0, :], scalar1=crs), reads=[csb, pb], writes=[ctb])
                k.op(E, lambda e: e.scalar_tensor_tensor(out=cb_[:, 0, :], in0=cs_[:, 1, :], scalar=ncis, in1=ct[:, 0, :],
                                                         op0=ALU.mult, op1=ALU.add), reads=[csb, ctb, pb], writes=[cbb])
                k.op(E, lambda e: e.tensor_scalar_mul(out=ct[:, 1, :], in0=cs_[:, 0, :], scalar1=ncis), reads=[csb, pb], writes=[ctb])
                k.op(E, lambda e: e.tensor_scalar(out=ct[:, 0, :], in0=cs_[:, 1, :], scalar1=crs, scalar2=-1.0, op0=ALU.mult, op1=ALU.mult),
                     reads=[csb, pb], writes=[ctb])
                k.op(E, lambda e: e.tensor_tensor(out=cb_[:, 1, :], in0=ct[:, 1, :], in1=ct[:, 0, :], op=ALU.add), reads=[ctb], writes=[cbb])
                h, hb = hR.next()
                for part in range(2):
                    for tb in range(NTB):
                        p, pbb = psb.next()
                        k.op(k.pe, lambda e: e.matmul(p[:, 0:TB], lhsT=bb[:, part, :], rhs=x[:, tb * TB:(tb + 1) * TB], start=True, stop=True),
                             reads=[bbb, xb], writes=[pbb])
                        k.op(k.act, lambda e: e.activation(out=h[:, part, tb * TB:(tb + 1) * TB], in_=p[:, 0:TB], func=AF.Copy),
                             reads=[pbb], writes=[hb])
                bufs = ([hb, pb], [hb])
                scan_seq(k, E, h, Tn, Th, (d == 1), PW, FL, slab, bufs, flag[:, 0:1], sv)
                hbf, hbb = hB.next()
                k.op(k.act, lambda e: e.activation(out=hbf[:], in_=h[:], func=AF.Copy), reads=[hb], writes=[hbb])
                for tb in range(NTB):
                    py, pyb = psy.next()
                    k.op(k.pe, lambda e: e.matmul(py[:, 0:TB], lhsT=cb_[:, 0, :], rhs=hbf[:, 0, tb * TB:(tb + 1) * TB], start=True, stop=False),
                         reads=[cbb, hbb], writes=[pyb])
                    k.op(k.pe, lambda e: e.matmul(py[:, 0:TB], lhsT=cb_[:, 1, :], rhs=hbf[:, 1, tb * TB:(tb + 1) * TB], start=False, stop=True),
                         reads=[cbb, hbb], writes=[pyb])
                    k.op(k.dve, lambda e: e.tensor_tensor(out=ya[:, tb * TB:(tb + 1) * TB], in0=ya[:, tb * TB:(tb + 1) * TB], in1=py[:, 0:TB], op=ALU.add),
                         reads=[pyb, yab], writes=[yab])
        o, ob = yo.next()
        for tb in range(NTB):
            sl = slice(tb * TB, (tb + 1) * TB)
            g, gb_ = gtmp.next()
            k.op(k.dve, lambda e: e.tensor_tensor(out=g[:, 0:TB], in0=ya[:, sl], in1=ya[:, sl], op=ALU.mult), reads=[yab], writes=[gb_])
            k.op(k.dve, lambda e: e.tensor_scalar(out=g[:, 0:TB], in0=g[:, 0:TB], scalar1=0.044715, scalar2=1.0, op0=ALU.mult, op1=ALU.add), reads=[gb_], writes=[gb_])
            k.op(k.dve, lambda e: e.tensor_tensor(out=g[:, 0:TB], in0=g[:, 0:TB], in1=ya[:, sl], op=ALU.mult), reads=[gb_, yab], writes=[gb_])
            k.op(k.act, lambda e: e.activation(out=g[:, 0:TB], in_=g[:, 0:TB], func=AF.Sigmoid, scale=1.5957691216057308), reads=[gb_], writes=[gb_])
            k.op(k.dve, lambda e: e.tensor_tensor(out=o[:, sl], in0=g[:, 0:TB], in1=ya[:, sl], op=ALU.mult), reads=[gb_, yab], writes=[ob])
        k.dma(k.pool, A["s5T"][oc * 128:(oc + 1) * 128, :], o[:], ob, False)
    ph.close()


def s5_host_layout(inp, G):
    P = 64
    out = {}

    def st(a):
        a = np.asarray(a).reshape(2, G // 2, 2, P)
        return np.ascontiguousarray(a.transpose(2, 3, 0, 1).reshape(128, G))

    out["s5_lr"] = st(inp["s5_lam_re"][0])
    out["s5_li"] = st(inp["s5_lam_im"][0])
    out["s5_ls"] = st(np.broadcast_to(np.asarray(inp["s5_log_step"][0])[:, :, None], (2, G, P)))
    NS = G
    for nm, src in (("s5_Bre", inp["s5_b_re"][0]), ("s5_Bim", inp["s5_b_im"][0])):
        src = np.asarray(src)
        blk = np.zeros((8, 16, NS, 2, P), np.float32)
        for d in range(2):
            for gp in range(G // 2):
                slab = d * (G // 2) + gp
                for g2 in range(2):
                    gl = 2 * (gp % 4) + g2
                    blk[gl, :, slab, g2, :] = src[d, 2 * gp + g2].T
        out[nm] = blk.reshape(128, NS * 128)
    for nm, src in (("s5_Cre", inp["s5_c_re"][0]), ("s5_Cim", inp["s5_c_im"][0])):
        src = np.asarray(src)
        blk = np.zeros((2, P, NS, 8, 16), np.float32)
        for d in range(2):
            for gp in range(G // 2):
                slab = d * (G // 2) + gp
                for g2 in range(2):
                    gl = 2 * (gp % 4) + g2
                    blk[g2, :, slab, gl, :] = src[d, 2 * gp + g2].T
        out[nm] = blk.reshape(128, NS * 128)
    dsk = np.asarray(inp["s5_d"][0]).reshape(-1)
    out["s5_dT"] = np.ascontiguousarray(dsk.reshape(-1, 128).T)
    return out


def na_host_tables(rpb, Rw, kind):
    rpb = np.asarray(rpb)
    H = rpb.shape[0]
    kc = np.arange(64)[:, None]
    qc = np.arange(64)[None, :]
    cidx = np.clip(kc - qc + 15, 0, 30)
    qstart = np.clip(qc - 8, 0, 48)
    cvalid = (kc >= qstart) & (kc < qstart + 16)
    TT = np.zeros((2, 64, H, 14, 64), np.float32)
    for kp in range(2):
        for e in range(14):
            TT[kp, :, :, e, :] = rpb[:, e + kp][:, cidx].transpose(1, 0, 2)
    CM = np.where(cvalid, 0.0, -30000.0).astype(np.float32)
    CM = np.concatenate([CM, CM], 0)
    grids = [(0, Rw)] if kind == "p" else [(0, Rw // 2), (Rw // 2, Rw // 2)]
    rm = np.full((Rw, Rw), -30000.0, np.float32)
    for base, Rg in grids:
        kr_ = min(8, Rg)
        for rr in range(Rg):
            r0 = int(np.clip(rr - kr_ // 2, 0, Rg - kr_))
            rm[base + rr, base + r0:base + r0 + kr_] = 0.0
    NP = Rw // 2
    RMq = np.full((2, NP, 7, 2, 64), -30000.0, np.float32)
    for pi in range(NP):
        r = 2 * pi
        for ci in range(7):
            kr = r - 6 + 2 * ci
            if kr < 0 or kr > Rw - 2:
                continue
            for kp in range(2):
                for qp in range(2):
                    RMq[kp, pi, ci, qp, :] = rm[r + qp, kr + kp]
    Aind = np.zeros((2, 2, 64), np.float32)
    Aind[0, 0] = 1.0
    Aind[1, 1] = 1.0
    return dict(na_TT=TT.reshape(128, H * 14 * 64), na_CM=CM,
                na_RMq=RMq.reshape(2, NP * 7 * 128).astype(ml_dtypes.bfloat16),
                na_A=Aind.reshape(2, 128).astype(ml_dtypes.bfloat16))


def phase_na(k, C, A, Tn):
    H = C["NAH"]
    Rw = Tn // 64
    NP = Rw // 2
    NT = Tn // 128
    ph = Phase(k)
    TT = ph.sb([128, H, 14, 64], F32, "naTT")
    ttb = ph.buf("naTT")
    k.dma(k.sync, TT[:], A["na_TT"].rearrange("p (h e q) -> p h e q", h=H, e=14), ttb, True)
    CM = ph.sb([128, 64], F32, "naCM")
    k.dma(k.sync, CM[:], A["na_CM"], ttb, True)
    for h in range(H):
        k.op(k.dve, lambda e: e.tensor_tensor(out=TT[:, h, :, :], in0=TT[:, h, :, :],
                                              in1=CM[:].unsqueeze(1).broadcast_to([128, 14, 64]), op=ALU.add), reads=[ttb], writes=[ttb])
        k.op(k.dve, lambda e: e.tensor_scalar_mul(out=TT[:, h, :, :], in0=TT[:, h, :, :], scalar1=math.sqrt(128.0)), reads=[ttb], writes=[ttb])
    RM = ph.sb([2, NP * 7 * 128], BF16, "naRM")
    rmb = ph.buf("naRM")
    k.dma(k.sync, RM[:], A["na_RMq"], rmb, True)
    Ai = ph.sb([2, 128], BF16, "naA")
    k.dma(k.sync, Ai[:], A["na_A"], rmb, True)
    qr = Ring(ph, 2, [128, Tn], BF16, "naq")
    kr_ = Ring(ph, 2, [128, Tn], BF16, "nak")
    vr = Ring(ph, 2, [128, NT, 128], BF16, "nav")
    og = Ring(ph, 2, [128, Tn], BF16, "nao")
    pS = Ring(ph, 2, [128, 8, 128], F32, "naS", psum=True)
    pO = Ring(ph, 2, [128, 128], F32, "naO", psum=True)
    pD = Ring(ph, 2, [128, 128], F32, "naD", psum=True)
    sS = Ring(ph, 2, [128, 8, 128], F32, "naSs")
    sP = Ring(ph, 2, [128, 8, 128], BF16, "naP")
    rD = Ring(ph, 2, [128, 128], F32, "naR")
    scale = 1.0 / math.sqrt(128.0)
    for h in range(H):
        q, qb = qr.next()
        kk, kb = kr_.next()
        v, vb = vr.next()
        o, ob = og.next()
        k.dma(k.sync, q[:], A["qT"][h * 128:(h + 1) * 128, :], qb, True)
        k.dma(k.sync, kk[:], A["kT"][h * 128:(h + 1) * 128, :], kb, True)
        k.dma(k.sync, v[:], A["v"].rearrange("(n p) c -> p n c", p=128)[:, :, h * 128:(h + 1) * 128], vb, True)
        for pi in range(NP):
            r = 2 * pi
            chunks = [ci for ci in range(7) if 0 <= r - 6 + 2 * ci <= Rw - 2]
            S, Sb = pS.next()
            for ci in chunks:
                krow = r - 6 + 2 * ci
                k.op(k.pe, lambda e: e.matmul(S[:, ci, :], lhsT=kk[:, 64 * krow:64 * krow + 128], rhs=q[:, 64 * r:64 * r + 128],
                                              start=True, stop=False), reads=[kb, qb], writes=[Sb])
                off = (pi * 7 + ci) * 128
                k.op(k.pe, lambda e: e.matmul(S[:, ci, :], lhsT=Ai[:, :], rhs=RM[:, off:off + 128], start=False, stop=True),
                     reads=[rmb], writes=[Sb])
            Ss, Ssb = sS.next()
            c0, c1 = chunks[0], chunks[-1] + 1
            TT5 = TT[:, h, :, :].rearrange("p (a b) q -> p a b q", b=2)
            for qp in range(2):
                k.op(k.dve, lambda e: e.tensor_tensor(out=Ss[:, c0:c1, qp * 64:(qp + 1) * 64], in0=S[:, c0:c1, qp * 64:(qp + 1) * 64],
                                                      in1=TT5[:, c0:c1, 1 - qp, :], op=ALU.add), reads=[Sb, ttb], writes=[Ssb])
            Pp, Pb = sP.next()
            k.op(k.act, lambda e: e.activation(out=Pp[:, c0:c1, :], in_=Ss[:, c0:c1, :], func=AF.Exp, scale=scale), reads=[Ssb], writes=[Pb])
            O, Ob = pO.next()
            Dn, Db = pD.next()
            for j, ci in enumerate(chunks):
                krow = r - 6 + 2 * ci
                k.op(k.pe, lambda e: e.matmul(O[:], lhsT=v[:, krow // 2, :], rhs=Pp[:, ci, :], start=(j == 0), stop=(j == len(chunks) - 1)),
                     reads=[vb, Pb], writes=[Ob])
            for j, ci in enumerate(chunks):
                k.op(k.pe, lambda e: e.matmul(Dn[:], lhsT=C["ones"][:], rhs=Pp[:, ci, :], start=(j == 0), stop=(j == len(chunks) - 1)),
                     reads=[C["onesb"], Pb], writes=[Db])
            rr, rrb = rD.next()
            k.op(k.dve, lambda e: e.reciprocal(out=rr[:], in_=Dn[:]), reads=[Db], writes=[rrb])
            k.op(k.dve, lambda e: e.tensor_tensor(out=o[:, 64 * r:64 * r + 128], in0=O[:], in1=rr[:], op=ALU.mult), reads=[Ob, rrb], writes=[ob])
        k.dma(k.pool, A["mixT"][h * 128:(h + 1) * 128, :], o[:], ob, False)
    ph.close()


def hy_host_consts(Tn, kind):
    L = Tn if kind == "p" else Tn // 2
    nseq = Tn // L
    pos = np.tile(np.arange(L), nseq)
    t = (pos / L).astype(np.float32)
    ang = 2.0 * math.pi * t[:, None].astype(np.float64) * np.arange(1, 17)
    z = np.concatenate([t[:, None], np.cos(ang), np.sin(ang)], -1).astype(np.float32)
    NT = Tn // 128
    lay = lambda a: np.ascontiguousarray(a.reshape(NT, 128).T)
    m0 = (pos != 0).astype(np.float32)
    ws = np.zeros(Tn, np.float32)
    ws[:L] = 1.0
    f = np.arange(L)
    w = math.pi * (2 * f[None, :] + 1) * np.arange(L)[:, None] / (2.0 * L)
    Cb, Sb = np.cos(w), np.sin(w)
    Cm = np.zeros((Tn, Tn), np.float32)
    Sm = np.zeros((Tn, Tn), np.float32)
    for s in range(nseq):
        Cm[s * L:(s + 1) * L, s * L:(s + 1) * L] = Cb
        Sm[s * L:(s + 1) * L, s * L:(s + 1) * L] = Sb
    bf = ml_dtypes.bfloat16
    Fm = np.concatenate([Cm, -Sm], 1).astype(bf)
    Wi = (np.concatenate([Cm.T, -Sm.T], 0) / L).astype(bf)
    return dict(hy_zT=np.ascontiguousarray(z.T), hy_ntn=lay(-t), hy_m0=lay(m0), hy_ws=lay(ws).astype(bf),
                hy_F=blk_host(Fm, pick_nb(2 * Tn)), hy_C=blk_host(Cm.astype(bf), pick_nb(Tn)),
                hy_S=blk_host(Sm.astype(bf), pick_nb(Tn)), hy_Wi=blk_host(Wi, pick_nb(Tn)))


def phase_hy_filter(k, C, A, Tn):
    HW = C["HW"]
    NT = Tn // 128
    ph = Phase(k)
    cb = ph.buf("hyc")
    ld = lambda t, src: k.dma(k.sync, t, src, cb, True)
    zT = ph.sb([33, Tn], F32, "hzT"); ld(zT[:], A["hy_zT"])
    w1 = ph.sb([33, 64], F32, "hw1"); ld(w1[:], A["hy_f_w1"])
    w2 = ph.sb([64, 64], F32, "hw2"); ld(w2[:], A["hy_f_w2"])
    w3 = ph.sb([64, 2 * HW], F32, "hw3"); ld(w3[:], A["hy_f_w3"])
    sc = ph.sb([64, 8], F32, "hsc")
    col = lambda a: a.rearrange("(p o) -> p o", o=1)
    ld(sc[:, 0:1], col(A["hy_f_freq"])); ld(sc[:, 1:2], col(A["hy_f_b1"])); ld(sc[:, 2:3], col(A["hy_f_b2"]))
    k.op(k.dve, lambda e: e.tensor_tensor(out=sc[:, 3:4], in0=sc[:, 0:1], in1=sc[:, 1:2], op=ALU.mult), reads=[cb], writes=[cb])
    k.op(k.dve, lambda e: e.tensor_tensor(out=sc[:, 4:5], in0=sc[:, 0:1], in1=sc[:, 2:3], op=ALU.mult), reads=[cb], writes=[cb])
    b3 = ph.sb([128, 2 * HW], F32, "hb3"); ld(b3[:], A["hy_f_b3"].partition_broadcast(128))
    eld = ph.sb([128, 2 * HW], F32, "held"); ld(eld[:], A["hy_log_decay"].rearrange("a b -> (a b)").partition_broadcast(128))
    k.op(k.act, lambda e: e.activation(out=eld[:], in_=eld[:], func=AF.Exp), reads=[cb], writes=[cb])
    ntn = ph.sb([128, NT], F32, "hntn"); ld(ntn[:], A["hy_ntn"])
    m0 = ph.sb([128, NT], F32, "hm0"); ld(m0[:], A["hy_m0"])
    ws = ph.sb([128, NT], BF16, "hws"); ld(ws[:], A["hy_ws"])
    h1 = ph.sb([64, Tn], F32, "hh1")
    h2 = ph.sb([64, Tn], F32, "hh2")
    hb = ph.buf("hh")
    pm = Ring(ph, 2, [64, 512], F32, "hpm", psum=True)
    tr = Ring(ph, 2, [64, 512], F32, "htr")
    MAGIC = 12582912.0
    TB = min(512, Tn)
    for (wt, Kd, src, dst, fbc) in ((w1, 33, zT, h1, 3), (w2, 64, h1, h2, 4)):
        for tb in range(Tn // TB):
            p, pb = pm.next()
            k.op(k.pe, lambda e: e.matmul(p[:, 0:TB], lhsT=wt[0:Kd, :], rhs=src[0:Kd, tb * TB:(tb + 1) * TB], start=True, stop=True),
                 reads=[cb, hb], writes=[pb])
            t1, t1b = tr.next()
            t2, t2b = tr.next()
            k.op(k.dve, lambda e: e.tensor_scalar(out=t1[:, 0:TB], in0=p[:, 0:TB], scalar1=sc[:, 0:1], scalar2=sc[:, fbc:fbc + 1],
                                                  op0=ALU.mult, op1=ALU.add), reads=[pb, cb], writes=[t1b])
            k.op(k.dve, lambda e: e.tensor_scalar(out=t2[:, 0:TB], in0=t1[:, 0:TB], scalar1=1.0 / (2 * math.pi), scalar2=MAGIC,
                                                  op0=ALU.mult, op1=ALU.add), reads=[t1b], writes=[t2b])
            k.op(k.dve, lambda e: e.tensor_scalar(out=t2[:, 0:TB], in0=t2[:, 0:TB], scalar1=-MAGIC, scalar2=-2 * math.pi,
                                                  op0=ALU.add, op1=ALU.mult), reads=[t2b], writes=[t2b])
            k.op(k.dve, lambda e: e.tensor_tensor(out=t1[:, 0:TB], in0=t1[:, 0:TB], in1=t2[:, 0:TB], op=ALU.add), reads=[t1b, t2b], writes=[t1b])
            k.op(k.dve, lambda e: e.tensor_scalar(out=t1[:, 0:TB], in0=t1[:, 0:TB], scalar1=math.pi, scalar2=-math.pi,
                                                  op0=ALU.min, op1=ALU.max), reads=[t1b], writes=[t1b])
            k.op(k.act, lambda e: e.activation(out=dst[:, tb * TB:(tb + 1) * TB], in_=t1[:, 0:TB], func=AF.Sin), reads=[t1b], writes=[hb])
    pf = Ring(ph, 4, [128, 512], F32, "hpf", psum=True)
    pnr = Ring(ph, 2, [128, 4], F32, "hpn", psum=True)
    nacc = ph.sb([128, HW // 128], F32, "hnacc")
    pnb = ph.buf("hnacc")
    k.op(k.dve, lambda e: e.memset(nacc[:], 0.0), writes=[pnb])
    fr = Ring(ph, 4, [128, 512], F32, "hfr")
    dr = Ring(ph, 2, [128, 512], F32, "hdr")
    ar = Ring(ph, 2, [128, 512], BF16, "har")
    orr = Ring(ph, 4, [128, 512], BF16, "hor")
    CB = min(512, HW)
    for n in range(NT):
        for cbk in range(HW // CB):
            fd = []
            for d in range(2):
                c0 = d * HW + cbk * CB
                p, pb = pf.next()
                k.op(k.pe, lambda e: e.matmul(p[:, 0:CB], lhsT=h2[:, n * 128:(n + 1) * 128], rhs=w3[:, c0:c0 + CB], start=True, stop=True),
                     reads=[hb, cb], writes=[pb])
                dc, dcb = dr.next()
                k.op(k.act, lambda e: e.activation(out=dc[:, 0:CB], in_=eld[:, c0:c0 + CB], func=AF.Exp, scale=ntn[:, n:n + 1]),
                     reads=[cb], writes=[dcb])
                f, fb = fr.next()
                k.op(k.dve, lambda e: e.tensor_tensor(out=f[:, 0:CB], in0=p[:, 0:CB], in1=b3[:, c0:c0 + CB], op=ALU.add), reads=[pb, cb], writes=[fb])
                k.op(k.dve, lambda e: e.tensor_tensor(out=f[:, 0:CB], in0=f[:, 0:CB], in1=dc[:, 0:CB], op=ALU.mult), reads=[fb, dcb], writes=[fb])
                fd.append((f, fb))
            (ff, ffb), (fw, fwb) = fd
            a1, a1b = dr.next()
            k.op(k.act, lambda e: e.activation(out=a1[:, 0:CB], in_=ff[:, 0:CB], func=AF.Abs), reads=[ffb], writes=[a1b])
            a2, a2b = dr.next()
            k.op(k.act, lambda e: e.activation(out=a2[:, 0:CB], in_=fw[:, 0:CB], func=AF.Abs), reads=[fwb], writes=[a2b])
            ab_, abb = ar.next()
            k.op(k.dve, lambda e: e.tensor_tensor(out=ab_[:, 0:CB], in0=a1[:, 0:CB], in1=a2[:, 0:CB], op=ALU.add), reads=[a1b, a2b], writes=[abb])
            pq, pqb = pnr.next()
            nj = CB // 128
            for j in range(nj):
                k.op(k.pe, lambda e: e.matmul(pq[:, j:j + 1], lhsT=ab_[:, j * 128:(j + 1) * 128], rhs=ws[:, n:n + 1],
                                              start=True, stop=True), reads=[abb, cb], writes=[pqb])
            k.op(k.dve, lambda e: e.tensor_tensor(out=nacc[:, cbk * nj:(cbk + 1) * nj], in0=nacc[:, cbk * nj:(cbk + 1) * nj],
                                                  in1=pq[:, 0:nj], op=ALU.add), reads=[pqb, pnb], writes=[pnb])
            o1, o1b = orr.next()
            k.op(k.dve, lambda e: e.scalar_tensor_tensor(out=o1[:, 0:CB], in0=fw[:, 0:CB], scalar=m0[:, n:n + 1], in1=ff[:, 0:CB],
                                                         op0=ALU.mult, op1=ALU.add), reads=[ffb, fwb, cb], writes=[o1b])
            o2, o2b = orr.next()
            k.op(k.dve, lambda e: e.scalar_tensor_tensor(out=o2[:, 0:CB], in0=fw[:, 0:CB], scalar=m0[:, n:n + 1], in1=ff[:, 0:CB],
                                                         op0=ALU.mult, op1=ALU.subtract), reads=[ffb, fwb, cb], writes=[o2b])
            k.dma(k.pool, A["hy_hs"][n * 128:(n + 1) * 128, cbk * CB:(cbk + 1) * CB], o1[:, 0:CB], o1b, False)
            k.dma(k.pool, A["hy_hd"][n * 128:(n + 1) * 128, cbk * CB:(cbk + 1) * CB], o2[:, 0:CB], o2b, False)
    rn = ph.sb([128, HW // 128], F32, "hrn")
    rnb = ph.buf("hrn")
    k.op(k.dve, lambda e: e.reciprocal(out=rn[:], in_=nacc[:]), reads=[pnb], writes=[rnb])
    k.dma(k.pool, A["hy_rn"], rn[:], rnb, False)
    ph.close()


def phase_hy_conv(k, C, A, Tn):
    HW = C["HW"]
    NT = Tn // 128
    Th = Tn // 2
    NC = HW // 128
    ph = Phase(k)
    cb = ph.buf("hcc")
    cw = ph.sb([128, 3 * NC, 3], F32, "hcw")
    k.dma(k.sync, cw[:], A["hy_cwT"].rearrange("p (c j) -> p c j", j=3), cb, True)
    cbias = ph.sb([128, 3 * NC], F32, "hcb")
    k.dma(k.sync, cbias[:], A["hy_cbT"], cb, True)
    flag = ph.sb([128, 1], F32, "hfl")
    k.dma(k.sync, flag[:], A["flag"], cb, True)
    cwf = ph.sb([128, 3 * NC, 3], F32, "hcwf")
    k.op(k.dve, lambda e: e.tensor_scalar_mul(out=cwf[:], in0=cw[:], scalar1=flag[:, 0:1]), reads=[cb], writes=[cb])
    zr = Ring(ph, 4, [128, Tn], BF16, "hz")
    zc = Ring(ph, 4, [128, Tn], F32, "hzc")
    ur = Ring(ph, 2, [128, Tn], BF16, "hu")
    xr = Ring(ph, 2, [128, Tn], BF16, "hx0")
    pt = Ring(ph, 2, [128, 4, 128], BF16, "hpt", psum=True)
    us = Ring(ph, 2, [128, NT, 128], BF16, "hus")
    for c in range(NC):
        outs = []
        for part in range(3):
            cc = part * NC + c
            z, zb = zr.next()
            k.dma(k.sync, z[:], A["zT"][cc * 128:(cc + 1) * 128, :], zb, True)
            o, ob = zc.next()
            w0, w1, w2 = cw[:, cc, 0:1], cw[:, cc, 1:2], cw[:, cc, 2:3]
            k.op(k.dve, lambda e: e.tensor_scalar(out=o[:], in0=z[:], scalar1=w1, scalar2=cbias[:, cc:cc + 1], op0=ALU.mult, op1=ALU.add),
                 reads=[zb, cb], writes=[ob])
            for a in (0, Th):
                k.op(k.dve, lambda e: e.scalar_tensor_tensor(out=o[:, a + 1:a + Th], in0=z[:, a:a + Th - 1], scalar=w0, in1=o[:, a + 1:a + Th],
                                                             op0=ALU.mult, op1=ALU.add), reads=[zb, cb, ob], writes=[ob])
                k.op(k.dve, lambda e: e.scalar_tensor_tensor(out=o[:, a:a + Th - 1], in0=z[:, a + 1:a + Th], scalar=w2, in1=o[:, a:a + Th - 1],
                                                             op0=ALU.mult, op1=ALU.add), reads=[zb, cb, ob], writes=[ob])
            k.op(k.dve, lambda e: e.scalar_tensor_tensor(out=o[:, Th:Th + 1], in0=z[:, Th - 1:Th], scalar=cwf[:, cc, 0:1], in1=o[:, Th:Th + 1],
                                                         op0=ALU.mult, op1=ALU.add), reads=[zb, cb, ob], writes=[ob])
            k.op(k.dve, lambda e: e.scalar_tensor_tensor(out=o[:, Th - 1:Th], in0=z[:, Th:Th + 1], scalar=cwf[:, cc, 2:3], in1=o[:, Th - 1:Th],
                                                         op0=ALU.mult, op1=ALU.add), reads=[zb, cb, ob], writes=[ob])
            outs.append((o, ob))
        (x0, x0b), (x1, x1b), (vv, vvb) = outs
        xo, xob = xr.next()
        k.op(k.act, lambda e: e.activation(out=xo[:], in_=x0[:], func=AF.Copy), reads=[x0b], writes=[xob])
        k.dma(k.pool, A["hy_x0T"][c * 128:(c + 1) * 128, :], xo[:], xob, False)
        u, ub = ur.next()
        k.op(k.dve, lambda e: e.tensor_tensor(out=u[:], in0=vv[:], in1=x1[:], op=ALU.mult), reads=[vvb, x1b], writes=[ub])
        k.dma(k.pool, A["hy_uT"][c * 128:(c + 1) * 128, :], u[:], ub, False)
        s, sb_ = us.next()
        for n4 in range(0, NT, 4):
            p, pb = pt.next()
            for j in range(min(4, NT - n4)):
                n = n4 + j
                k.op(k.pe, lambda e: e.transpose(out=p[:, j, :], in_=u[:, n * 128:(n + 1) * 128], identity=C["ident"][:]),
                     reads=[ub, C["identb"]], writes=[pb])
            k.op(k.act, lambda e: e.activation(out=s[:, n4:n4 + 4, :], in_=p[:], func=AF.Copy), reads=[pb], writes=[sb_])
        k.dma(k.pool, A["hy_u"].rearrange("(n p) c -> p n c", p=128)[:, :, c * 128:(c + 1) * 128], s[:], sb_, False)
    ph.close()


def phase_hy_mul(k, C, A, Tn):
    HW = C["HW"]
    ph = Phase(k)
    CB = min(512, HW)
    ir = Ring(ph, 8, [128, 512], BF16, "hmi")
    tr = Ring(ph, 4, [128, 512], F32, "hmt")
    orr = Ring(ph, 4, [128, 512], BF16, "hmo")
    for fi in range(Tn // 128):
        for cbk in range(HW // CB):
            cs = slice(cbk * CB, (cbk + 1) * CB)
            t = []
            for (src, r0) in ((A["hy_Uf"], fi * 128), (A["hy_Uf"], Tn + fi * 128), (A["hy_Tf"], fi * 128), (A["hy_Tf"], Tn + fi * 128)):
                x, xb = ir.next()
                k.dma(k.sync, x[:, 0:CB], src[r0:r0 + 128, cs], xb, True)
                t.append((x, xb))
            (ur_, urb), (ui, uib), (tr_, trb), (ti, tib) = t
            a, ab_ = tr.next()
            b, bb_ = tr.next()
            k.op(k.dve, lambda e: e.tensor_tensor(out=a[:, 0:CB], in0=ur_[:, 0:CB], in1=tr_[:, 0:CB], op=ALU.mult), reads=[urb, trb], writes=[ab_])
            k.op(k.dve, lambda e: e.tensor_tensor(out=b[:, 0:CB], in0=ui[:, 0:CB], in1=ti[:, 0:CB], op=ALU.mult), reads=[uib, tib], writes=[bb_])
            o1, o1b = orr.next()
            k.op(k.dve, lambda e: e.tensor_tensor(out=o1[:, 0:CB], in0=a[:, 0:CB], in1=b[:, 0:CB], op=ALU.subtract), reads=[ab_, bb_], writes=[o1b])
            a2, a2b = tr.next()
            b2, b2b = tr.next()
            k.op(k.dve, lambda e: e.tensor_tensor(out=a2[:, 0:CB], in0=ur_[:, 0:CB], in1=ti[:, 0:CB], op=ALU.mult), reads=[urb, tib], writes=[a2b])
            k.op(k.dve, lambda e: e.tensor_tensor(out=b2[:, 0:CB], in0=ui[:, 0:CB], in1=tr_[:, 0:CB], op=ALU.mult), reads=[uib, trb], writes=[b2b])
            o2, o2b = orr.next()
            k.op(k.dve, lambda e: e.tensor_tensor(out=o2[:, 0:CB], in0=a2[:, 0:CB], in1=b2[:, 0:CB], op=ALU.add), reads=[a2b, b2b], writes=[o2b])
            k.dma(k.pool, A["hy_Yf"][fi * 128:(fi + 1) * 128, cs], o1[:, 0:CB], o1b, False)
            k.dma(k.pool, A["hy_Yf"][Tn + fi * 128:Tn + (fi + 1) * 128, cs], o2[:, 0:CB], o2b, False)
    ph.close()


def epi_hy_final(k, C, A):
    HW, NAW = C["HW"], C["NAW"]

    def setup(ph):
        b = ph.buf("hfs")
        rn = ph.sb([128, HW // 128], F32, "hfrn")
        k.dma(k.sync, rn[:], A["hy_rn"], b, True)
        bi = ph.sb([128, HW // 128], F32, "hfbi")
        k.dma(k.sync, bi[:], A["hy_biasT"], b, True)
        return dict(b=b, rn=rn, bi=bi, u=Ring(ph, 3, [128, 512], BF16, "hfu"), x=Ring(ph, 3, [128, 512], BF16, "hfx"),
                    y=Ring(ph, 2, [128, 512], F32, "hfy"), o=Ring(ph, 3, [128, 512], BF16, "hfo"))

    def epi(ph, st, pts, n0, nsz, t0, tsz):
        p, pb = pts[0]
        c = t0 // 128
        u, ub = st["u"].next()
        k.dma(k.pool, u[:, 0:nsz], A["hy_uT"][t0:t0 + 128, n0:n0 + nsz], ub, True)
        x, xb = st["x"].next()
        k.dma(k.pool, x[:, 0:nsz], A["hy_x0T"][t0:t0 + 128, n0:n0 + nsz], xb, True)
        y, yb = st["y"].next()
        k.op(k.dve, lambda e: e.tensor_scalar_mul(out=y[:, 0:nsz], in0=p[:, 0:nsz], scalar1=st["rn"][:, c:c + 1]), reads=[pb, st["b"]], writes=[yb])
        k.op(k.dve, lambda e: e.scalar_tensor_tensor(out=y[:, 0:nsz], in0=u[:, 0:nsz], scalar=st["bi"][:, c:c + 1], in1=y[:, 0:nsz],
                                                     op0=ALU.mult, op1=ALU.add), reads=[ub, yb, st["b"]], writes=[yb])
        o, ob = st["o"].next()
        k.op(k.dve, lambda e: e.tensor_tensor(out=o[:, 0:nsz], in0=y[:, 0:nsz], in1=x[:, 0:nsz], op=ALU.mult), reads=[yb, xb], writes=[ob])
        k.dma(k.pool, A["mixT"][NAW + t0:NAW + t0 + 128, n0:n0 + nsz], o[:, 0:nsz], ob, False)
    return setup, epi


def hyena_all(k, C, A, Tn):
    HW = C["HW"]
    phase_hy_filter(k, C, A, Tn)
    phase_hy_conv(k, C, A, Tn)
    s, e = epi_store_fm(k, A["hy_Uf"])
    gemm(k, C, A["hy_u"], [A["hy_F"]], HW, "FM", e, epi_setup=s)
    s, e = epi_store_fm(k, A["hy_Tf"], row0=0)
    gemm(k, C, A["hy_hs"], [A["hy_C"]], HW, "FM", e, epi_setup=s)
    s, e = epi_store_fm(k, A["hy_Tf"], row0=Tn)
    gemm(k, C, A["hy_hd"], [A["hy_S"]], HW, "FM", e, epi_setup=s)
    phase_hy_mul(k, C, A, Tn)
    s, e = epi_hy_final(k, C, A)
    gemm(k, C, A["hy_Yf"], [A["hy_Wi"]], HW, "TM", e, epi_setup=s, TG=512, KBmax=32)


SCR_KIND = "Internal"
PHASES = None
DBG = {}
CFG = dict(D=4096, F=11008, XH=4, XW=512, NM=256, NAH=16, NAW=2048, HW=2048, G=256, T=4096)


def build_program(cfg, in_shapes):
    nc = bass.Bass("TRN2", target_bir_lowering=False)
    k = KB(nc)
    C = dict(cfg)
    D, F, XW, NM, NAW, HW, T = C["D"], C["F"], C["XW"], C["NM"], C["NAW"], C["HW"], C["T"]
    A = {}
    for name, (shape, dt) in in_shapes.items():
        A[name] = nc.dram_tensor(name, list(shape), BF16 if dt == "bf16" else F32, kind="ExternalInput").ap()
    A["y"] = nc.dram_tensor("y", [T, D], F32, kind="ExternalOutput").ap()

    def scr(name, shape, dt=BF16):
        A[name] = nc.dram_tensor("scr_" + name, list(shape), dt, kind=SCR_KIND).ap()

    scr("xnT", [D, T]); scr("mnT", [D, 2 * NM]); scr("cqT", [XW, T]); scr("ckT", [XW, 2 * NM]); scr("cv", [2 * NM, XW])
    scr("coT", [XW, T]); scr("hT", [F, T]); scr("qT", [NAW, T]); scr("kT", [NAW, T]); scr("v", [T, NAW]); scr("zT", [3 * HW, T])
    scr("mixT", [NAW + HW, T]); scr("s5T", [D, T])
    scr("hy_hs", [T, HW]); scr("hy_hd", [T, HW]); scr("hy_rn", [128, HW // 128], F32); scr("hy_x0T", [HW, T]); scr("hy_uT", [HW, T])
    scr("hy_u", [T, HW]); scr("hy_Uf", [2 * T, HW]); scr("hy_Tf", [2 * T, HW]); scr("hy_Yf", [2 * T, HW])
    setup_consts(k, C, A["ident"])
    for n in ["g_mix", "g_cross", "g_mem", "g_ffn", "x_wq", "x_wk", "x_wv", "x_wo", "x_q_gain", "x_k_gain", "w_ffn_gate", "w_ffn_up", "w_ffn_down"]:
        A[n] = [A[n][0], A[n][1]]
    w_in = A["w_in"]
    W = {}
    items = []

    def wb(name, src, dual=False):
        Kd, N = src.shape
        NB = pick_nb(N, dual)
        W[name] = nc.dram_tensor("wb_" + name, [N // NB, 128, Kd // 128, NB], BF16, kind="Internal").ap()
        items.append((src, W[name]))

    wb("q", w_in[:, 0:NAW]); wb("k", w_in[:, NAW:2 * NAW]); wb("v", w_in[:, 2 * NAW:3 * NAW]); wb("z", w_in[:, 3 * NAW:3 * NAW + 3 * HW])
    wb("w_out", A["w_out"])
    for l in range(2):
        wb("x_wq%d" % l, A["x_wq"][l]); wb("x_wk%d" % l, A["x_wk"][l]); wb("x_wv%d" % l, A["x_wv"][l]); wb("x_wo%d" % l, A["x_wo"][l])
        wb("gate%d" % l, A["w_ffn_gate"][l], True); wb("up%d" % l, A["w_ffn_up"][l], True); wb("down%d" % l, A["w_ffn_down"][l])
    wb("glu_v", A["w_glu"][:, 0:D], True); wb("glu_g", A["w_glu"][:, D:2 * D], True)
    A["Wb"] = W
    on = lambda n: (PHASES is None) or (n in PHASES)
    if on("cast"):
        phase_cast_weights(k, items)
    if on("l0proj"):
        phase_normT(k, C, A["x"], A["g_mix"][0], A["xnT"], T)
        s, e = epi_headnorm_fm(k, C, A["qT"], A["na_q_gain"])
        gemm(k, C, A["xnT"], [W["q"]], T, "FM", e, epi_setup=s)
        s, e = epi_headnorm_fm(k, C, A["kT"], A["na_k_gain"])
        gemm(k, C, A["xnT"], [W["k"]], T, "FM", e, epi_setup=s)
        s, e = epi_store_tm(k, A["v"])
        gemm(k, C, A["xnT"], [W["v"]], T, "TM", e, epi_setup=s)
        s, e = epi_store_fm(k, A["zT"])
        gemm(k, C, A["xnT"], [W["z"]], T, "FM", e, epi_setup=s)
    if on("na"):
        phase_na(k, C, A, T)
    if on("hy"):
        hyena_all(k, C, A, T)
    if on("wout"):
        s, e = epi_resid_tm(k, A["x"], A["y"])
        gemm(k, C, A["mixT"], [W["w_out"]], T, "TM", e, epi_setup=s)
    if on("tail0"):
        dense_tail(k, C, A, 0, T)
    if on("s5"):
        phase_normT(k, C, A["y"], A["g_mix"][1], A["xnT"], T)
        phase_s5(k, C, A, T)
    if on("glu"):
        s, e = epi_resid_tm(k, A["y"], A["y"], glu=True)
        gemm(k, C, A["s5T"], [W["glu_v"], W["glu_g"]], T, "TM", e, epi_setup=s)
    if on("tail1"):
        dense_tail(k, C, A, 1, T)
    k.barrier()
    return nc


def host_inputs(cfg, inputs, x_core, mem_core, kind):
    bf = ml_dtypes.bfloat16
    T, HW, G = cfg["T"], cfg["HW"], cfg["G"]
    f = lambda n: np.ascontiguousarray(np.asarray(inputs[n]))
    im = {"x": x_core, "mem": mem_core, "ident": np.eye(128).astype(bf),
          "flag": np.full((128, 1), 1.0 if kind == "p" else 0.0, np.float32)}
    for n in ["g_mix", "g_cross", "g_mem", "g_ffn", "x_wq", "x_wk", "x_wv", "x_wo", "x_q_gain", "x_k_gain",
              "w_ffn_gate", "w_ffn_up", "w_ffn_down"]:
        im[n] = f(n)
    for n in ["w_in", "na_q_gain", "na_k_gain", "w_out", "w_glu", "hy_f_w1", "hy_f_w2", "hy_f_w3", "hy_f_freq", "hy_f_b1", "hy_f_b2",
              "hy_f_b3", "hy_log_decay"]:
        im[n] = f(n)[0]
    cw = f("hy_conv_w")[0]
    im["hy_cwT"] = np.ascontiguousarray(cw.reshape(3, 3 * HW // 128, 128).transpose(2, 1, 0).reshape(128, -1))
    im["hy_cbT"] = np.ascontiguousarray(f("hy_conv_b")[0].reshape(-1, 128).T)
    im["hy_biasT"] = np.ascontiguousarray(f("hy_bias")[0].reshape(-1, 128).T)
    im.update(hy_host_consts(T, kind))
    im.update(na_host_tables(f("na_rpb")[0], T // 64, kind))
    im.update(s5_host_layout(inputs, G))
    return im


_CACHE = {}


def run_cfg(cfg, inputs, cores):
    base = {}
    ims = []
    for (x, m, kd) in cores:
        if kd not in base:
            base[kd] = host_inputs(cfg, inputs, x, m, kd)
        im = dict(base[kd])
        im["x"] = x
        im["mem"] = m
        ims.append(im)
    shapes = {n: (a.shape, "bf16" if a.dtype == ml_dtypes.bfloat16 else "f32") for n, a in ims[0].items()}
    nc = build_program(cfg, shapes)
    res = run_bass_kernel_spmd(nc, ims, core_ids=list(range(len(ims))))
    DBG["res"] = res.results
    return [r["y"] for r in res.results]


def kernel(**inputs):
    cfg = CFG
    T, D, NM = cfg["T"], cfg["D"], cfg["NM"]
    xp = np.asarray(inputs["x_prompt"])
    xs = np.asarray(inputs["x_sample"])
    mp = np.asarray(inputs["mem_prompt"])
    ms = np.asarray(inputs["mem_sample"])
    cores = []
    for b in range(2):
        cores.append((np.ascontiguousarray(xp[b]), np.ascontiguousarray(np.concatenate([mp[b], mp[b]], 0)), "p"))
    for j in range(4):
        cores.append((np.ascontiguousarray(xs[2 * j:2 * j + 2].reshape(T, D)), np.ascontiguousarray(ms[2 * j:2 * j + 2].reshape(2 * NM, D)), "s"))
    cores.append(cores[2])
    cores.append(cores[3])
    ys = run_cfg(cfg, inputs, cores)
    y_prompt = np.stack([ys[0], ys[1]], 0).astype(np.float32)
    y_sample = np.concatenate([ys[2 + j].reshape(2, T // 2, D) for j in range(4)], 0).astype(np.float32)
    return (y_prompt, y_sample)
```
